# Optimizing a Trainium2 kernel written in Bass

```python
import math
import jax, jax.numpy as jnp
from jax import lax
import numpy as np

D_MODEL = 1024
BATCH = 8
SEQ = 4096
DEPTH = 2

EPS = 1e-6
DA_HEADS = 8
DA_QK_DIM = 64
DA_V_DIM = 2 * DA_QK_DIM
DA_WIDTH = DA_HEADS * DA_V_DIM
Q_BLOCK = 128
ML_HEADS = 4
ML_QK_DIM = 128
ML_V_DIM = 256
ML_WIDTH = ML_HEADS * ML_V_DIM
ML_CHUNK = 128
CONV_WIDTH = 4
SG_GROUPS = 8
SG_CHUNK = 128
SG_WIDTH = 2 * D_MODEL
SG_GROUP_DIM = SG_WIDTH // SG_GROUPS

EVEN_SIZES = (
    DA_HEADS * 2 * DA_QK_DIM,
    DA_HEADS * 2 * DA_QK_DIM,
    DA_WIDTH,
    DA_WIDTH,
    2 * ML_HEADS * ML_QK_DIM,
    ML_WIDTH,
    ML_HEADS,
    ML_HEADS,
    ML_WIDTH,
    ML_WIDTH,
)
EVEN_IN = sum(EVEN_SIZES)
ODD_SIZES = (SG_WIDTH, SG_WIDTH, SG_WIDTH)
ODD_IN = sum(ODD_SIZES)

kernel_name = "hybrid_diffattn_mlstm_gmlp_block"


def _split(t, sizes):
    out, off = [], 0
    for n in sizes:
        out.append(t[..., off:off + n])
        off += n
    return out


def rmsnorm(x, g):
    xf = x.astype(jnp.float32)
    y = xf * lax.rsqrt(jnp.mean(xf * xf, axis=-1, keepdims=True) + EPS)
    return (y * g.astype(jnp.float32)).astype(x.dtype)


def causal_dwconv(x, w, b):
    S = x.shape[1]
    xp = jnp.pad(x, ((0, 0), (CONV_WIDTH - 1, 0), (0, 0)))
    return sum(w[j] * xp[:, j:j + S] for j in range(CONV_WIDTH)) + b


def diff_attention(q, k, v, lam):
    B, S = q.shape[:2]
    nb = S // Q_BLOCK
    qb = (q * DA_QK_DIM ** -0.5).reshape(B, nb, Q_BLOCK, DA_HEADS, 2, DA_QK_DIM)
    qb = qb.transpose(1, 0, 3, 4, 2, 5)
    kt = k.transpose(0, 2, 3, 1, 4)
    vt = v.transpose(0, 2, 1, 3)
    k_pos = jnp.arange(S)

    def block(args):
        qi, i = args
        s = jnp.einsum('bhcqd,bhckd->bhcqk', qi, kt).astype(jnp.float32)
        q_pos = i * Q_BLOCK + jnp.arange(Q_BLOCK)
        s = jnp.where(k_pos[None, :] <= q_pos[:, None], s, -jnp.inf)
        p = jax.nn.softmax(s, axis=-1)
        a = p[:, :, 0] - lam * p[:, :, 1]
        return jnp.einsum('bhqk,bhkv->bhqv', a.astype(vt.dtype), vt)

    o = lax.map(block, (qb, jnp.arange(nb)))
    return o.transpose(1, 0, 3, 2, 4).reshape(B, S, DA_HEADS, DA_V_DIM)


def mlstm_chunkwise(q, k, v, i_pre, f_pre):
    B, S, H, dk = q.shape
    dv = v.shape[-1]
    L = ML_CHUNK
    nc = S // L
    f32 = jnp.float32

    def chunks(t):
        t = t.astype(f32).reshape(B, nc, L, H, *t.shape[3:])
        return jnp.moveaxis(t, 3, 1)

    qc = chunks(q) * dk ** -0.5
    kc = chunks(k)
    vc = chunks(v)
    ic = chunks(i_pre)
    lf = jax.nn.log_sigmoid(chunks(f_pre))
    bcum = jnp.cumsum(lf, axis=-1)
    b_last = bcum[..., -1]
    g = b_last[..., None] - bcum + ic

    def step(carry, inp):
        C, n, m = carry
        bl, gs, ks, vs = inp
        m_new = jnp.maximum(bl + m, jnp.max(gs, axis=-1))
        decay = jnp.exp(bl + m - m_new)
        w = jnp.exp(gs - m_new[..., None])
        C_new = decay[..., None, None] * C + jnp.einsum('bhl,bhlk,bhlv->bhkv', w, ks, vs)
        n_new = decay[..., None] * n + jnp.einsum('bhl,bhlk->bhk', w, ks)
        return (C_new, n_new, m_new), (C, n, m)

    init = (jnp.zeros((B, H, dk, dv), f32), jnp.zeros((B, H, dk), f32), jnp.zeros((B, H), f32))
    xs = (jnp.moveaxis(b_last, 2, 0), jnp.moveaxis(g, 2, 0),
          jnp.moveaxis(kc, 2, 0), jnp.moveaxis(vc, 2, 0))
    _, (C_prev, n_prev, m_prev) = lax.scan(step, init, xs)
    C_prev = jnp.moveaxis(C_prev, 0, 2)
    n_prev = jnp.moveaxis(n_prev, 0, 2)
    m_prev = jnp.moveaxis(m_prev, 0, 2)

    causal = jnp.tril(jnp.ones((L, L), dtype=bool))
    log_d = bcum[..., :, None] - bcum[..., None, :] + ic[..., None, :]
    log_d = jnp.where(causal, log_d, -jnp.inf)
    log_inter = bcum + m_prev[..., None]
    m_t = jnp.maximum(log_inter, jnp.max(log_d, axis=-1))
    dmat = jnp.exp(log_d - m_t[..., None])
    inter_w = jnp.exp(log_inter - m_t)
    qk = jnp.einsum('bhctd,bhcsd->bhcts', qc, kc) * dmat
    num = (jnp.einsum('bhcts,bhcsv->bhctv', qk, vc)
           + inter_w[..., None] * jnp.einsum('bhctd,bhcdv->bhctv', qc, C_prev))
    den = jnp.sum(qk, axis=-1) + inter_w * jnp.einsum('bhctd,bhcd->bhct', qc, n_prev)
    h = num / jnp.maximum(jnp.abs(den), jnp.exp(-m_t))[..., None]
    h = jnp.moveaxis(h, 1, 3).reshape(B, S, H, dv)
    return h.astype(q.dtype)


def even_mixer(h, layer, w_in, b_igate, b_fgate, conv_w, conv_b,
               lambda_q1, lambda_k1, lambda_q2, lambda_k2, da_head_g, ml_head_g, w_out):
    B, S, _ = h.shape
    p = jnp.einsum('bsd,de->bse', h, w_in)
    da_q, da_k, da_v, da_z, ml_qk, ml_v, ml_i, ml_f, ml_o, ml_z = _split(p, EVEN_SIZES)

    f32 = jnp.float32
    lam_init = 0.8 - 0.6 * math.exp(-0.3 * layer)
    lam = (jnp.exp(jnp.dot(lambda_q1.astype(f32), lambda_k1.astype(f32)))
           - jnp.exp(jnp.dot(lambda_q2.astype(f32), lambda_k2.astype(f32))) + lam_init)
    o_a = diff_attention(da_q.reshape(B, S, DA_HEADS, 2, DA_QK_DIM),
                         da_k.reshape(B, S, DA_HEADS, 2, DA_QK_DIM),
                         da_v.reshape(B, S, DA_HEADS, DA_V_DIM), lam)
    o_a = rmsnorm(o_a, da_head_g) * (1.0 - lam_init)
    y_a = o_a.reshape(B, S, DA_WIDTH) * jax.nn.silu(da_z)

    qk = jax.nn.silu(causal_dwconv(ml_qk, conv_w, conv_b))
    ml_q, ml_k = _split(qk, (ML_HEADS * ML_QK_DIM, ML_HEADS * ML_QK_DIM))
    hm = mlstm_chunkwise(ml_q.reshape(B, S, ML_HEADS, ML_QK_DIM),
                         ml_k.reshape(B, S, ML_HEADS, ML_QK_DIM),
                         ml_v.reshape(B, S, ML_HEADS, ML_V_DIM),
                         ml_i + b_igate, ml_f + b_fgate)
    hm = hm * jax.nn.sigmoid(ml_o).reshape(B, S, ML_HEADS, ML_V_DIM)
    hm = rmsnorm(hm, ml_head_g)
    y_b = hm.reshape(B, S, ML_WIDTH) * jax.nn.silu(ml_z)

    y = jnp.concatenate([y_a, y_b], axis=-1)
    return jnp.einsum('bse,ed->bsd', y, w_out)


def odd_mixer(h, sg_norm_g, w_in, w_spatial, b_spatial, w_out):
    B, S, _ = h.shape
    nc = S // SG_CHUNK
    p = jnp.einsum('bsd,de->bse', h, w_in)
    u, v, z = _split(p, ODD_SIZES)
    u = jax.nn.gelu(u)
    v = rmsnorm(jax.nn.gelu(v), sg_norm_g)
    vg = v.reshape(B, nc, SG_CHUNK, SG_GROUPS, SG_GROUP_DIM)
    causal = jnp.tril(jnp.ones((SG_CHUNK, SG_CHUNK), dtype=w_spatial.dtype))
    wm = w_spatial * causal
    vs = jnp.einsum('gts,bcsgd->bctgd', wm, vg) + b_spatial.T[None, None, :, :, None]
    y = u * vs.reshape(B, S, SG_WIDTH) * jax.nn.silu(z)
    return jnp.einsum('bse,ed->bsd', y, w_out)


def setup_inputs(seed: int = 0) -> dict:
    key = jax.random.key(seed)
    ks = jax.random.split(key, 24)
    f32 = jnp.float32

    def nrm(k, shape, scale):
        return jax.random.normal(k, shape, f32) * scale

    def gain(k, n):
        return 1.0 + 0.05 * jax.random.normal(k, (n,), f32)

    return {
        "x": nrm(ks[0], (BATCH, SEQ, D_MODEL), 1.0),
        "l0_pre_g": gain(ks[1], D_MODEL),
        "l0_w_in": nrm(ks[2], (D_MODEL, EVEN_IN), D_MODEL ** -0.5),
        "l0_b_igate": nrm(ks[3], (ML_HEADS,), 0.1),
        "l0_b_fgate": 3.0 + nrm(ks[4], (ML_HEADS,), 0.5),
        "l0_conv_w": nrm(ks[5], (CONV_WIDTH, 2 * ML_HEADS * ML_QK_DIM), CONV_WIDTH ** -0.5),
        "l0_conv_b": nrm(ks[6], (2 * ML_HEADS * ML_QK_DIM,), 0.02),
        "l0_lambda_q1": nrm(ks[7], (DA_QK_DIM,), 0.1),
        "l0_lambda_k1": nrm(ks[8], (DA_QK_DIM,), 0.1),
        "l0_lambda_q2": nrm(ks[9], (DA_QK_DIM,), 0.1),
        "l0_lambda_k2": nrm(ks[10], (DA_QK_DIM,), 0.1),
        "l0_da_head_g": gain(ks[11], DA_V_DIM),
        "l0_ml_head_g": gain(ks[12], ML_V_DIM),
        "l0_w_out": nrm(ks[13], (DA_WIDTH + ML_WIDTH, D_MODEL), (DA_WIDTH + ML_WIDTH) ** -0.5),
        "l0_post_g": gain(ks[14], D_MODEL),
        "l1_pre_g": gain(ks[15], D_MODEL),
        "l1_w_in": nrm(ks[16], (D_MODEL, ODD_IN), D_MODEL ** -0.5),
        "l1_sg_norm_g": gain(ks[17], SG_WIDTH),
        "l1_w_spatial": nrm(ks[18], (SG_GROUPS, SG_CHUNK, SG_CHUNK), SG_CHUNK ** -0.5),
        "l1_b_spatial": 1.0 + nrm(ks[19], (SG_GROUPS, SG_CHUNK), 0.1),
        "l1_w_out": nrm(ks[20], (SG_WIDTH, D_MODEL), SG_WIDTH ** -0.5),
        "l1_post_g": gain(ks[21], D_MODEL),
    }


def reference(x, l0_pre_g, l0_w_in, l0_b_igate, l0_b_fgate, l0_conv_w, l0_conv_b,
              l0_lambda_q1, l0_lambda_k1, l0_lambda_q2, l0_lambda_k2, l0_da_head_g,
              l0_ml_head_g, l0_w_out, l0_post_g,
              l1_pre_g, l1_w_in, l1_sg_norm_g, l1_w_spatial, l1_b_spatial, l1_w_out, l1_post_g):
    layers = [
        (l0_pre_g, l0_post_g, (l0_w_in, l0_b_igate, l0_b_fgate, l0_conv_w, l0_conv_b,
                               l0_lambda_q1, l0_lambda_k1, l0_lambda_q2, l0_lambda_k2,
                               l0_da_head_g, l0_ml_head_g, l0_w_out)),
        (l1_pre_g, l1_post_g, (l1_sg_norm_g, l1_w_in, l1_w_spatial, l1_b_spatial, l1_w_out)),
    ]
    h = x
    for l in range(DEPTH):
        pre_g, post_g, params = layers[l]
        hn = rmsnorm(h, pre_g)
        if l % 2 == 0:
            y = even_mixer(hn, l, *params)
        else:
            y = odd_mixer(hn, *params)
        h = h + rmsnorm(y, post_g)
    return h
```

```python
import numpy as np
import ml_dtypes
from contextlib import ExitStack
import concourse.bass as bass
import concourse.mybir as mybir
from concourse.bass_utils import run_bass_kernel_spmd

F32, BF16 = mybir.dt.float32, mybir.dt.bfloat16
AF = mybir.ActivationFunctionType
ALU = mybir.AluOpType

D = 1024
EVEN_IN = 8200
ODD_IN = 6144
EPS = 1e-6
NDS = 8


class Trk:
    def __init__(self, nc, es):
        self.nc = nc
        self.E = {"pe": nc.tensor, "act": nc.scalar, "dve": nc.vector, "pool": nc.gpsimd, "sp": nc.sync}
        self.csem, self.ccnt = {}, {}
        for e in ("pe", "act", "dve", "pool"):
            self.csem[e] = es.enter_context(nc.semaphore("c_" + e))
            self.ccnt[e] = 0
        self.dq = {}
        for q in ("sp", "pool", "act"):
            self.dq[q] = dict(sems=[es.enter_context(nc.semaphore("d_%s%d" % (q, i))) for i in range(NDS)],
                              cnt=[0] * NDS, nxt=0)
        self.waited = {e: {} for e in self.E}
        self.res = {}
        self.pend_r, self.pend_w = [], []

    def _deps(self, r, w):
        deps = []
        for k in r:
            ent = self.res.get(k)
            if ent and ent[0]:
                deps.append(ent[0])
        for k in w:
            ent = self.res.get(k)
            if ent:
                if ent[0]:
                    deps.append(ent[0])
                deps.extend(ent[1])
        return deps

    def _wait(self, e, deps):
        best = {}
        for (key, sem, val) in deps:
            if key not in best or best[key][1] < val:
                best[key] = (sem, val)
        for key, (sem, val) in best.items():
            if e == "pe" and key == "c_pe":
                continue
            if self.waited[e].get(key, 0) >= val:
                continue
            self.E[e].wait_ge(sem, val)
            self.waited[e][key] = val

    def _reg(self, ev, r, w):
        for k in r:
            ent = self.res.setdefault(k, [None, []])
            ent[1].append(ev)
        for k in w:
            self.res[k] = [ev, []]

    def op(self, e, emit, r=(), w=(), inc=True):
        r, w = list(r), list(w)
        self._wait(e, self._deps(r, w))
        ins = emit()
        if e == "pe" and not inc:
            self.pend_r += r
            self.pend_w += w
            return ins
        self.ccnt[e] += 1
        ins.then_inc(self.csem[e], 1)
        ev = ("c_" + e, self.csem[e], self.ccnt[e])
        if e == "pe":
            r = r + self.pend_r
            w = w + self.pend_w
            self.pend_r, self.pend_w = [], []
        self._reg(ev, r, w)
        return ins

    def dma(self, q, out, in_, r=(), w=(), **kw):
        r, w = list(r), list(w)
        Q = self.dq[q]
        k = Q["nxt"]
        Q["nxt"] = (k + 1) % NDS
        key = "d_%s%d" % (q, k)
        deps = self._deps(r, w)
        if Q["cnt"][k] > 0:
            deps.append((key, Q["sems"][k], Q["cnt"][k]))
        self._wait(q, deps)
        ins = self.E[q].dma_start(out=out, in_=in_, **kw)
        Q["cnt"][k] += 16
        ins.then_inc(Q["sems"][k], 16)
        self._reg((key, Q["sems"][k], Q["cnt"][k]), r, w)
        return ins

    def barrier(self):
        assert not self.pend_r and not self.pend_w
        evs = [("c_" + e, self.csem[e], self.ccnt[e]) for e in self.csem if self.ccnt[e] > 0]
        for q, Q in self.dq.items():
            for k in range(NDS):
                if Q["cnt"][k] > 0:
                    evs.append(("d_%s%d" % (q, k), Q["sems"][k], Q["cnt"][k]))
        for e in self.E:
            self._wait(e, evs)
        self.res = {}

    def finish(self, e="sp"):
        evs = []
        for q, Q in self.dq.items():
            for k in range(NDS):
                if Q["cnt"][k] > 0:
                    evs.append(("d_%s%d" % (q, k), Q["sems"][k], Q["cnt"][k]))
        self._wait(e, evs)


def build(S=4096, dbg=False):
    assert S % 512 == 0
    NT = S // 128
    NB = S // 512
    nc = bass.Bass("TRN2", target_bir_lowering=False)

    def din(name, shape, dt=F32):
        return nc.dram_tensor(name, list(shape), dt, kind="ExternalInput").ap()

    x = din("x", [S, D])
    l0_pre_g = din("l0_pre_g", [D]); l0_w_in = din("l0_w_in", [D, EVEN_IN])
    l0_b_igate = din("l0_b_igate", [4]); l0_b_fgate = din("l0_b_fgate", [4])
    l0_conv_w = din("l0_conv_w", [4, 1024]); l0_conv_b = din("l0_conv_b", [1024])
    lq1 = din("l0_lambda_q1", [64]); lk1 = din("l0_lambda_k1", [64])
    lq2 = din("l0_lambda_q2", [64]); lk2 = din("l0_lambda_k2", [64])
    l0_da_head_g = din("l0_da_head_g", [128]); l0_ml_head_g = din("l0_ml_head_g", [256])
    l0_w_out = din("l0_w_out", [2048, D]); l0_post_g = din("l0_post_g", [D])
    l1_pre_g = din("l1_pre_g", [D]); l1_w_in = din("l1_w_in", [D, ODD_IN])
    l1_sg_norm_g = din("l1_sg_norm_g", [2048]); l1_w_spatial = din("l1_w_spatial", [8, 128, 128])
    l1_b_spatial = din("l1_b_spatial", [8, 128]); l1_w_out = din("l1_w_out", [2048, D])
    l1_post_g = din("l1_post_g", [D])
    c_ident = din("c_ident", [128, 128], BF16)
    c_tri = din("c_tri", [128, 128], BF16)
    out = nc.dram_tensor("out", [S, D], F32, kind="ExternalOutput").ap()
    yT = nc.dram_tensor("yT_scr", [2048, S], BF16, kind="Internal").ap()
    rows_scr = nc.dram_tensor("rows_scr", [2, 4, S], F32, kind="Internal").ap()
    wo0_bf = nc.dram_tensor("wo0_bf", [2048, D], BF16, kind="Internal").ap()
    w1_bf = nc.dram_tensor("w1_bf", [D, ODD_IN], BF16, kind="Internal").ap()
    wo1_bf = nc.dram_tensor("wo1_bf", [2048, D], BF16, kind="Internal").ap()
    dbg_out = {}
    if dbg:
        dbg_out["dbg_h1"] = nc.dram_tensor("dbg_h1", [S, D], F32, kind="ExternalOutput").ap()
        dbg_out["dbg_yT"] = nc.dram_tensor("dbg_yT", [2048, S], BF16, kind="ExternalOutput").ap()
        dbg_out["dbg_rows"] = nc.dram_tensor("dbg_rows", [3, 4, S], F32, kind="ExternalOutput").ap()
        dbg_out["dbg_cols"] = nc.dram_tensor("dbg_cols", [4, 3, 128, S // 128], F32, kind="ExternalOutput").ap()
        dbg_out["dbg_r32"] = nc.dram_tensor("dbg_r32", [4, 2, S], F32, kind="ExternalOutput").ap()

    w0v = l0_w_in.rearrange("(kc p) e -> p kc e", p=128)
    w1v = l1_w_in.rearrange("(kc p) e -> p kc e", p=128)
    wo0v = l0_w_out.rearrange("(kc p) e -> p kc e", p=128)
    wo1v = l1_w_out.rearrange("(kc p) e -> p kc e", p=128)
    w1bv = w1_bf.rearrange("(kc p) e -> p kc e", p=128)
    wo0bv = wo0_bf.rearrange("(kc p) e -> p kc e", p=128)
    wo1bv = wo1_bf.rearrange("(kc p) e -> p kc e", p=128)

    with ExitStack() as es:
        E = es.enter_context
        T = Trk(nc, es)

        def sb(name, shape, dt=F32, scope=None):
            return (scope or es).enter_context(nc.sbuf_tensor(name, list(shape), dt))

        S2 = [E(nc.psum_tensor("S2_%d" % i, [128, 1024], F32)) for i in range(2)]
        banks = [None] * 4 + [E(nc.psum_tensor("bank%d" % i, [128, 512], F32)) for i in range(4, 8)]

        def bk(i):
            if i < 4:
                return S2[i // 2][:, (i % 2) * 512:(i % 2 + 1) * 512]
            return banks[i]

        def bkb(i):
            if i < 4:
                return S2[i // 2][:].bitcast(BF16)[:, (i % 2) * 1024:(i % 2 + 1) * 1024]
            return banks[i][:].bitcast(BF16)

        ident = sb("ident", [128, 128], BF16)
        tri = sb("tri", [128, 128], BF16)
        ones_bf = sb("ones_bf", [128, 128], BF16)
        inv128_bf = sb("inv128_bf", [128, 128], BF16)
        ones_f = sb("ones_f", [128, 128], F32)
        T.dma("sp", ident[:], c_ident[:, :], w=["ident"])
        T.dma("sp", tri[:], c_tri[:, :], w=["tri"])
        T.op("pool", lambda: nc.gpsimd.memset(ones_bf[:], 1.0), w=["ones_bf"])
        T.op("pool", lambda: nc.gpsimd.memset(inv128_bf[:], 1.0 / 128), w=["inv128_bf"])
        T.op("pool", lambda: nc.gpsimd.memset(ones_f[:], 1.0), w=["ones_f"])

        NSTv = [2]
        wst = []
        wst_i = [0]
        wst_gen = [0]

        def alloc_wst(scope):
            wst_gen[0] += 1
            wst[:] = [sb("wst%d_%d" % (wst_gen[0], i), [128, 8, 512], F32, scope) for i in range(NSTv[0])]

        def load_w(dst_fn, srcs, dst_key, defer=False):
            b = wst_i[0] % len(wst)
            wst_i[0] += 1
            tot = 0
            kcn = srcs[0][0].shape[1]
            for (src, off) in srcs:
                n = src.shape[2]
                T.dma("sp", wst[b][:, 0:kcn, off:off + n], src, w=[("wst", b, off)], r=[])
                tot = max(tot, off + n)
            def cast():
                T.op("dve", lambda: nc.vector.tensor_copy(out=dst_fn(), in_=wst[b][:, 0:kcn, 0:tot]),
                     r=[("wst", b, off) for (_, off) in srcs], w=[dst_key])
            if defer:
                return cast
            cast()

        def rstd_from_ssq(ssq_ap, tmp_ap, out_ap, n, keys_r, key_tmp, key_out):
            T.op("act", lambda: nc.scalar.activation(out=tmp_ap, in_=ssq_ap, func=AF.Ln, scale=1.0 / n, bias=EPS),
                 r=keys_r, w=[key_tmp])
            T.op("act", lambda: nc.scalar.activation(out=out_ap, in_=tmp_ap, func=AF.Exp, scale=-0.5),
                 r=[key_tmp], w=[key_out])

        with ExitStack() as sc0:
            hnT = sb("hnT", [128, 8, S], BF16, sc0)
            with ExitStack() as scA:
                NSTv[0] = 2
                alloc_wst(scA)
                QT = sb("QT", [128, S], BF16, scA)
                KT = sb("KT", [128, S], BF16, scA)
                V = sb("V", [128, NT, 128], BF16, scA)
                zT2 = [sb("zT%d" % i, [128, S], BF16, scA) for i in range(2)]
                Ocp = sb("Ocp", [128, 2, 512], F32, scA)
                Lcp = sb("Lcp", [128, 2, 512], F32, scA)
                pend_tail = [None]
                wbf2 = [sb("wbfA%d" % i, [128, 8, 512], BF16, scA) for i in range(2)]
                P2 = [sb("P2_%d" % b, [128, 2, 512], BF16, scA) for b in range(2)]
                r1 = sb("r1", [128, 512], F32, scA)
                t1 = sb("t1", [128, 512], F32, scA)
                t2 = sb("t2", [128, 512], F32, scA)
                oT = sb("oT", [128, 512], F32, scA)
                sq = sb("sqA", [128, 512], BF16, scA)
                lnv = sb("lnvA", [128, 512], F32, scA)
                rsd = sb("rsdA", [128, 512], F32, scA)
                yb = [sb("ybA%d" % i, [128, 512], BF16, scA) for i in range(2)]
                lam4 = sb("lam4", [128, 4, 64], F32, scA)
                lamj = sb("lamj", [128, 64], F32, scA)
                lams = sb("lams", [128, 8], F32, scA)
                gda = sb("gda", [128, 2, 16], F32, scA)
                for i, a in enumerate((lq1, lk1, lq2, lk2)):
                    T.dma("sp", lam4[:, i, :], a.partition_broadcast(128), w=[("lam4", i)])
                for i in range(2):
                    T.op("dve", lambda: nc.vector.tensor_tensor(out=lamj[:], in0=lam4[:, 2 * i, :], in1=lam4[:, 2 * i + 1, :],
                                                                op=ALU.mult),
                         r=[("lam4", 2 * i), ("lam4", 2 * i + 1)], w=["lamj"])
                    T.op("dve", lambda: nc.vector.reduce_sum(out=lams[:, i:i + 1], in_=lamj[:], axis=mybir.AxisListType.X),
                         r=["lamj"], w=[("lams", i)])
                    T.op("act", lambda: nc.scalar.activation(out=lams[:, 2 + i:3 + i], in_=lams[:, i:i + 1], func=AF.Exp),
                         r=[("lams", i)], w=[("lams", 2 + i)])
                T.op("dve", lambda: nc.vector.tensor_tensor(out=lams[:, 4:5], in0=lams[:, 3:4], in1=lams[:, 2:3], op=ALU.subtract),
                     r=[("lams", 2), ("lams", 3)], w=[("lams", 4)])
                T.op("dve", lambda: nc.vector.tensor_scalar(out=lams[:, 5:6], in0=lams[:, 4:5], scalar1=-0.2, scalar2=None,
                                                            op0=ALU.add),
                     r=[("lams", 4)], w=["neglam"])
                neglam = lams[:, 5:6]
                T.dma("sp", gda[:, 0, 0:1], l0_da_head_g.rearrange("(p o) -> p o", o=1), w=["gda0"])
                T.op("dve", lambda: nc.vector.tensor_scalar(out=gda[:, 1, 0:1], in0=gda[:, 0, 0:1], scalar1=0.8, scalar2=None,
                                                            op0=ALU.mult), r=["gda0"], w=["gda"])

                def load_head(hh):
                    srcs = [(w0v[:, :, off + hh * 128: off + (hh + 1) * 128], i * 128)
                            for i, off in enumerate((0, 1024, 2048, 3072))]
                    load_w(lambda: wbf2[hh % 2][:, :, :], srcs, ("wbfA", hh % 2))

                pcnt = [0]

                def proj_block(h, tb, wbf, wkey, zT):
                    for (dst, c0, kind) in ((QT, 0, "q"), (KT, 128, "k"), (zT, 384, "z")):
                        pb = pcnt[0] % 4
                        pcnt[0] += 1
                        for kc in range(8):
                            T.op("pe", lambda: nc.tensor.matmul(bk(pb)[:, :], lhsT=wbf[:, kc, c0:c0 + 128],
                                                                rhs=hnT[:, kc, tb * 512:(tb + 1) * 512],
                                                                start=(kc == 0), stop=(kc == 7)),
                                 r=[wkey, ("hnT", tb)], w=[("bank", pb)], inc=(kc == 7))
                        d = dst[:, tb * 512:(tb + 1) * 512]
                        if kind == "z":
                            T.op("act", lambda: nc.scalar.activation(out=d, in_=bk(pb)[:, :], func=AF.Silu),
                                 r=[("bank", pb)], w=[("z", h % 2, tb)])
                        else:
                            T.op("dve", lambda: nc.vector.tensor_copy(out=d, in_=bk(pb)[:, :]), r=[("bank", pb)], w=[(kind, tb)])
                    tg = tb
                    pb = pcnt[0] % 4
                    pcnt[0] += 1
                    for ti in range(4):
                        tt = tg * 4 + ti
                        for kc in range(8):
                            T.op("pe", lambda: nc.tensor.matmul(bk(pb)[:, ti * 128:(ti + 1) * 128],
                                                                lhsT=hnT[:, kc, tt * 128:(tt + 1) * 128],
                                                                rhs=wbf[:, kc, 256:384], start=(kc == 0), stop=(kc == 7)),
                                 r=[wkey, ("hnT", tg)], w=[("bank", pb)], inc=(kc == 7 and ti == 3))
                    T.op("dve", lambda: nc.vector.tensor_copy(out=V[:, tg * 4:(tg + 1) * 4, :],
                                                              in_=bk(pb)[:, :].rearrange("p (a b) -> p a b", a=4)),
                         r=[("bank", pb)], w=[("v", tg)])

                load_head(0)
                with ExitStack() as sc1:
                    g0bc = sb("g0bc", [128, D], F32, sc1)
                    T.dma("sp", g0bc[:], l0_pre_g.partition_broadcast(128), w=["g0bc"])
                    xt = [sb("xt%d" % i, [128, D], F32, sc1) for i in range(3)]
                    xn = [sb("xn%d" % i, [128, D], BF16, sc1) for i in range(2)]
                    junk = [sb("junk%d" % i, [128, D], BF16, sc1) for i in range(2)]
                    st = [sb("st%d" % i, [128, 4], F32, sc1) for i in range(2)]
                    for i in range(NT):
                        b = i % 2
                        bx = i % 3
                        T.dma("sp" if i % 2 == 0 else "pool", xt[bx][:], x[i * 128:(i + 1) * 128, :], w=[("xt", bx)])
                        T.op("act", lambda: nc.scalar.activation(out=junk[b][:], in_=xt[bx][:], func=AF.Square,
                                                                 accum_out=st[b][:, 0:1]),
                             r=[("xt", bx)], w=[("junk", b), ("ssq", b)])
                        rstd_from_ssq(st[b][:, 0:1], st[b][:, 1:2], st[b][:, 2:3], D, [("ssq", b)], ("lnv", b), ("rstd", b))
                        T.op("dve", lambda: nc.vector.scalar_tensor_tensor(out=xn[b][:], in0=xt[bx][:], scalar=st[b][:, 2:3],
                                                                           in1=g0bc[:], op0=ALU.mult, op1=ALU.mult),
                             r=[("xt", bx), ("rstd", b), "g0bc"], w=[("xn", b)])
                        pb = 6 + b
                        for kc in range(8):
                            T.op("pe", lambda: nc.tensor.transpose(bkb(pb)[:, kc * 128:(kc + 1) * 128],
                                                                   xn[b][:, kc * 128:(kc + 1) * 128], ident[:]),
                                 r=[("xn", b), "ident"], w=[("bank", pb)], inc=(kc == 7))
                        src = bkb(pb).rearrange("p (k t) -> p k t", k=8)
                        dst = hnT[:, :, i * 128:(i + 1) * 128]
                        if i % 2 == 0:
                            T.op("dve", lambda: nc.vector.tensor_copy(out=dst, in_=src), r=[("bank", pb)], w=[("hnT", i // 4)])
                        else:
                            T.op("act", lambda: nc.scalar.copy(out=dst, in_=src), r=[("bank", pb)], w=[("hnT", i // 4)])
                        if i % 4 == 3 and i // 4 >= 1:
                            proj_block(0, i // 4 - 1, wbf2[0], ("wbfA", 0), zT2[0])
                    proj_block(0, NB - 1, wbf2[0], ("wbfA", 0), zT2[0])
                for h in range(8):
                    wbf = wbf2[h % 2]
                    wkey = ("wbfA", h % 2)
                    zT = zT2[h % 2]
                    if h > 0:
                        for tb in range(NB):
                            proj_block(h, tb, wbf, wkey, zT)
                    if h == 0:
                        cast_jobs = []
                        for (src, dst, nrow) in ((l0_w_out, wo0_bf, 2048), (l1_w_in, w1_bf, D), (l1_w_out, wo1_bf, 2048)):
                            for r0 in range(0, nrow, 256):
                                cast_jobs.append((dst[r0:r0 + 256, :], src[r0:r0 + 256, :]))
                    if h + 1 < 8:
                        load_head(h + 1)

                    for qb in range(NB):
                        nj = 4 * qb + 4
                        if h >= 1 and cast_jobs:
                            cj = cast_jobs.pop(0)
                            T.dma("pool", cj[0], cj[1])

                        def qk(j):
                            c0 = max(0, j - 4 * qb) * 128
                            sbuf_i = j % 2
                            for c in range(2):
                                pb = 2 * sbuf_i + c
                                T.op("pe", lambda: nc.tensor.matmul(bk(pb)[:, c0:512],
                                                                    lhsT=KT[c * 64:(c + 1) * 64, j * 128:(j + 1) * 128],
                                                                    rhs=QT[c * 64:(c + 1) * 64, qb * 512 + c0:(qb + 1) * 512],
                                                                    start=True, stop=True),
                                     r=[("k", j // 4), ("q", qb)], w=[("bank", pb)], inc=True)

                        qk(0)
                        for j in range(nj):
                            c0 = max(0, j - 4 * qb) * 128
                            si = j % 2
                            if j + 1 < nj:
                                qk(j + 1)
                            T.op("act", lambda: nc.scalar.activation(
                                out=P2[si][:, :, c0:512],
                                in_=S2[si][:, :].rearrange("p (c q) -> p c q", c=2)[:, :, c0:512],
                                func=AF.Exp, scale=0.125),
                                 r=[("bank", 2 * si), ("bank", 2 * si + 1)], w=[("P", si)])
                            if j >= 4 * qb:
                                T.op("pool", lambda: nc.gpsimd.tensor_tensor(out=P2[si][:, :, c0:c0 + 128],
                                                                             in0=P2[si][:, :, c0:c0 + 128],
                                                                             in1=tri[:, None, :].to_broadcast([128, 2, 128]),
                                                                             op=ALU.mult),
                                     r=[("P", si), "tri"], w=[("P", si)])
                            if pend_tail[0] is not None and j == min(8, nj - 1):
                                pend_tail[0](2 * si)
                                pend_tail[0] = None
                            for c in range(2):
                                T.op("pe", lambda: nc.tensor.matmul(bk(4 + c)[:, c0:512], lhsT=V[:, j, :],
                                                                    rhs=P2[si][:, c, c0:512], start=(j == 0), stop=(j == nj - 1)),
                                     r=[("P", si), ("v", j // 4)], w=[("bank", 4 + c)], inc=False)
                                T.op("pe", lambda: nc.tensor.matmul(bk(6 + c)[:, c0:512], lhsT=ones_bf[:],
                                                                    rhs=P2[si][:, c, c0:512], start=(j == 0), stop=(j == nj - 1)),
                                     r=[("P", si), "ones_bf"], w=[("bank", 6 + c)], inc=True)
                        for c in range(2):
                            T.op("dve", lambda: nc.vector.tensor_copy(out=Lcp[:, c, :], in_=bk(6 + c)[:, :]), r=[("bank", 6 + c)], w=[("Lcp", c)])
                            T.op("dve", lambda: nc.vector.tensor_copy(out=Ocp[:, c, :], in_=bk(4 + c)[:, :]), r=[("bank", 4 + c)], w=[("Ocp", c)])
                        T.op("dve", lambda: nc.vector.reciprocal(out=r1[:], in_=Lcp[:, 0, :]), r=[("Lcp", 0)], w=["r1"])
                        T.op("dve", lambda: nc.vector.tensor_tensor(out=t1[:], in0=Ocp[:, 0, :], in1=r1[:], op=ALU.mult),
                             r=[("Ocp", 0), "r1"], w=["t1"])
                        T.op("dve", lambda: nc.vector.reciprocal(out=r1[:], in_=Lcp[:, 1, :]), r=[("Lcp", 1)], w=["r1"])
                        T.op("dve", lambda: nc.vector.tensor_tensor(out=t2[:], in0=Ocp[:, 1, :], in1=r1[:], op=ALU.mult),
                             r=[("Ocp", 1), "r1"], w=["t2"])
                        T.op("dve", lambda: nc.vector.scalar_tensor_tensor(out=oT[:], in0=t2[:], scalar=neglam, in1=t1[:],
                                                                           op0=ALU.mult, op1=ALU.add),
                             r=["t1", "t2", "neglam"], w=["oT"])
                        T.op("dve", lambda: nc.vector.tensor_tensor(out=sq[:], in0=oT[:], in1=oT[:], op=ALU.mult),
                             r=["oT"], w=["sqA"])

                        def tail(pbank, h=h, qb=qb, zT=zT):
                            T.op("pe", lambda: nc.tensor.matmul(bk(pbank)[:, :], lhsT=inv128_bf[:], rhs=sq[:], start=True, stop=True),
                                 r=["sqA", "inv128_bf"], w=[("bank", pbank)])
                            T.op("act", lambda: nc.scalar.activation(out=lnv[:], in_=bk(pbank)[:, :], func=AF.Ln, bias=EPS),
                                 r=[("bank", pbank)], w=["lnvA"])
                            T.op("act", lambda: nc.scalar.activation(out=rsd[:], in_=lnv[:], func=AF.Exp, scale=-0.5),
                                 r=["lnvA"], w=["rsdA"])
                            T.op("dve", lambda: nc.vector.tensor_tensor(out=t1[:], in0=oT[:], in1=rsd[:], op=ALU.mult),
                                 r=["oT", "rsdA"], w=["t1"])
                            ybb = yb[qb % 2]
                            T.op("dve", lambda: nc.vector.scalar_tensor_tensor(out=ybb[:], in0=t1[:], scalar=gda[:, 1, 0:1],
                                                                               in1=zT[:, qb * 512:(qb + 1) * 512],
                                                                               op0=ALU.mult, op1=ALU.mult),
                                 r=["t1", "gda", ("z", h % 2, qb)], w=[("ybA", qb % 2)])
                            T.dma("pool", yT[h * 128:(h + 1) * 128, qb * 512:(qb + 1) * 512], ybb[:],
                                  r=[("ybA", qb % 2)], w=[("yT", h, qb)])

                        assert pend_tail[0] is None
                        pend_tail[0] = tail
                pend_tail[0](0)
                pend_tail[0] = None
                while cast_jobs:
                    cj = cast_jobs.pop(0)
                    T.dma("pool", cj[0], cj[1])
                T.barrier()

            with ExitStack() as scB:
                NSTv[0] = 1
                alloc_wst(scB)
                rowA_t = sb("rowA", [64, S], F32, scB)
                rowG_t = sb("rowG", [64, S], F32, scB)
                rowA = rowA_t[0:4, :]
                rowG = rowG_t[0:4, :]
                rows1 = (rowA_t, rowG_t)
                rowM = sb("rowM", [4, S], F32, scB)
                rowc = sb("rowc", [4, 4, NT], F32, scB)
                rowc1 = sb("rowc1", [64, 2, NT], F32, scB)
                Lm2 = sb("Lm2", [NT + 1, 2, 128], F32, scB)
                Rm = sb("Rm", [NT + 1, NT], F32, scB)
                gb = sb("gb", [4, 3, 16], F32, scB)
                wif = sb("wif", [128, 8, 8], BF16, scB)
                colw = sb("colw", [128, NT], F32, scB)
                colf = sb("colf", [128, NT], F32, scB)
                decb = sb("decb", [128, NT], F32, scB)
                qTm = sb("qTm", [128, S], BF16, scB)
                kTm = sb("kTm", [128, S], BF16, scB)
                wqk2 = [sb("wqk%d" % i, [128, 8, 256], BF16, scB) for i in range(2)]
                wvoz = sb("wvoz", [128, 8, 768], BF16, scB)
                pre = [sb("pre%d" % i, [128, 515], F32, scB) for i in range(2)]
                acc2 = [sb("acc%d" % i, [128, 512], F32, scB) for i in range(2)]
                cw_all = sb("cw", [128, 8, 16], F32, scB)
                cb_all = sb("cb", [128, 8, 16], F32, scB)
                for hh in range(4):
                    for qi in range(2):
                        ch0 = qi * 512 + hh * 128
                        T.dma("pool", cw_all[:, hh * 2 + qi, 0:4], l0_conv_w[:, ch0:ch0 + 128].rearrange("j c -> c j"),
                              w=[("cw", hh, qi)], allow_slow_non_contiguous=True)
                        T.dma("pool", cb_all[:, hh * 2 + qi, 0:1], l0_conv_b[ch0:ch0 + 128].rearrange("(p o) -> p o", o=1),
                              w=[("cb", hh, qi)])
                gml = sb("gml", [128, 256], F32, scB)
                Cst = sb("Cst", [128, 257], F32, scB)
                Cd = sb("Cd", [128, 257], F32, scB)
                Cdb = sb("Cdb", [128, 257], BF16, scB)
                SmT4 = [sb("SmT4_%d" % i, [128, 4, 128], BF16, scB) for i in range(2)]
                ktok4 = [sb("ktok4_%d" % i, [128, 4, 128], BF16, scB) for i in range(2)]
                vw4 = [sb("vw4_%d" % i, [128, 4, 257], BF16, scB) for i in range(2)]
                sg4 = [sb("sg4_%d" % i, [128, 4, 256], BF16, scB) for i in range(2)]
                zs4 = [sb("zs4_%d" % i, [128, 4, 256], BF16, scB) for i in range(2)]
                hmt4 = [sb("hmt4_%d" % i, [128, 4, 256], BF16, scB) for i in range(2)]
                ssq4 = [sb("ssq4_%d" % i, [128, 4], F32, scB) for i in range(2)]
                rs4 = [sb("rs4_%d" % i, [128, 8], F32, scB) for i in range(2)]
                stR = sb("stR", [128, 4, 4], F32, scB)
                junkB = sb("junkB", [128, 256], BF16, scB)
                ytok4 = sb("ytok4", [128, 4, 256], BF16, scB)
                yTb = [sb("yTb%d" % i, [128, 2, 512], BF16, scB) for i in range(2)]

                T.dma("sp", gml[:], l0_ml_head_g.partition_broadcast(128), w=["gml"])
                T.dma("sp", gb[:, 0, 0:1], l0_b_igate.rearrange("(p o) -> p o", o=1), w=["gb0"])
                T.dma("sp", gb[:, 2, 0:1], l0_b_fgate.rearrange("(p o) -> p o", o=1), w=["gb2"])
                T.op("dve", lambda: nc.vector.tensor_scalar(out=gb[:, 1, 0:1], in0=gb[:, 2, 0:1], scalar1=-1.0, scalar2=None,
                                                            op0=ALU.mult), r=["gb2"], w=["gb1"])
                load_w(lambda: wif[:, :, :], [(w0v[:, :, 6144:6152], 0)], "wif")
                for gi, (row, key) in enumerate(((rowA, "rowA"), (rowG, "rowG"))):
                    for tb in range(NB):
                        pb = tb % 2
                        for kc in range(8):
                            T.op("pe", lambda: nc.tensor.matmul(bk(pb)[0:4, :], lhsT=wif[:, kc, gi * 4:gi * 4 + 4],
                                                                rhs=hnT[:, kc, tb * 512:(tb + 1) * 512],
                                                                start=(kc == 0), stop=(kc == 7)),
                                 r=["wif", ("hnT", tb)], w=[("bank", pb)], inc=(kc == 7))
                        T.op("dve", lambda: nc.vector.tensor_copy(out=row[:, tb * 512:(tb + 1) * 512], in_=bk(pb)[0:4, :]),
                             r=[("bank", pb)], w=[key])
                T.op("act", lambda: nc.scalar.activation(out=rowG[:], in_=rowG[:], func=AF.Exp, scale=-1.0, bias=gb[:, 1, 0:1]),
                     r=["rowG", "gb1"], w=["rowG"])
                T.op("act", lambda: nc.scalar.activation(out=rowG[:], in_=rowG[:], func=AF.Ln, bias=1.0),
                     r=["rowG"], w=["rowG"])
                T.op("dve", lambda: nc.vector.tensor_tensor_scan(out=rowG[:], data0=ones_f[0:4, 0:1].to_broadcast([4, S]),
                                                                 data1=rowG[:], initial=0.0, op0=ALU.mult, op1=ALU.add),
                     r=["rowG", "ones_f"], w=["rowG"])
                T.op("dve", lambda: nc.vector.scalar_tensor_tensor(out=rowA[:], in0=rowA[:], scalar=gb[:, 0, 0:1], in1=rowG[:],
                                                                   op0=ALU.add, op1=ALU.add),
                     r=["rowA", "rowG", "gb0"], w=["rowA"])
                T.op("dve", lambda: nc.vector.tensor_tensor_scan(out=rowM[:], data0=rowA[:], data1=rowA[:], initial=0.0,
                                                                 op0=ALU.max, op1=ALU.max),
                     r=["rowA"], w=["rowM"])
                mend = rowM[:].rearrange("p (c t) -> p c t", t=128)[:, :, 127]
                T.op("dve", lambda: nc.vector.tensor_scalar(out=rowc[:, 0, :], in0=mend, scalar1=-1.0, scalar2=None, op0=ALU.mult),
                     r=["rowM"], w=[("rowc", 0)])
                T.op("dve", lambda: nc.vector.memset(rowc[:, 1, 0:1], 0.0), w=[("rowc", 1, 0)])
                if NT > 1:
                    T.op("dve", lambda: nc.vector.tensor_copy(out=rowc[:, 1, 1:NT], in_=mend[:, 0:NT - 1]),
                         r=["rowM"], w=[("rowc", 1, 1)])
                T.op("dve", lambda: nc.vector.tensor_tensor(out=rowc[:, 2, :], in0=rowc[:, 1, :], in1=rowc[:, 0, :], op=ALU.add),
                     r=[("rowc", 0), ("rowc", 1, 0), ("rowc", 1, 1)], w=[("rowc", 2)])

                if dbg:
                    T.dma("sp", dbg_out["dbg_rows"][0], rowA, r=["rowA"])
                    T.dma("sp", dbg_out["dbg_rows"][1], rowG, r=["rowG"])
                    T.dma("sp", dbg_out["dbg_rows"][2], rowM[:], r=["rowM"])
                T.dma("sp", rows_scr[0], rowA, r=["rowA"], w=[("rows_scr", 0)])
                T.dma("sp", rows_scr[1], rowG, r=["rowG"], w=[("rows_scr", 1)])
                T.op("dve", lambda: nc.vector.memset(Lm2[:], 1.0), w=[("Lm2", 0), ("Lm2", 1)])
                T.op("dve", lambda: nc.vector.tensor_copy(out=Rm[0:NT, :], in_=ident[0:NT, 0:NT]), r=["ident"], w=["Rm_id"])
                for h in range(4):
                    for which in range(2):
                        T.dma("sp", Lm2[0:NT, which, :], rows_scr[which, h, :].rearrange("(c t) -> c t", t=128),
                              r=[("rows_scr", which)], w=[("Lm2", which)])
                    T.dma("sp", Rm[NT:NT + 1, :], rowc[h:h + 1, 0, :], r=[("rowc", 0)], w=["Rm_m"])
                    T.dma("sp", rowc1[32:33, 1, :], rowc[h:h + 1, 2, :], r=[("rowc", 2)], w=[("rowc1", 1)])
                    for which, (bnk, dstc, bias) in enumerate(((0, colw, 0.0), (1, colf, 0.5 * float(np.log(128.0))))):
                        T.op("pe", lambda: nc.tensor.matmul(bk(bnk)[:, 0:NT], lhsT=Lm2[:, which, :], rhs=Rm[:, :], start=True, stop=True),
                             r=[("Lm2", which), "Rm_id", "Rm_m"], w=[("bank", bnk)])
                        T.op("act", lambda: nc.scalar.activation(out=dstc[:], in_=bk(bnk)[:, 0:NT], func=AF.Exp, bias=bias),
                             r=[("bank", bnk)], w=[("col", which)])
                    T.op("pe", lambda: nc.tensor.matmul(bk(2)[:, 0:NT], lhsT=ones_f[32:33, 0:128], rhs=rowc1[32:33, 1, :],
                                                        start=True, stop=True),
                         r=[("rowc1", 1), "ones_f"], w=[("bank", 2)])
                    T.op("act", lambda: nc.scalar.activation(out=decb[:], in_=bk(2)[:, 0:NT], func=AF.Exp),
                         r=[("bank", 2)], w=["decb"])
                    if dbg:
                        T.dma("sp", dbg_out["dbg_cols"][h, 0], colw[:], r=[("col", 0)])
                        T.dma("sp", dbg_out["dbg_cols"][h, 1], colf[:], r=[("col", 1)])
                        T.dma("sp", dbg_out["dbg_cols"][h, 2], decb[:], r=["decb"])
                    def load_wqk(hh, defer=False):
                        return load_w(lambda: wqk2[hh % 2][:, :, :], [(w0v[:, :, 4096 + hh * 128:4096 + (hh + 1) * 128], 0),
                                                                      (w0v[:, :, 4608 + hh * 128:4608 + (hh + 1) * 128], 128)],
                                      ("wqk", hh % 2), defer=defer)

                    wqk = wqk2[h % 2]
                    if h == 0:
                        load_wqk(0)
                    cast_voz0 = load_w(lambda: wvoz[:, :, 0:512], [(w0v[:, :, 5120 + h * 256:5120 + (h + 1) * 256], 0),
                                                                   (w0v[:, :, 6152 + h * 256:6152 + (h + 1) * 256], 256)], "wvoz0",
                                       defer=True)
                    steps = [(qi, tb) for qi in range(2) for tb in range(NB)]
                    dsts = (qTm, kTm)

                    def conv_front(n):
                        qi, tb = steps[n]
                        pb = 4 + (n % 2)
                        for kc in range(8):
                            T.op("pe", lambda: nc.tensor.matmul(bk(pb)[:, :], lhsT=wqk[:, kc, qi * 128:(qi + 1) * 128],
                                                                rhs=hnT[:, kc, tb * 512:(tb + 1) * 512],
                                                                start=(kc == 0), stop=(kc == 7)),
                                 r=[("wqk", h % 2), ("hnT", tb)], w=[("bank", pb)], inc=(kc == 7))
                        T.op("act", lambda: nc.scalar.copy(out=pre[n % 2][:, 3:515], in_=bk(pb)[:, :]),
                             r=[("bank", pb)], w=[("pre", n % 2, "m")])

                    def conv_back(n):
                        qi, tb = steps[n]
                        pr = pre[n % 2]
                        if tb == 0:
                            T.op("dve", lambda: nc.vector.memset(pr[:, 0:3], 0.0), w=[("pre", n % 2, "c")])
                        T.op("dve", lambda: nc.vector.tensor_scalar(out=acc2[n % 2][:], in0=pr[:, 3:515], scalar1=cw_all[:, h * 2 + qi, 3:4],
                                                                    scalar2=None, op0=ALU.mult),
                             r=[("pre", n % 2, "m"), ("cw", h, qi)], w=[("acc", n % 2)])
                        for j in (2, 1, 0):
                            T.op("dve", lambda: nc.vector.scalar_tensor_tensor(out=acc2[n % 2][:], in0=pr[:, j:j + 512],
                                                                               scalar=cw_all[:, h * 2 + qi, j:j + 1], in1=acc2[n % 2][:],
                                                                               op0=ALU.mult, op1=ALU.add),
                                 r=[("pre", n % 2, "m"), ("pre", n % 2, "c"), ("cw", h, qi), ("acc", n % 2)], w=[("acc", n % 2)])
                        T.op("act", lambda: nc.scalar.activation(out=dsts[qi][:, tb * 512:(tb + 1) * 512], in_=acc2[n % 2][:],
                                                                 func=AF.Silu, bias=cb_all[:, h * 2 + qi, 0:1]),
                             r=[("acc", n % 2), ("cb", h, qi)], w=[("qk", qi, tb)])
                        if n + 1 < len(steps) and steps[n + 1][1] > 0:
                            T.op("dve", lambda: nc.vector.tensor_copy(out=pre[(n + 1) % 2][:, 0:3], in_=pr[:, 512:515]),
                                 r=[("pre", n % 2, "m")], w=[("pre", (n + 1) % 2, "c")])

                    conv_front(0)
                    cast_voz1 = None
                    for n in range(len(steps)):
                        if n + 1 < len(steps):
                            conv_front(n + 1)
                        conv_back(n)
                        if n == len(steps) // 2 - 1:
                            cast_voz0()
                            cast_voz1 = load_w(lambda: wvoz[:, :, 512:768], [(w0v[:, :, 7176 + h * 256:7176 + (h + 1) * 256], 0)],
                                               "wvoz1", defer=True)
                    cast_voz1()
                    T.op("dve", lambda: nc.vector.memset(Cst[:], 0.0), w=["Cst"])

                    import os
                    _sk = os.environ.get("KSKIP", "")

                    def P_pre(tb):
                        par = tb % 2
                        for ci in range(4):
                            c = tb * 4 + ci
                            T.op("pe", lambda: nc.tensor.transpose(bkb(7)[:, 512 + ci * 128:512 + (ci + 1) * 128],
                                                                   kTm[:, c * 128:(c + 1) * 128], ident[:]),
                                 r=[("qk", 1, tb), "ident"], w=[("bank", 7)], inc=(ci == 3))
                        T.op("act", lambda: nc.scalar.copy(out=ktok4[par][:], in_=bkb(7)[:, 512:1024].rearrange("p (a b) -> p a b", a=4)),
                             r=[("bank", 7)], w=[("ktok", par, ci) for ci in range(4)])
                        if "v" not in _sk:
                          T.op("pool", lambda: nc.gpsimd.tensor_copy(out=vw4[par][:, :, 256], in_=colw[:, tb * 4:(tb + 1) * 4]),
                             r=[("col", 0)], w=[("vw1", par)])

                    def P_chunk(tb, ci):
                        par = tb % 2
                        if True:
                            c = tb * 4 + ci
                            cs = slice(c * 128, (c + 1) * 128)
                            xb = ci % 2
                            T.op("pe", lambda: nc.tensor.matmul(bk(xb)[:, 0:128], lhsT=kTm[:, cs], rhs=qTm[:, cs], start=True, stop=True),
                                 r=[("qk", 0, tb), ("qk", 1, tb)], w=[("bank", xb)])
                            for kc in range(8):
                                T.op("pe", lambda: nc.tensor.matmul(bk(xb)[:, 128:384], lhsT=hnT[:, kc, cs], rhs=wvoz[:, kc, 0:256],
                                                                    start=(kc == 0), stop=(kc == 7)),
                                     r=[("hnT", tb), "wvoz0"], w=[("bank", xb)], inc=(kc == 7))
                            hf = ci % 2
                            ob, zb = 2, 3 + ci // 2
                            for kc in range(8):
                                T.op("pe", lambda: nc.tensor.matmul(bk(ob)[:, hf * 256:(hf + 1) * 256], lhsT=hnT[:, kc, cs],
                                                                    rhs=wvoz[:, kc, 256:512], start=(kc == 0), stop=(kc == 7)),
                                     r=[("hnT", tb), "wvoz0"], w=[("bank", 2)], inc=(kc == 7))
                            for kc in range(8):
                                T.op("pe", lambda: nc.tensor.matmul(bk(zb)[:, hf * 256:(hf + 1) * 256], lhsT=hnT[:, kc, cs],
                                                                    rhs=wvoz[:, kc, 512:768], start=(kc == 0), stop=(kc == 7)),
                                     r=[("hnT", tb), "wvoz1"], w=[("bank", 3 + ci // 2)], inc=(kc == 7))
                            T.op("dve", lambda: nc.vector.tensor_tensor(out=SmT4[par][:, ci, :], in0=bk(xb)[:, 0:128], in1=tri[:], op=ALU.mult),
                                 r=[("bank", xb), "tri"], w=[("SmT", par, ci)])
                            T.op("dve", lambda: nc.vector.tensor_scalar(out=vw4[par][:, ci, 0:256], in0=bk(xb)[:, 128:384],
                                                                        scalar1=colw[:, c:c + 1], scalar2=None, op0=ALU.mult),
                                 r=[("bank", xb), ("col", 0)], w=[("vw", par, ci)])
                            if hf == 1 and "s" not in _sk:
                                T.op("act", lambda: nc.scalar.activation(out=sg4[par][:, ci - 1:ci + 1, :],
                                                                         in_=bk(ob)[:, :].rearrange("p (a b) -> p a b", a=2),
                                                                         func=AF.Sigmoid),
                                     r=[("bank", 2)], w=[("sg", par, ci // 2)])

                    def P_post(tb):
                        par = tb % 2
                        for pr_ in range(2 if "z" not in _sk else 0):
                            T.op("act", lambda: nc.scalar.activation(out=zs4[par][:, 2 * pr_:2 * pr_ + 2, :],
                                                                     in_=bk(3 + pr_)[:, :].rearrange("p (a b) -> p a b", a=2),
                                                                     func=AF.Silu),
                                 r=[("bank", 3 + pr_)], w=[("zs", par, pr_)])
                        if "g" not in _sk:
                          T.op("pool", lambda: nc.gpsimd.tensor_tensor(out=zs4[par][:], in0=zs4[par][:],
                                                                     in1=gml[:, None, :].to_broadcast([128, 4, 256]), op=ALU.mult),
                             r=[("zs", par, 0), ("zs", par, 1), "gml"], w=[("zs", par, 0), ("zs", par, 1)])

                    def rB(tb, ci):
                        par = tb % 2
                        c = tb * 4 + ci
                        T.op("dve", lambda: nc.vector.scalar_tensor_tensor(out=stR[:, ci, 0:1], in0=bk(5)[:, 256:257], scalar=-1.0,
                                                                           in1=colf[:, c:c + 1], op0=ALU.mult, op1=ALU.max),
                             r=[("bank", 5), ("col", 1)], w=[("stR", ci, 0)])
                        T.op("dve", lambda: nc.vector.tensor_tensor(out=stR[:, ci, 1:2], in0=stR[:, ci, 0:1], in1=bk(5)[:, 256:257],
                                                                    op=ALU.max),
                             r=[("stR", ci, 0), ("bank", 5)], w=[("stR", ci, 1)])
                        T.op("dve", lambda: nc.vector.reciprocal(out=stR[:, ci, 2:3], in_=stR[:, ci, 1:2]),
                             r=[("stR", ci, 1)], w=[("stR", ci, 2)])
                        T.op("dve", lambda: nc.vector.scalar_tensor_tensor(out=hmt4[par][:, ci, :], in0=bk(5)[:, 0:256],
                                                                           scalar=stR[:, ci, 2:3], in1=sg4[par][:, ci, :],
                                                                           op0=ALU.mult, op1=ALU.mult),
                             r=[("bank", 5), ("stR", ci, 2), ("sg", par, ci // 2)], w=[("hmt", par, ci)])
                        T.op("act", lambda: nc.scalar.activation(out=junkB[:], in_=hmt4[par][:, ci, :], func=AF.Square,
                                                                 accum_out=ssq4[par][:, ci:ci + 1]),
                             r=[("hmt", par, ci)], w=["junkB", ("ssq", par, ci)])

                    def R_chunk(tb, ci):
                        par = tb % 2
                        c = tb * 4 + ci
                        cs = slice(c * 128, (c + 1) * 128)
                        T.op("dve", lambda: nc.vector.tensor_scalar(out=Cd[:], in0=Cst[:], scalar1=decb[:, c:c + 1], scalar2=None,
                                                                    op0=ALU.mult), r=["Cst", "decb"], w=["Cd"])
                        T.op("dve", lambda: nc.vector.tensor_scalar(out=Cdb[:], in0=Cst[:], scalar1=decb[:, c:c + 1], scalar2=None,
                                                                    op0=ALU.mult), r=["Cst", "decb"], w=["Cdb"])
                        T.op("pe", lambda: nc.tensor.matmul(bk(6)[:, 0:257], lhsT=ktok4[par][:, ci, :], rhs=vw4[par][:, ci, :],
                                                            start=True, stop=True),
                             r=[("ktok", par, ci), ("vw", par, ci), ("vw1", par)], w=[("bank", 6)])
                        T.op("dve", lambda: nc.vector.tensor_tensor(out=Cst[:], in0=bk(6)[:, 0:257], in1=Cd[:], op=ALU.add),
                             r=[("bank", 6), "Cd"], w=["Cst"])
                        if ci > 0:
                            rB(tb, ci - 1)
                        T.op("pe", lambda: nc.tensor.matmul(bk(5)[:, 0:257], lhsT=SmT4[par][:, ci, :], rhs=vw4[par][:, ci, :],
                                                            start=True, stop=False),
                             r=[("SmT", par, ci), ("vw", par, ci), ("vw1", par)], w=[("bank", 5)], inc=False)
                        T.op("pe", lambda: nc.tensor.matmul(bk(5)[:, 0:257], lhsT=qTm[:, cs], rhs=Cdb[:], start=False, stop=True),
                             r=[("qk", 0, tb), "Cdb"], w=[("bank", 5)])

                    def N_pre(tb):
                        par = tb % 2
                        T.op("act", lambda: nc.scalar.activation(out=rs4[par][:, 0:4], in_=ssq4[par][:, 0:4], func=AF.Ln,
                                                                 scale=1.0 / 256, bias=EPS),
                             r=[("ssq", par, ci) for ci in range(4)], w=[("rs", par, 0)])
                        T.op("act", lambda: nc.scalar.activation(out=rs4[par][:, 4:8], in_=rs4[par][:, 0:4], func=AF.Exp, scale=-0.5),
                             r=[("rs", par, 0)], w=[("rs", par, 1)])
                        for ci in range(4):
                            T.op("dve", lambda: nc.vector.scalar_tensor_tensor(out=ytok4[:, ci, :], in0=hmt4[par][:, ci, :],
                                                                               scalar=rs4[par][:, 4 + ci:5 + ci], in1=zs4[par][:, ci, :],
                                                                               op0=ALU.mult, op1=ALU.mult),
                                 r=[("hmt", par, ci), ("rs", par, 1), ("zs", par, ci // 2)], w=[("ytok", ci)])

                    def N_T(tb, hf):
                        par = tb % 2
                        ytb = yTb[par]
                        for ci in range(4):
                            T.op("pe", lambda: nc.tensor.transpose(bkb(7)[:, ci * 128:(ci + 1) * 128],
                                                                   ytok4[:, ci, hf * 128:(hf + 1) * 128], ident[:]),
                                 r=[("ytok", ci), "ident"], w=[("bank", 7)], inc=(ci == 3))
                        T.op("act", lambda: nc.scalar.copy(out=ytb[:, hf, :], in_=bkb(7)[:, 0:512]),
                             r=[("bank", 7)], w=[("yTb", par)])

                    def N_out(tb):
                        par = tb % 2
                        r0 = 1024 + h * 256
                        T.dma("pool", yT[r0:r0 + 256, tb * 512:(tb + 1) * 512].rearrange("(a p) t -> p a t", p=128), yTb[par][:],
                              r=[("yTb", par)], w=[("yT", 8 + h, tb)])

                    P_pre(0)
                    for ci in range(4):
                        P_chunk(0, ci)
                    P_post(0)
                    cast_wqk_next = None
                    for tb in range(NB):
                        if h + 1 < 4 and tb == min(2, NB - 1):
                            cast_wqk_next = load_wqk(h + 1, defer=True)
                        elif cast_wqk_next is not None:
                            cast_wqk_next()
                            cast_wqk_next = None
                        if tb >= 1:
                            N_pre(tb - 1)
                        if tb + 1 < NB:
                            P_pre(tb + 1)
                        for ci in range(4):
                            if tb + 1 < NB:
                                P_chunk(tb + 1, ci)
                            R_chunk(tb, ci)
                            if tb >= 1 and ci == 1:
                                N_T(tb - 1, 0)
                            if tb >= 1 and ci == 3:
                                N_T(tb - 1, 1)
                                N_out(tb - 1)
                        rB(tb, 3)
                        if tb + 1 < NB:
                            P_post(tb + 1)
                    if cast_wqk_next is not None:
                        cast_wqk_next()
                        cast_wqk_next = None
                    N_pre(NB - 1)
                    N_T(NB - 1, 0)
                    N_T(NB - 1, 1)
                    N_out(NB - 1)
                T.barrier()
        h1dst = out
        scCD = es.enter_context(ExitStack())
        w1 = sb("w1", [128, 8, ODD_IN], BF16, scCD)
        rs1 = sb("rs1", [128, NT], F32, scCD)
        with ExitStack() as scC:
            wo0 = sb("wo0", [128, 16, D], BF16, scC)
            gp0 = sb("gp0", [128, D], F32, scC)
            T.dma("sp", gp0[:], l0_post_g.partition_broadcast(128), w=["gp0"])
            for kg in range(2):
                for cg in range(2):
                    T.dma("sp", wo0[:, kg * 8:(kg + 1) * 8, cg * 512:(cg + 1) * 512],
                          wo0bv[:, kg * 8:(kg + 1) * 8, cg * 512:(cg + 1) * 512], w=[("wo0", kg, cg)])
            wo0_keys = [("wo0", a, b) for a in range(2) for b in range(2)]
            w1_loads = list(range(ODD_IN // 512))
            yblk = [sb("yblk%d" % i, [128, 16, 512], BF16, scC) for i in range(2)]
            xt = [sb("xtC%d" % i, [128, D], F32, scC) for i in range(2)]
            junkC = sb("junkC", [128, D], BF16, scC)
            stC = [sb("stC%d" % i, [128, 8], F32, scC) for i in range(2)]
            h1 = [sb("h1_%d" % i, [128, D], F32, scC) for i in range(2)]
            for i in range(NT):
                tb, b = i // 4, i % 2
                yb_ = yblk[tb % 2]
                st_ = stC[b]
                if i % 4 == 0:
                    for half in range(2):
                        T.dma("sp", yb_[:, half * 8:(half + 1) * 8, :],
                              yT[half * 1024:(half + 1) * 1024, tb * 512:(tb + 1) * 512].rearrange("(a p) t -> p a t", p=128),
                              r=[("yT", hh, tb) for hh in range(half * 8, half * 8 + (8 if half == 0 else 4))],
                              w=[("yblk", tb % 2, half)])
                T.dma("sp", xt[b][:], x[i * 128:(i + 1) * 128, :], w=[("xtC", b)])
                if w1_loads and (i % 2 == 1 or NT - i <= len(w1_loads)):
                    cgp = w1_loads.pop(0)
                    T.dma("sp", w1[:, :, cgp * 512:(cgp + 1) * 512], w1bv[:, :, cgp * 512:(cgp + 1) * 512], w=[("w1", cgp)])
                ts_ = slice((i % 4) * 128, (i % 4 + 1) * 128)
                pbs = (2 * b, 2 * b + 1)
                for cg in range(2):
                    for kc in range(16):
                        T.op("pe", lambda: nc.tensor.matmul(bk(pbs[cg])[:, :], lhsT=yb_[:, kc, ts_], rhs=wo0[:, kc, cg * 512:(cg + 1) * 512],
                                                            start=(kc == 0), stop=(kc == 15)),
                             r=[("yblk", tb % 2, 0), ("yblk", tb % 2, 1)] + wo0_keys, w=[("bank", pbs[cg])], inc=(kc == 15))
                for cg in range(2):
                    T.op("act", lambda: nc.scalar.activation(out=junkC[:, cg * 512:(cg + 1) * 512], in_=bk(pbs[cg])[:, :], func=AF.Square,
                                                             accum_out=st_[:, cg:cg + 1]),
                         r=[("bank", pbs[cg])], w=[("junkC", cg), ("stC", b, cg)])
                T.op("dve", lambda: nc.vector.tensor_tensor(out=st_[:, 2:3], in0=st_[:, 0:1], in1=st_[:, 1:2], op=ALU.add),
                     r=[("stC", b, 0), ("stC", b, 1)], w=[("stC", b, 2)])
                rstd_from_ssq(st_[:, 2:3], st_[:, 3:4], st_[:, 4:5], D, [("stC", b, 2)], ("stC", b, 3), ("stC", b, 4))
                for cg in range(2):
                    T.op("dve", lambda: nc.vector.scalar_tensor_tensor(out=h1[b][:, cg * 512:(cg + 1) * 512], in0=bk(pbs[cg])[:, :],
                                                                       scalar=st_[:, 4:5], in1=gp0[:, cg * 512:(cg + 1) * 512],
                                                                       op0=ALU.mult, op1=ALU.mult),
                         r=[("bank", pbs[cg]), ("stC", b, 4), "gp0"], w=[("h1", b, cg)])
                T.op("dve", lambda: nc.vector.tensor_tensor(out=h1[b][:], in0=h1[b][:], in1=xt[b][:], op=ALU.add),
                     r=[("h1", b, 0), ("h1", b, 1), ("xtC", b)], w=[("h1", b, 0), ("h1", b, 1)])
                T.op("act", lambda: nc.scalar.activation(out=junkC[:], in_=h1[b][:], func=AF.Square, accum_out=st_[:, 5:6]),
                     r=[("h1", b, 0), ("h1", b, 1)], w=[("junkC", 0), ("junkC", 1), ("stC", b, 5)])
                rstd_from_ssq(st_[:, 5:6], st_[:, 6:7], rs1[:, i:i + 1], D, [("stC", b, 5)], ("stC", b, 6), ("rs1", i))
                T.dma("pool", h1dst[i * 128:(i + 1) * 128, :], h1[b][:], r=[("h1", b, 0), ("h1", b, 1)], w=[("h1d", i)])
                if dbg:
                    T.dma("pool", dbg_out["dbg_h1"][i * 128:(i + 1) * 128, :], h1[b][:], r=[("h1", b, 0), ("h1", b, 1)], w=[("dbgh1", i)])
            while w1_loads:
                cgp = w1_loads.pop(0)
                T.dma("sp", w1[:, :, cgp * 512:(cgp + 1) * 512], w1bv[:, :, cgp * 512:(cgp + 1) * 512], w=[("w1", cgp)])
            if dbg:
                T.barrier()
                T.dma("sp", dbg_out["dbg_yT"][:, :], yT[:, :])
            T.barrier()
        with ExitStack() as scD:
            wo1 = sb("wo1", [128, 16, D], BF16, scD)
            for kg in range(2):
                for cg in range(2):
                    T.dma("sp", wo1[:, kg * 8:(kg + 1) * 8, cg * 512:(cg + 1) * 512],
                          wo1bv[:, kg * 8:(kg + 1) * 8, cg * 512:(cg + 1) * 512], w=[("wo1", kg, cg)])
            wo1_keys = [("wo1", a, b) for a in range(2) for b in range(2)]
            gp1 = sb("gp1", [128, D], F32, scD)
            gq1 = sb("gq1", [128, D], F32, scD)
            gsg = sb("gsg", [128, 2048], F32, scD)
            bsp = sb("bsp", [128, 8], F32, scD)
            wmT = sb("wmT", [128, 8, 128], BF16, scD)
            T.dma("sp", gp1[:], l1_pre_g.partition_broadcast(128), w=["gp1"])
            T.dma("sp", gq1[:], l1_post_g.partition_broadcast(128), w=["gq1"])
            T.dma("sp", gsg[:], l1_sg_norm_g.partition_broadcast(128), w=["gsg"])
            T.dma("sp", bsp[:], l1_b_spatial.rearrange("g t -> t g"), w=["bsp"], allow_slow_non_contiguous=True)
            with ExitStack() as scS:
                wsp_f = sb("wsp_f", [128, 8, 128], F32, scS)
                wsp_b = sb("wsp_b", [128, 8, 128], BF16, scS)
                T.dma("sp", wsp_f[:], l1_w_spatial.rearrange("g t s -> t g s"), w=["wsp_f"])
                T.op("dve", lambda: nc.vector.tensor_copy(out=wsp_b[:], in_=wsp_f[:]), r=["wsp_f"], w=["wsp_b"])
                for g in range(8):
                    T.op("pe", lambda: nc.tensor.transpose(bkb(7)[:, g * 128:(g + 1) * 128], wsp_b[:, g, :], ident[:]),
                         r=["wsp_b", "ident"], w=[("bank", 7)], inc=(g == 7))
                T.op("dve", lambda: nc.vector.tensor_tensor(out=wmT[:], in0=bkb(7).rearrange("p (g t) -> p g t", g=8),
                                                            in1=tri[:, None, :].to_broadcast([128, 8, 128]), op=ALU.mult),
                     r=[("bank", 7), "tri"], w=["wmT"])
                T.barrier()

            ht = [sb("ht%d" % i, [128, D], F32, scD) for i in range(3)]
            hn1 = sb("hn1", [128, D], BF16, scD)
            hn1T = [sb("hn1T%d" % i, [128, 8, 128], BF16, scD) for i in range(2)]
            junkD = sb("junkD", [128, D], BF16, scD)
            stD = [sb("stD%d" % i, [128, 12], F32, scD) for i in range(2)]
            gu = [sb("gu%d" % i, [128, 2048], BF16, scD) for i in range(2)]
            gv = [sb("gv%d" % i, [128, 2048], BF16, scD) for i in range(2)]
            sz = [sb("sz%d" % i, [128, 2048], BF16, scD) for i in range(2)]
            vn = sb("vn", [128, 2048], BF16, scD)
            y1 = sb("y1", [128, 2048], BF16, scD)
            y1T = sb("y1T", [128, 16, 128], BF16, scD)
            ymx = sb("ymx", [128, D], F32, scD)
            guk = lambda p: [("gu", p, k) for k in range(4)]
            gvk = lambda p: [("gv", p, k) for k in range(4)]
            szk = lambda p: [("sz", p, k) for k in range(4)]
            y1k = [("y1", g) for g in range(8)]

            def stA(i):
                p = i % 2
                T.dma("sp", ht[i % 3][:], h1dst[i * 128:(i + 1) * 128, :], r=[("h1d", i)], w=[("ht", i % 3)])
                T.op("dve", lambda: nc.vector.scalar_tensor_tensor(out=hn1[:], in0=ht[i % 3][:], scalar=rs1[:, i:i + 1], in1=gp1[:],
                                                                   op0=ALU.mult, op1=ALU.mult),
                     r=[("ht", i % 3), "gp1"], w=["hn1"])
                for kc in range(8):
                    T.op("pe", lambda: nc.tensor.transpose(bkb(7)[:, kc * 128:(kc + 1) * 128], hn1[:, kc * 128:(kc + 1) * 128], ident[:]),
                         r=["hn1", "ident"], w=[("bank", 7)], inc=(kc == 7))
                T.op("dve", lambda: nc.vector.tensor_copy(out=hn1T[p][:], in_=bkb(7).rearrange("p (k t) -> p k t", k=8)),
                     r=[("bank", 7)], w=[("hn1T", p)])

            def stB(i, groups):
                p = i % 2
                for cgp in groups:
                    pb = (0, 1, 3, 4)[cgp % 4]
                    for kc in range(8):
                        T.op("pe", lambda: nc.tensor.matmul(bk(pb)[:, :], lhsT=hn1T[p][:, kc, :], rhs=w1[:, kc, cgp * 512:(cgp + 1) * 512],
                                                            start=(kc == 0), stop=(kc == 7)),
                             r=[("hn1T", p)], w=[("bank", pb)], inc=(kc == 7))
                    sec, cc = cgp // 4, (cgp % 4) * 512
                    dst, fn, key = ((gu[p], AF.Gelu_apprx_tanh, "gu"), (gv[p], AF.Gelu_apprx_tanh, "gv"), (sz[p], AF.Silu, "sz"))[sec]
                    T.op("act", lambda: nc.scalar.activation(out=dst[:, cc:cc + 512], in_=bk(pb)[:, :], func=fn),
                         r=[("bank", pb)], w=[(key, p, cgp % 4)])

            def stC1a(i):
                p = i % 2
                T.op("act", lambda: nc.scalar.activation(out=vn[:], in_=gv[p][:], func=AF.Square, accum_out=stD[p][:, 3:4]),
                     r=gvk(p), w=["vn", ("stD", p, 3)])
                rstd_from_ssq(stD[p][:, 3:4], stD[p][:, 4:5], stD[p][:, 5:6], 2048, [("stD", p, 3)], ("stD", p, 4), ("stD", p, 5))

            def stC1b(i):
                p = i % 2
                T.op("dve", lambda: nc.vector.scalar_tensor_tensor(out=vn[:], in0=gv[p][:], scalar=stD[p][:, 5:6], in1=gsg[:],
                                                                   op0=ALU.mult, op1=ALU.mult),
                     r=gvk(p) + [("stD", p, 5), "gsg"], w=["vn"])
                T.op("pool", lambda: nc.gpsimd.tensor_tensor(out=gu[p][:], in0=gu[p][:], in1=sz[p][:], op=ALU.mult),
                     r=guk(p) + szk(p), w=guk(p))

            def stC2(i):
                p = i % 2
                for half in range(2):
                    for gg in range(4):
                        g = half * 4 + gg
                        pb = 5 + gg // 2
                        T.op("pe", lambda: nc.tensor.matmul(bk(pb)[:, (gg % 2) * 256:(gg % 2 + 1) * 256], lhsT=wmT[:, g, :],
                                                            rhs=vn[:, g * 256:(g + 1) * 256], start=True, stop=True),
                             r=["wmT", "vn"], w=[("bank", pb)], inc=(gg % 2 == 1))
                    for gg in range(4):
                        g = half * 4 + gg
                        pb = 5 + gg // 2
                        T.op("dve", lambda: nc.vector.scalar_tensor_tensor(out=y1[:, g * 256:(g + 1) * 256],
                                                                           in0=bk(pb)[:, (gg % 2) * 256:(gg % 2 + 1) * 256],
                                                                           scalar=bsp[:, g:g + 1], in1=gu[p][:, g * 256:(g + 1) * 256],
                                                                           op0=ALU.add, op1=ALU.mult),
                             r=[("bank", pb), "bsp"] + guk(p), w=[("y1", g)])

            def stC3(i):
                for half in range(2):
                    for kk in range(8):
                        kc = half * 8 + kk
                        T.op("pe", lambda: nc.tensor.transpose(bkb(2)[:, kk * 128:(kk + 1) * 128], y1[:, kc * 128:(kc + 1) * 128], ident[:]),
                             r=y1k + ["ident"], w=[("bank", 2)], inc=(kk == 7))
                    if half == 0:
                        T.op("dve", lambda: nc.vector.tensor_copy(out=y1T[:, 0:8, :], in_=bkb(2).rearrange("p (k t) -> p k t", k=8)),
                             r=[("bank", 2)], w=[("y1T", 0)])
                    else:
                        T.op("dve", lambda: nc.vector.tensor_copy(out=y1T[:, 8:16, :], in_=bkb(2).rearrange("p (k t) -> p k t", k=8)),
                             r=[("bank", 2)], w=[("y1T", 1)])

            def stD_(i):
                p = i % 2
                for cg in range(2):
                    pb = 5 + cg
                    for kc in range(16):
                        T.op("pe", lambda: nc.tensor.matmul(bk(pb)[:, :], lhsT=y1T[:, kc, :], rhs=wo1[:, kc, cg * 512:(cg + 1) * 512],
                                                            start=(kc == 0), stop=(kc == 15)),
                             r=[("y1T", 0), ("y1T", 1)], w=[("bank", pb)], inc=(kc == 15))
                for cg in range(2):
                    T.op("dve", lambda: nc.vector.tensor_copy(out=ymx[:, cg * 512:(cg + 1) * 512], in_=bk(5 + cg)[:, :]),
                         r=[("bank", 5 + cg)], w=[("ymx", cg)])
                T.op("act", lambda: nc.scalar.activation(out=junkD[:, 0:D], in_=ymx[:], func=AF.Square, accum_out=stD[p][:, 6:7]),
                     r=[("ymx", 0), ("ymx", 1)], w=["junkD", ("stD", p, 6)])
                rstd_from_ssq(stD[p][:, 6:7], stD[p][:, 7:8], stD[p][:, 8:9], D, [("stD", p, 6)], ("stD", p, 7), ("stD", p, 8))
                T.op("dve", lambda: nc.vector.scalar_tensor_tensor(out=ymx[:], in0=ymx[:], scalar=stD[p][:, 8:9], in1=gq1[:],
                                                                   op0=ALU.mult, op1=ALU.mult),
                     r=[("ymx", 0), ("ymx", 1), ("stD", p, 8), "gq1"], w=[("ymx", 0), ("ymx", 1)])
                T.op("dve", lambda: nc.vector.tensor_tensor(out=ht[i % 3][:], in0=ymx[:], in1=ht[i % 3][:], op=ALU.add),
                     r=[("ymx", 0), ("ymx", 1), ("ht", i % 3)], w=[("ht", i % 3)])
                T.dma("pool", out[i * 128:(i + 1) * 128, :], ht[i % 3][:], r=[("ht", i % 3), ("h1d", i)], w=[("outd", i)])

            stA(0)
            for i in range(NT):
                stB(i, range(0, 3))
                if i >= 1:
                    stC1b(i - 1)
                stB(i, range(3, 5))
                if i + 1 < NT:
                    stA(i + 1)
                if i >= 1:
                    stC2(i - 1)
                stB(i, range(5, 8))
                if i >= 1:
                    stC3(i - 1)
                stB(i, range(8, 10))
                if i >= 1:
                    stD_(i - 1)
                stC1a(i)
                stB(i, range(10, 12))
            stC1b(NT - 1)
            stC2(NT - 1)
            stC3(NT - 1)
            stD_(NT - 1)
            T.finish("sp")
            T.barrier()
    return nc


_NC_CACHE = {}


def _consts():
    ident = np.eye(128, dtype=np.float32).astype(ml_dtypes.bfloat16)
    k = np.arange(128)
    tri = (k[:, None] <= k[None, :]).astype(np.float32).astype(ml_dtypes.bfloat16)
    return {"c_ident": ident, "c_tri": tri}


def kernel(**inputs):
    x = np.ascontiguousarray(inputs["x"], dtype=np.float32)
    B, S, _ = x.shape
    if S not in _NC_CACHE:
        _NC_CACHE[S] = build(S)
    nc = _NC_CACHE[S]
    shared = {k: np.ascontiguousarray(v, dtype=np.float32) for k, v in inputs.items() if k != "x"}
    shared.update(_consts())
    in_maps = []
    for b in range(B):
        m = dict(shared)
        m["x"] = x[b]
        in_maps.append(m)
    res = run_bass_kernel_spmd(nc, in_maps, core_ids=list(range(B)))
    return np.stack([r["out"] for r in res.results], axis=0).astype(np.float32)
```

```python
import numpy as np
import ml_dtypes
from contextlib import ExitStack
import concourse.bass as bass
import concourse.mybir as mybir
from concourse.bass_utils import run_bass_kernel_spmd

F32, BF16 = mybir.dt.float32, mybir.dt.bfloat16
AF = mybir.ActivationFunctionType
ALU = mybir.AluOpType

D = 1024
EVEN_IN = 8200
ODD_IN = 6144
EPS = 1e-6
NDS = 8


class Trk:
    def __init__(self, nc, es):
        self.nc = nc
        self.E = {"pe": nc.tensor, "act": nc.scalar, "dve": nc.vector, "pool": nc.gpsimd, "sp": nc.sync}
        self.csem, self.ccnt = {}, {}
        for e in ("pe", "act", "dve", "pool"):
            self.csem[e] = es.enter_context(nc.semaphore("c_" + e))
            self.ccnt[e] = 0
        self.dq = {}
        for q in ("sp", "pool", "act"):
            self.dq[q] = dict(sems=[es.enter_context(nc.semaphore("d_%s%d" % (q, i))) for i in range(NDS)],
                              cnt=[0] * NDS, nxt=0)
        self.waited = {e: {} for e in self.E}
        self.res = {}
        self.pend_r, self.pend_w = [], []

    def _deps(self, r, w):
        deps = []
        for k in r:
            ent = self.res.get(k)
            if ent and ent[0]:
                deps.append(ent[0])
        for k in w:
            ent = self.res.get(k)
            if ent:
                if ent[0]:
                    deps.append(ent[0])
                deps.extend(ent[1])
        return deps

    def _wait(self, e, deps):
        best = {}
        for (key, sem, val) in deps:
            if key not in best or best[key][1] < val:
                best[key] = (sem, val)
        for key, (sem, val) in best.items():
            if e == "pe" and key == "c_pe":
                continue
            if self.waited[e].get(key, 0) >= val:
                continue
            self.E[e].wait_ge(sem, val)
            self.waited[e][key] = val

    def _reg(self, ev, r, w):
        for k in r:
            ent = self.res.setdefault(k, [None, []])
            ent[1].append(ev)
        for k in w:
            self.res[k] = [ev, []]

    def op(self, e, emit, r=(), w=(), inc=True):
        r, w = list(r), list(w)
        self._wait(e, self._deps(r, w))
        ins = emit()
        if e == "pe" and not inc:
            self.pend_r += r
            self.pend_w += w
            return ins
        self.ccnt[e] += 1
        ins.then_inc(self.csem[e], 1)
        ev = ("c_" + e, self.csem[e], self.ccnt[e])
        if e == "pe":
            r = r + self.pend_r
            w = w + self.pend_w
            self.pend_r, self.pend_w = [], []
        self._reg(ev, r, w)
        return ins

    def dma(self, q, out, in_, r=(), w=(), **kw):
        r, w = list(r), list(w)
        Q = self.dq[q]
        k = Q["nxt"]
        Q["nxt"] = (k + 1) % NDS
        key = "d_%s%d" % (q, k)
        deps = self._deps(r, w)
        if Q["cnt"][k] > 0:
            deps.append((key, Q["sems"][k], Q["cnt"][k]))
        self._wait(q, deps)
        ins = self.E[q].dma_start(out=out, in_=in_, **kw)
        Q["cnt"][k] += 16
        ins.then_inc(Q["sems"][k], 16)
        self._reg((key, Q["sems"][k], Q["cnt"][k]), r, w)
        return ins

    def barrier(self):
        assert not self.pend_r and not self.pend_w
        evs = [("c_" + e, self.csem[e], self.ccnt[e]) for e in self.csem if self.ccnt[e] > 0]
        for q, Q in self.dq.items():
            for k in range(NDS):
                if Q["cnt"][k] > 0:
                    evs.append(("d_%s%d" % (q, k), Q["sems"][k], Q["cnt"][k]))
        for e in self.E:
            self._wait(e, evs)
        self.res = {}

    def finish(self, e="sp"):
        evs = []
        for q, Q in self.dq.items():
            for k in range(NDS):
                if Q["cnt"][k] > 0:
                    evs.append(("d_%s%d" % (q, k), Q["sems"][k], Q["cnt"][k]))
        self._wait(e, evs)


def build(S=4096, dbg=False):
    assert S % 512 == 0
    NT = S // 128
    NB = S // 512
    nc = bass.Bass("TRN2", target_bir_lowering=False)

    def din(name, shape, dt=F32):
        return nc.dram_tensor(name, list(shape), dt, kind="ExternalInput").ap()

    x = din("x", [S, D])
    l0_pre_g = din("l0_pre_g", [D]); l0_w_in = din("l0_w_in", [D, EVEN_IN])
    l0_b_igate = din("l0_b_igate", [4]); l0_b_fgate = din("l0_b_fgate", [4])
    l0_conv_w = din("l0_conv_w", [4, 1024]); l0_conv_b = din("l0_conv_b", [1024])
    lq1 = din("l0_lambda_q1", [64]); lk1 = din("l0_lambda_k1", [64])
    lq2 = din("l0_lambda_q2", [64]); lk2 = din("l0_lambda_k2", [64])
    l0_da_head_g = din("l0_da_head_g", [128]); l0_ml_head_g = din("l0_ml_head_g", [256])
    l0_w_out = din("l0_w_out", [2048, D]); l0_post_g = din("l0_post_g", [D])
    l1_pre_g = din("l1_pre_g", [D]); l1_w_in = din("l1_w_in", [D, ODD_IN])
    l1_sg_norm_g = din("l1_sg_norm_g", [2048]); l1_w_spatial = din("l1_w_spatial", [8, 128, 128])
    l1_b_spatial = din("l1_b_spatial", [8, 128]); l1_w_out = din("l1_w_out", [2048, D])
    l1_post_g = din("l1_post_g", [D])
    c_ident = din("c_ident", [128, 128], BF16)
    c_tri = din("c_tri", [128, 128], BF16)
    out = nc.dram_tensor("out", [S, D], F32, kind="ExternalOutput").ap()
    yT = nc.dram_tensor("yT_scr", [2048, S], BF16, kind="Internal").ap()
    rows_scr = nc.dram_tensor("rows_scr", [2, 4, S], F32, kind="Internal").ap()
    wo0_bf = nc.dram_tensor("wo0_bf", [2048, D], BF16, kind="Internal").ap()
    w1_bf = nc.dram_tensor("w1_bf", [D, ODD_IN], BF16, kind="Internal").ap()
    wo1_bf = nc.dram_tensor("wo1_bf", [2048, D], BF16, kind="Internal").ap()
    dbg_out = {}
    if dbg:
        dbg_out["dbg_h1"] = nc.dram_tensor("dbg_h1", [S, D], F32, kind="ExternalOutput").ap()
        dbg_out["dbg_yT"] = nc.dram_tensor("dbg_yT", [2048, S], BF16, kind="ExternalOutput").ap()
        dbg_out["dbg_rows"] = nc.dram_tensor("dbg_rows", [3, 4, S], F32, kind="ExternalOutput").ap()
        dbg_out["dbg_cols"] = nc.dram_tensor("dbg_cols", [4, 3, 128, S // 128], F32, kind="ExternalOutput").ap()
        dbg_out["dbg_r32"] = nc.dram_tensor("dbg_r32", [4, 2, S], F32, kind="ExternalOutput").ap()

    w0v = l0_w_in.rearrange("(kc p) e -> p kc e", p=128)
    w1v = l1_w_in.rearrange("(kc p) e -> p kc e", p=128)
    wo0v = l0_w_out.rearrange("(kc p) e -> p kc e", p=128)
    wo1v = l1_w_out.rearrange("(kc p) e -> p kc e", p=128)
    w1bv = w1_bf.rearrange("(kc p) e -> p kc e", p=128)
    wo0bv = wo0_bf.rearrange("(kc p) e -> p kc e", p=128)
    wo1bv = wo1_bf.rearrange("(kc p) e -> p kc e", p=128)

    with ExitStack() as es:
        E = es.enter_context
        T = Trk(nc, es)

        def sb(name, shape, dt=F32, scope=None):
            return (scope or es).enter_context(nc.sbuf_tensor(name, list(shape), dt))

        S2 = [E(nc.psum_tensor("S2_%d" % i, [128, 1024], F32)) for i in range(2)]
        banks = [None] * 4 + [E(nc.psum_tensor("bank%d" % i, [128, 512], F32)) for i in range(4, 8)]

        def bk(i):
            if i < 4:
                return S2[i // 2][:, (i % 2) * 512:(i % 2 + 1) * 512]
            return banks[i]

        def bkb(i):
            if i < 4:
                return S2[i // 2][:].bitcast(BF16)[:, (i % 2) * 1024:(i % 2 + 1) * 1024]
            return banks[i][:].bitcast(BF16)

        ident = sb("ident", [128, 128], BF16)
        tri = sb("tri", [128, 128], BF16)
        ones_bf = sb("ones_bf", [128, 128], BF16)
        inv128_bf = sb("inv128_bf", [128, 128], BF16)
        ones_f = sb("ones_f", [128, 128], F32)
        T.dma("sp", ident[:], c_ident[:, :], w=["ident"])
        T.dma("sp", tri[:], c_tri[:, :], w=["tri"])
        T.op("pool", lambda: nc.gpsimd.memset(ones_bf[:], 1.0), w=["ones_bf"])
        T.op("pool", lambda: nc.gpsimd.memset(inv128_bf[:], 1.0 / 128), w=["inv128_bf"])
        T.op("pool", lambda: nc.gpsimd.memset(ones_f[:], 1.0), w=["ones_f"])

        NSTv = [2]
        wst = []
        wst_i = [0]
        wst_gen = [0]

        def alloc_wst(scope):
            wst_gen[0] += 1
            wst[:] = [sb("wst%d_%d" % (wst_gen[0], i), [128, 8, 512], F32, scope) for i in range(NSTv[0])]

        def load_w(dst_fn, srcs, dst_key, defer=False):
            b = wst_i[0] % len(wst)
            wst_i[0] += 1
            tot = 0
            kcn = srcs[0][0].shape[1]
            for (src, off) in srcs:
                n = src.shape[2]
                T.dma("sp", wst[b][:, 0:kcn, off:off + n], src, w=[("wst", b, off)], r=[])
                tot = max(tot, off + n)
            def cast():
                T.op("dve", lambda: nc.vector.tensor_copy(out=dst_fn(), in_=wst[b][:, 0:kcn, 0:tot]),
                     r=[("wst", b, off) for (_, off) in srcs], w=[dst_key])
            if defer:
                return cast
            cast()

        def rstd_from_ssq(ssq_ap, tmp_ap, out_ap, n, keys_r, key_tmp, key_out):
            T.op("act", lambda: nc.scalar.activation(out=tmp_ap, in_=ssq_ap, func=AF.Ln, scale=1.0 / n, bias=EPS),
                 r=keys_r, w=[key_tmp])
            T.op("act", lambda: nc.scalar.activation(out=out_ap, in_=tmp_ap, func=AF.Exp, scale=-0.5),
                 r=[key_tmp], w=[key_out])

        with ExitStack() as sc0:
            hnT = sb("hnT", [128, 8, S], BF16, sc0)
            with ExitStack() as scA:
                NSTv[0] = 2
                alloc_wst(scA)
                QT = sb("QT", [128, S], BF16, scA)
                KT = sb("KT", [128, S], BF16, scA)
                V = sb("V", [128, NT, 128], BF16, scA)
                zT2 = [sb("zT%d" % i, [128, S], BF16, scA) for i in range(2)]
                Ocp = sb("Ocp", [128, 2, 512], F32, scA)
                Lcp = sb("Lcp", [128, 2, 512], F32, scA)
                pend_tail = [None]
                wbf2 = [sb("wbfA%d" % i, [128, 8, 512], BF16, scA) for i in range(2)]
                P2 = [sb("P2_%d" % b, [128, 2, 512], BF16, scA) for b in range(2)]
                r1 = sb("r1", [128, 512], F32, scA)
                t1 = sb("t1", [128, 512], F32, scA)
                t2 = sb("t2", [128, 512], F32, scA)
                oT = sb("oT", [128, 512], F32, scA)
                sq = sb("sqA", [128, 512], BF16, scA)
                lnv = sb("lnvA", [128, 512], F32, scA)
                rsd = sb("rsdA", [128, 512], F32, scA)
                yb = [sb("ybA%d" % i, [128, 512], BF16, scA) for i in range(2)]
                lam4 = sb("lam4", [128, 4, 64], F32, scA)
                lamj = sb("lamj", [128, 64], F32, scA)
                lams = sb("lams", [128, 8], F32, scA)
                gda = sb("gda", [128, 2, 16], F32, scA)
                for i, a in enumerate((lq1, lk1, lq2, lk2)):
                    T.dma("sp", lam4[:, i, :], a.partition_broadcast(128), w=[("lam4", i)])
                for i in range(2):
                    T.op("dve", lambda: nc.vector.tensor_tensor(out=lamj[:], in0=lam4[:, 2 * i, :], in1=lam4[:, 2 * i + 1, :],
                                                                op=ALU.mult),
                         r=[("lam4", 2 * i), ("lam4", 2 * i + 1)], w=["lamj"])
                    T.op("dve", lambda: nc.vector.reduce_sum(out=lams[:, i:i + 1], in_=lamj[:], axis=mybir.AxisListType.X),
                         r=["lamj"], w=[("lams", i)])
                    T.op("act", lambda: nc.scalar.activation(out=lams[:, 2 + i:3 + i], in_=lams[:, i:i + 1], func=AF.Exp),
                         r=[("lams", i)], w=[("lams", 2 + i)])
                T.op("dve", lambda: nc.vector.tensor_tensor(out=lams[:, 4:5], in0=lams[:, 3:4], in1=lams[:, 2:3], op=ALU.subtract),
                     r=[("lams", 2), ("lams", 3)], w=[("lams", 4)])
                T.op("dve", lambda: nc.vector.tensor_scalar(out=lams[:, 5:6], in0=lams[:, 4:5], scalar1=-0.2, scalar2=None,
                                                            op0=ALU.add),
                     r=[("lams", 4)], w=["neglam"])
                neglam = lams[:, 5:6]
                T.dma("sp", gda[:, 0, 0:1], l0_da_head_g.rearrange("(p o) -> p o", o=1), w=["gda0"])
                T.op("dve", lambda: nc.vector.tensor_scalar(out=gda[:, 1, 0:1], in0=gda[:, 0, 0:1], scalar1=0.8, scalar2=None,
                                                            op0=ALU.mult), r=["gda0"], w=["gda"])

                def load_head(hh):
                    srcs = [(w0v[:, :, off + hh * 128: off + (hh + 1) * 128], i * 128)
                            for i, off in enumerate((0, 1024, 2048, 3072))]
                    load_w(lambda: wbf2[hh % 2][:, :, :], srcs, ("wbfA", hh % 2))

                pcnt = [0]

                def proj_block(h, tb, wbf, wkey, zT):
                    for (dst, c0, kind) in ((QT, 0, "q"), (KT, 128, "k"), (zT, 384, "z")):
                        pb = pcnt[0] % 4
                        pcnt[0] += 1
                        for kc in range(8):
                            T.op("pe", lambda: nc.tensor.matmul(bk(pb)[:, :], lhsT=wbf[:, kc, c0:c0 + 128],
                                                                rhs=hnT[:, kc, tb * 512:(tb + 1) * 512],
                                                                start=(kc == 0), stop=(kc == 7)),
                                 r=[wkey, ("hnT", tb)], w=[("bank", pb)], inc=(kc == 7))
                        d = dst[:, tb * 512:(tb + 1) * 512]
                        if kind == "z":
                            T.op("act", lambda: nc.scalar.activation(out=d, in_=bk(pb)[:, :], func=AF.Silu),
                                 r=[("bank", pb)], w=[("z", h % 2, tb)])
                        else:
                            T.op("dve", lambda: nc.vector.tensor_copy(out=d, in_=bk(pb)[:, :]), r=[("bank", pb)], w=[(kind, tb)])
                    tg = tb
                    pb = pcnt[0] % 4
                    pcnt[0] += 1
                    for ti in range(4):
                        tt = tg * 4 + ti
                        for kc in range(8):
                            T.op("pe", lambda: nc.tensor.matmul(bk(pb)[:, ti * 128:(ti + 1) * 128],
                                                                lhsT=hnT[:, kc, tt * 128:(tt + 1) * 128],
                                                                rhs=wbf[:, kc, 256:384], start=(kc == 0), stop=(kc == 7)),
                                 r=[wkey, ("hnT", tg)], w=[("bank", pb)], inc=(kc == 7 and ti == 3))
                    T.op("dve", lambda: nc.vector.tensor_copy(out=V[:, tg * 4:(tg + 1) * 4, :],
                                                              in_=bk(pb)[:, :].rearrange("p (a b) -> p a b", a=4)),
                         r=[("bank", pb)], w=[("v", tg)])

                load_head(0)
                with ExitStack() as sc1:
                    g0bc = sb("g0bc", [128, D], F32, sc1)
                    T.dma("sp", g0bc[:], l0_pre_g.partition_broadcast(128), w=["g0bc"])
                    xt = [sb("xt%d" % i, [128, D], F32, sc1) for i in range(3)]
                    xn = [sb("xn%d" % i, [128, D], BF16, sc1) for i in range(2)]
                    junk = [sb("junk%d" % i, [128, D], BF16, sc1) for i in range(2)]
                    st = [sb("st%d" % i, [128, 4], F32, sc1) for i in range(2)]
                    for i in range(NT):
                        b = i % 2
                        bx = i % 3
                        T.dma("sp" if i % 2 == 0 else "pool", xt[bx][:], x[i * 128:(i + 1) * 128, :], w=[("xt", bx)])
                        T.op("act", lambda: nc.scalar.activation(out=junk[b][:], in_=xt[bx][:], func=AF.Square,
                                                                 accum_out=st[b][:, 0:1]),
                             r=[("xt", bx)], w=[("junk", b), ("ssq", b)])
                        rstd_from_ssq(st[b][:, 0:1], st[b][:, 1:2], st[b][:, 2:3], D, [("ssq", b)], ("lnv", b), ("rstd", b))
                        T.op("dve", lambda: nc.vector.scalar_tensor_tensor(out=xn[b][:], in0=xt[bx][:], scalar=st[b][:, 2:3],
                                                                           in1=g0bc[:], op0=ALU.mult, op1=ALU.mult),
                             r=[("xt", bx), ("rstd", b), "g0bc"], w=[("xn", b)])
                        pb = 6 + b
                        for kc in range(8):
                            T.op("pe", lambda: nc.tensor.transpose(bkb(pb)[:, kc * 128:(kc + 1) * 128],
                                                                   xn[b][:, kc * 128:(kc + 1) * 128], ident[:]),
                                 r=[("xn", b), "ident"], w=[("bank", pb)], inc=(kc == 7))
                        src = bkb(pb).rearrange("p (k t) -> p k t", k=8)
                        dst = hnT[:, :, i * 128:(i + 1) * 128]
                        if i % 2 == 0:
                            T.op("dve", lambda: nc.vector.tensor_copy(out=dst, in_=src), r=[("bank", pb)], w=[("hnT", i // 4)])
                        else:
                            T.op("act", lambda: nc.scalar.copy(out=dst, in_=src), r=[("bank", pb)], w=[("hnT", i // 4)])
                        if i % 4 == 3 and i // 4 >= 1:
                            proj_block(0, i // 4 - 1, wbf2[0], ("wbfA", 0), zT2[0])
                    proj_block(0, NB - 1, wbf2[0], ("wbfA", 0), zT2[0])
                for h in range(8):
                    wbf = wbf2[h % 2]
                    wkey = ("wbfA", h % 2)
                    zT = zT2[h % 2]
                    if h > 0:
                        for tb in range(NB):
                            proj_block(h, tb, wbf, wkey, zT)
                    if h == 0:
                        cast_jobs = []
                        for (src, dst, nrow) in ((l0_w_out, wo0_bf, 2048), (l1_w_in, w1_bf, D), (l1_w_out, wo1_bf, 2048)):
                            for r0 in range(0, nrow, 256):
                                cast_jobs.append((dst[r0:r0 + 256, :], src[r0:r0 + 256, :]))
                    if h + 1 < 8:
                        load_head(h + 1)

                    for qb in range(NB):
                        nj = 4 * qb + 4
                        if h >= 1 and cast_jobs:
                            cj = cast_jobs.pop(0)
                            T.dma("pool", cj[0], cj[1])

                        def qk(j):
                            c0 = max(0, j - 4 * qb) * 128
                            sbuf_i = j % 2
                            for c in range(2):
                                pb = 2 * sbuf_i + c
                                T.op("pe", lambda: nc.tensor.matmul(bk(pb)[:, c0:512],
                                                                    lhsT=KT[c * 64:(c + 1) * 64, j * 128:(j + 1) * 128],
                                                                    rhs=QT[c * 64:(c + 1) * 64, qb * 512 + c0:(qb + 1) * 512],
                                                                    start=True, stop=True),
                                     r=[("k", j // 4), ("q", qb)], w=[("bank", pb)], inc=True)

                        qk(0)
                        for j in range(nj):
                            c0 = max(0, j - 4 * qb) * 128
                            si = j % 2
                            if j + 1 < nj:
                                qk(j + 1)
                            T.op("act", lambda: nc.scalar.activation(
                                out=P2[si][:, :, c0:512],
                                in_=S2[si][:, :].rearrange("p (c q) -> p c q", c=2)[:, :, c0:512],
                                func=AF.Exp, scale=0.125),
                                 r=[("bank", 2 * si), ("bank", 2 * si + 1)], w=[("P", si)])
                            if j >= 4 * qb:
                                T.op("pool", lambda: nc.gpsimd.tensor_tensor(out=P2[si][:, :, c0:c0 + 128],
                                                                             in0=P2[si][:, :, c0:c0 + 128],
                                                                             in1=tri[:, None, :].to_broadcast([128, 2, 128]),
                                                                             op=ALU.mult),
                                     r=[("P", si), "tri"], w=[("P", si)])
                            if pend_tail[0] is not None and j == min(8, nj - 1):
                                pend_tail[0](2 * si)
                                pend_tail[0] = None
                            for c in range(2):
                                T.op("pe", lambda: nc.tensor.matmul(bk(4 + c)[:, c0:512], lhsT=V[:, j, :],
                                                                    rhs=P2[si][:, c, c0:512], start=(j == 0), stop=(j == nj - 1)),
                                     r=[("P", si), ("v", j // 4)], w=[("bank", 4 + c)], inc=False)
                                T.op("pe", lambda: nc.tensor.matmul(bk(6 + c)[:, c0:512], lhsT=ones_bf[:],
                                                                    rhs=P2[si][:, c, c0:512], start=(j == 0), stop=(j == nj - 1)),
                                     r=[("P", si), "ones_bf"], w=[("bank", 6 + c)], inc=True)
                        for c in range(2):
                            T.op("dve", lambda: nc.vector.tensor_copy(out=Lcp[:, c, :], in_=bk(6 + c)[:, :]), r=[("bank", 6 + c)], w=[("Lcp", c)])
                            T.op("dve", lambda: nc.vector.tensor_copy(out=Ocp[:, c, :], in_=bk(4 + c)[:, :]), r=[("bank", 4 + c)], w=[("Ocp", c)])
                        T.op("dve", lambda: nc.vector.reciprocal(out=r1[:], in_=Lcp[:, 0, :]), r=[("Lcp", 0)], w=["r1"])
                        T.op("dve", lambda: nc.vector.tensor_tensor(out=t1[:], in0=Ocp[:, 0, :], in1=r1[:], op=ALU.mult),
                             r=[("Ocp", 0), "r1"], w=["t1"])
                        T.op("dve", lambda: nc.vector.reciprocal(out=r1[:], in_=Lcp[:, 1, :]), r=[("Lcp", 1)], w=["r1"])
                        T.op("dve", lambda: nc.vector.tensor_tensor(out=t2[:], in0=Ocp[:, 1, :], in1=r1[:], op=ALU.mult),
                             r=[("Ocp", 1), "r1"], w=["t2"])
                        T.op("dve", lambda: nc.vector.scalar_tensor_tensor(out=oT[:], in0=t2[:], scalar=neglam, in1=t1[:],
                                                                           op0=ALU.mult, op1=ALU.add),
                             r=["t1", "t2", "neglam"], w=["oT"])
                        T.op("dve", lambda: nc.vector.tensor_tensor(out=sq[:], in0=oT[:], in1=oT[:], op=ALU.mult),
                             r=["oT"], w=["sqA"])

                        def tail(pbank, h=h, qb=qb, zT=zT):
                            T.op("pe", lambda: nc.tensor.matmul(bk(pbank)[:, :], lhsT=inv128_bf[:], rhs=sq[:], start=True, stop=True),
                                 r=["sqA", "inv128_bf"], w=[("bank", pbank)])
                            T.op("act", lambda: nc.scalar.activation(out=lnv[:], in_=bk(pbank)[:, :], func=AF.Ln, bias=EPS),
                                 r=[("bank", pbank)], w=["lnvA"])
                            T.op("act", lambda: nc.scalar.activation(out=rsd[:], in_=lnv[:], func=AF.Exp, scale=-0.5),
                                 r=["lnvA"], w=["rsdA"])
                            T.op("dve", lambda: nc.vector.tensor_tensor(out=t1[:], in0=oT[:], in1=rsd[:], op=ALU.mult),
                                 r=["oT", "rsdA"], w=["t1"])
                            ybb = yb[qb % 2]
                            T.op("dve", lambda: nc.vector.scalar_tensor_tensor(out=ybb[:], in0=t1[:], scalar=gda[:, 1, 0:1],
                                                                               in1=zT[:, qb * 512:(qb + 1) * 512],
                                                                               op0=ALU.mult, op1=ALU.mult),
                                 r=["t1", "gda", ("z", h % 2, qb)], w=[("ybA", qb % 2)])
                            T.dma("pool", yT[h * 128:(h + 1) * 128, qb * 512:(qb + 1) * 512], ybb[:],
                                  r=[("ybA", qb % 2)], w=[("yT", h, qb)])

                        assert pend_tail[0] is None
                        pend_tail[0] = tail
                pend_tail[0](0)
                pend_tail[0] = None
                while cast_jobs:
                    cj = cast_jobs.pop(0)
                    T.dma("pool", cj[0], cj[1])
                T.barrier()

            with ExitStack() as scB:
                NSTv[0] = 1
                alloc_wst(scB)
                rowA_t = sb("rowA", [64, S], F32, scB)
                rowG_t = sb("rowG", [64, S], F32, scB)
                rowA = rowA_t[0:4, :]
                rowG = rowG_t[0:4, :]
                rows1 = (rowA_t, rowG_t)
                rowM = sb("rowM", [4, S], F32, scB)
                rowc = sb("rowc", [4, 4, NT], F32, scB)
                rowc1 = sb("rowc1", [64, 2, NT], F32, scB)
                Lm2 = sb("Lm2", [NT + 1, 2, 128], F32, scB)
                Rm = sb("Rm", [NT + 1, NT], F32, scB)
                gb = sb("gb", [4, 3, 16], F32, scB)
                wif = sb("wif", [128, 8, 8], BF16, scB)
                colw = sb("colw", [128, NT], F32, scB)
                colf = sb("colf", [128, NT], F32, scB)
                decb = sb("decb", [128, NT], F32, scB)
                qTm = sb("qTm", [128, S], BF16, scB)
                kTm = sb("kTm", [128, S], BF16, scB)
                wqk2 = [sb("wqk%d" % i, [128, 8, 256], BF16, scB) for i in range(2)]
                wvoz = sb("wvoz", [128, 8, 768], BF16, scB)
                pre = [sb("pre%d" % i, [128, 515], F32, scB) for i in range(2)]
                acc2 = [sb("acc%d" % i, [128, 512], F32, scB) for i in range(2)]
                cw_all = sb("cw", [128, 8, 16], F32, scB)
                cb_all = sb("cb", [128, 8, 16], F32, scB)
                for hh in range(4):
                    for qi in range(2):
                        ch0 = qi * 512 + hh * 128
                        T.dma("pool", cw_all[:, hh * 2 + qi, 0:4], l0_conv_w[:, ch0:ch0 + 128].rearrange("j c -> c j"),
                              w=[("cw", hh, qi)], allow_slow_non_contiguous=True)
                        T.dma("pool", cb_all[:, hh * 2 + qi, 0:1], l0_conv_b[ch0:ch0 + 128].rearrange("(p o) -> p o", o=1),
                              w=[("cb", hh, qi)])
                gml = sb("gml", [128, 256], F32, scB)
                Cst = sb("Cst", [128, 257], F32, scB)
                Cd = sb("Cd", [128, 257], F32, scB)
                Cdb = sb("Cdb", [128, 257], BF16, scB)
                SmT4 = [sb("SmT4_%d" % i, [128, 4, 128], BF16, scB) for i in range(2)]
                ktok4 = [sb("ktok4_%d" % i, [128, 4, 128], BF16, scB) for i in range(2)]
                vw4 = [sb("vw4_%d" % i, [128, 4, 257], BF16, scB) for i in range(2)]
                sg4 = [sb("sg4_%d" % i, [128, 4, 256], BF16, scB) for i in range(2)]
                zs4 = [sb("zs4_%d" % i, [128, 4, 256], BF16, scB) for i in range(2)]
                hmt4 = [sb("hmt4_%d" % i, [128, 4, 256], BF16, scB) for i in range(2)]
                ssq4 = [sb("ssq4_%d" % i, [128, 4], F32, scB) for i in range(2)]
                rs4 = [sb("rs4_%d" % i, [128, 8], F32, scB) for i in range(2)]
                stR = sb("stR", [128, 4, 4], F32, scB)
                junkB = sb("junkB", [128, 256], BF16, scB)
                ytok4 = sb("ytok4", [128, 4, 256], BF16, scB)
                yTb = [sb("yTb%d" % i, [128, 2, 512], BF16, scB) for i in range(2)]

                T.dma("sp", gml[:], l0_ml_head_g.partition_broadcast(128), w=["gml"])
                T.dma("sp", gb[:, 0, 0:1], l0_b_igate.rearrange("(p o) -> p o", o=1), w=["gb0"])
                T.dma("sp", gb[:, 2, 0:1], l0_b_fgate.rearrange("(p o) -> p o", o=1), w=["gb2"])
                T.op("dve", lambda: nc.vector.tensor_scalar(out=gb[:, 1, 0:1], in0=gb[:, 2, 0:1], scalar1=-1.0, scalar2=None,
                                                            op0=ALU.mult), r=["gb2"], w=["gb1"])
                load_w(lambda: wif[:, :, :], [(w0v[:, :, 6144:6152], 0)], "wif")
                for gi, (row, key) in enumerate(((rowA, "rowA"), (rowG, "rowG"))):
                    for tb in range(NB):
                        pb = tb % 2
                        for kc in range(8):
                            T.op("pe", lambda: nc.tensor.matmul(bk(pb)[0:4, :], lhsT=wif[:, kc, gi * 4:gi * 4 + 4],
                                                                rhs=hnT[:, kc, tb * 512:(tb + 1) * 512],
                                                                start=(kc == 0), stop=(kc == 7)),
                                 r=["wif", ("hnT", tb)], w=[("bank", pb)], inc=(kc == 7))
                        T.op("dve", lambda: nc.vector.tensor_copy(out=row[:, tb * 512:(tb + 1) * 512], in_=bk(pb)[0:4, :]),
                             r=[("bank", pb)], w=[key])
                T.op("act", lambda: nc.scalar.activation(out=rowG[:], in_=rowG[:], func=AF.Exp, scale=-1.0, bias=gb[:, 1, 0:1]),
                     r=["rowG", "gb1"], w=["rowG"])
                T.op("act", lambda: nc.scalar.activation(out=rowG[:], in_=rowG[:], func=AF.Ln, bias=1.0),
                     r=["rowG"], w=["rowG"])
                T.op("dve", lambda: nc.vector.tensor_tensor_scan(out=rowG[:], data0=ones_f[0:4, 0:1].to_broadcast([4, S]),
                                                                 data1=rowG[:], initial=0.0, op0=ALU.mult, op1=ALU.add),
                     r=["rowG", "ones_f"], w=["rowG"])
                T.op("dve", lambda: nc.vector.scalar_tensor_tensor(out=rowA[:], in0=rowA[:], scalar=gb[:, 0, 0:1], in1=rowG[:],
                                                                   op0=ALU.add, op1=ALU.add),
                     r=["rowA", "rowG", "gb0"], w=["rowA"])
                T.op("dve", lambda: nc.vector.tensor_tensor_scan(out=rowM[:], data0=rowA[:], data1=rowA[:], initial=0.0,
                                                                 op0=ALU.max, op1=ALU.max),
                     r=["rowA"], w=["rowM"])
                mend = rowM[:].rearrange("p (c t) -> p c t", t=128)[:, :, 127]
                T.op("dve", lambda: nc.vector.tensor_scalar(out=rowc[:, 0, :], in0=mend, scalar1=-1.0, scalar2=None, op0=ALU.mult),
                     r=["rowM"], w=[("rowc", 0)])
                T.op("dve", lambda: nc.vector.memset(rowc[:, 1, 0:1], 0.0), w=[("rowc", 1, 0)])
                if NT > 1:
                    T.op("dve", lambda: nc.vector.tensor_copy(out=rowc[:, 1, 1:NT], in_=mend[:, 0:NT - 1]),
                         r=["rowM"], w=[("rowc", 1, 1)])
                T.op("dve", lambda: nc.vector.tensor_tensor(out=rowc[:, 2, :], in0=rowc[:, 1, :], in1=rowc[:, 0, :], op=ALU.add),
                     r=[("rowc", 0), ("rowc", 1, 0), ("rowc", 1, 1)], w=[("rowc", 2)])

                if dbg:
                    T.dma("sp", dbg_out["dbg_rows"][0], rowA, r=["rowA"])
                    T.dma("sp", dbg_out["dbg_rows"][1], rowG, r=["rowG"])
                    T.dma("sp", dbg_out["dbg_rows"][2], rowM[:], r=["rowM"])
                T.dma("sp", rows_scr[0], rowA, r=["rowA"], w=[("rows_scr", 0)])
                T.dma("sp", rows_scr[1], rowG, r=["rowG"], w=[("rows_scr", 1)])
                T.op("dve", lambda: nc.vector.memset(Lm2[:], 1.0), w=[("Lm2", 0), ("Lm2", 1)])
                T.op("dve", lambda: nc.vector.tensor_copy(out=Rm[0:NT, :], in_=ident[0:NT, 0:NT]), r=["ident"], w=["Rm_id"])
                for h in range(4):
                    for which in range(2):
                        T.dma("sp", Lm2[0:NT, which, :], rows_scr[which, h, :].rearrange("(c t) -> c t", t=128),
                              r=[("rows_scr", which)], w=[("Lm2", which)])
                    T.dma("sp", Rm[NT:NT + 1, :], rowc[h:h + 1, 0, :], r=[("rowc", 0)], w=["Rm_m"])
                    T.dma("sp", rowc1[32:33, 1, :], rowc[h:h + 1, 2, :], r=[("rowc", 2)], w=[("rowc1", 1)])
                    for which, (bnk, dstc, bias) in enumerate(((0, colw, 0.0), (1, colf, 0.5 * float(np.log(128.0))))):
                        T.op("pe", lambda: nc.tensor.matmul(bk(bnk)[:, 0:NT], lhsT=Lm2[:, which, :], rhs=Rm[:, :], start=True, stop=True),
                             r=[("Lm2", which), "Rm_id", "Rm_m"], w=[("bank", bnk)])
                        T.op("act", lambda: nc.scalar.activation(out=dstc[:], in_=bk(bnk)[:, 0:NT], func=AF.Exp, bias=bias),
                             r=[("bank", bnk)], w=[("col", which)])
                    T.op("pe", lambda: nc.tensor.matmul(bk(2)[:, 0:NT], lhsT=ones_f[32:33, 0:128], rhs=rowc1[32:33, 1, :],
                                                        start=True, stop=True),
                         r=[("rowc1", 1), "ones_f"], w=[("bank", 2)])
                    T.op("act", lambda: nc.scalar.activation(out=decb[:], in_=bk(2)[:, 0:NT], func=AF.Exp),
                         r=[("bank", 2)], w=["decb"])
                    if dbg:
                        T.dma("sp", dbg_out["dbg_cols"][h, 0], colw[:], r=[("col", 0)])
                        T.dma("sp", dbg_out["dbg_cols"][h, 1], colf[:], r=[("col", 1)])
                        T.dma("sp", dbg_out["dbg_cols"][h, 2], decb[:], r=["decb"])
                    def load_wqk(hh, defer=False):
                        return load_w(lambda: wqk2[hh % 2][:, :, :], [(w0v[:, :, 4096 + hh * 128:4096 + (hh + 1) * 128], 0),
                                                                      (w0v[:, :, 4608 + hh * 128:4608 + (hh + 1) * 128], 128)],
                                      ("wqk", hh % 2), defer=defer)

                    wqk = wqk2[h % 2]
                    if h == 0:
                        load_wqk(0)
                    cast_voz0 = load_w(lambda: wvoz[:, :, 0:512], [(w0v[:, :, 5120 + h * 256:5120 + (h + 1) * 256], 0),
                                                                   (w0v[:, :, 6152 + h * 256:6152 + (h + 1) * 256], 256)], "wvoz0",
                                       defer=True)
                    steps = [(qi, tb) for qi in range(2) for tb in range(NB)]
                    dsts = (qTm, kTm)

                    def conv_front(n):
                        qi, tb = steps[n]
                        pb = 4 + (n % 2)
                        for kc in range(8):
                            T.op("pe", lambda: nc.tensor.matmul(bk(pb)[:, :], lhsT=wqk[:, kc, qi * 128:(qi + 1) * 128],
                                                                rhs=hnT[:, kc, tb * 512:(tb + 1) * 512],
                                                                start=(kc == 0), stop=(kc == 7)),
                                 r=[("wqk", h % 2), ("hnT", tb)], w=[("bank", pb)], inc=(kc == 7))
                        T.op("act", lambda: nc.scalar.copy(out=pre[n % 2][:, 3:515], in_=bk(pb)[:, :]),
                             r=[("bank", pb)], w=[("pre", n % 2, "m")])

                    def conv_back(n):
                        qi, tb = steps[n]
                        pr = pre[n % 2]
                        if tb == 0:
                            T.op("dve", lambda: nc.vector.memset(pr[:, 0:3], 0.0), w=[("pre", n % 2, "c")])
                        T.op("dve", lambda: nc.vector.tensor_scalar(out=acc2[n % 2][:], in0=pr[:, 3:515], scalar1=cw_all[:, h * 2 + qi, 3:4],
                                                                    scalar2=None, op0=ALU.mult),
                             r=[("pre", n % 2, "m"), ("cw", h, qi)], w=[("acc", n % 2)])
                        for j in (2, 1, 0):
                            T.op("dve", lambda: nc.vector.scalar_tensor_tensor(out=acc2[n % 2][:], in0=pr[:, j:j + 512],
                                                                               scalar=cw_all[:, h * 2 + qi, j:j + 1], in1=acc2[n % 2][:],
                                                                               op0=ALU.mult, op1=ALU.add),
                                 r=[("pre", n % 2, "m"), ("pre", n % 2, "c"), ("cw", h, qi), ("acc", n % 2)], w=[("acc", n % 2)])
                        T.op("act", lambda: nc.scalar.activation(out=dsts[qi][:, tb * 512:(tb + 1) * 512], in_=acc2[n % 2][:],
                                                                 func=AF.Silu, bias=cb_all[:, h * 2 + qi, 0:1]),
                             r=[("acc", n % 2), ("cb", h, qi)], w=[("qk", qi, tb)])
                        if n + 1 < len(steps) and steps[n + 1][1] > 0:
                            T.op("dve", lambda: nc.vector.tensor_copy(out=pre[(n + 1) % 2][:, 0:3], in_=pr[:, 512:515]),
                                 r=[("pre", n % 2, "m")], w=[("pre", (n + 1) % 2, "c")])

                    conv_front(0)
                    cast_voz1 = None
                    for n in range(len(steps)):
                        if n + 1 < len(steps):
                            conv_front(n + 1)
                        conv_back(n)
                        if n == len(steps) // 2 - 1:
                            cast_voz0()
                            cast_voz1 = load_w(lambda: wvoz[:, :, 512:768], [(w0v[:, :, 7176 + h * 256:7176 + (h + 1) * 256], 0)],
                                               "wvoz1", defer=True)
                    cast_voz1()
                    T.op("dve", lambda: nc.vector.memset(Cst[:], 0.0), w=["Cst"])

                    import os
                    _sk = os.environ.get("KSKIP", "")

                    def P_pre(tb):
                        par = tb % 2
                        for ci in range(4):
                            c = tb * 4 + ci
                            T.op("pe", lambda: nc.tensor.transpose(bkb(7)[:, 512 + ci * 128:512 + (ci + 1) * 128],
                                                                   kTm[:, c * 128:(c + 1) * 128], ident[:]),
                                 r=[("qk", 1, tb), "ident"], w=[("bank", 7)], inc=(ci == 3))
                        T.op("act", lambda: nc.scalar.copy(out=ktok4[par][:], in_=bkb(7)[:, 512:1024].rearrange("p (a b) -> p a b", a=4)),
                             r=[("bank", 7)], w=[("ktok", par, ci) for ci in range(4)])
                        if "v" not in _sk:
                          T.op("pool", lambda: nc.gpsimd.tensor_copy(out=vw4[par][:, :, 256], in_=colw[:, tb * 4:(tb + 1) * 4]),
                             r=[("col", 0)], w=[("vw1", par)])

                    def P_chunk(tb, ci):
                        par = tb % 2
                        if True:
                            c = tb * 4 + ci
                            cs = slice(c * 128, (c + 1) * 128)
                            xb = ci % 2
                            T.op("pe", lambda: nc.tensor.matmul(bk(xb)[:, 0:128], lhsT=kTm[:, cs], rhs=qTm[:, cs], start=True, stop=True),
                                 r=[("qk", 0, tb), ("qk", 1, tb)], w=[("bank", xb)])
                            for kc in range(8):
                                T.op("pe", lambda: nc.tensor.matmul(bk(xb)[:, 128:384], lhsT=hnT[:, kc, cs], rhs=wvoz[:, kc, 0:256],
                                                                    start=(kc == 0), stop=(kc == 7)),
                                     r=[("hnT", tb), "wvoz0"], w=[("bank", xb)], inc=(kc == 7))
                            hf = ci % 2
                            ob, zb = 2, 3 + ci // 2
                            for kc in range(8):
                                T.op("pe", lambda: nc.tensor.matmul(bk(ob)[:, hf * 256:(hf + 1) * 256], lhsT=hnT[:, kc, cs],
                                                                    rhs=wvoz[:, kc, 256:512], start=(kc == 0), stop=(kc == 7)),
                                     r=[("hnT", tb), "wvoz0"], w=[("bank", 2)], inc=(kc == 7))
                            for kc in range(8):
                                T.op("pe", lambda: nc.tensor.matmul(bk(zb)[:, hf * 256:(hf + 1) * 256], lhsT=hnT[:, kc, cs],
                                                                    rhs=wvoz[:, kc, 512:768], start=(kc == 0), stop=(kc == 7)),
                                     r=[("hnT", tb), "wvoz1"], w=[("bank", 3 + ci // 2)], inc=(kc == 7))
                            T.op("dve", lambda: nc.vector.tensor_tensor(out=SmT4[par][:, ci, :], in0=bk(xb)[:, 0:128], in1=tri[:], op=ALU.mult),
                                 r=[("bank", xb), "tri"], w=[("SmT", par, ci)])
                            T.op("dve", lambda: nc.vector.tensor_scalar(out=vw4[par][:, ci, 0:256], in0=bk(xb)[:, 128:384],
                                                                        scalar1=colw[:, c:c + 1], scalar2=None, op0=ALU.mult),
                                 r=[("bank", xb), ("col", 0)], w=[("vw", par, ci)])
                            if hf == 1 and "s" not in _sk:
                                T.op("act", lambda: nc.scalar.activation(out=sg4[par][:, ci - 1:ci + 1, :],
                                                                         in_=bk(ob)[:, :].rearrange("p (a b) -> p a b", a=2),
                                                                         func=AF.Sigmoid),
                                     r=[("bank", 2)], w=[("sg", par, ci // 2)])

                    def P_post(tb):
                        par = tb % 2
                        for pr_ in range(2 if "z" not in _sk else 0):
                            T.op("act", lambda: nc.scalar.activation(out=zs4[par][:, 2 * pr_:2 * pr_ + 2, :],
                                                                     in_=bk(3 + pr_)[:, :].rearrange("p (a b) -> p a b", a=2),
                                                                     func=AF.Silu),
                                 r=[("bank", 3 + pr_)], w=[("zs", par, pr_)])
                        if "g" not in _sk:
                          T.op("pool", lambda: nc.gpsimd.tensor_tensor(out=zs4[par][:], in0=zs4[par][:],
                                                                     in1=gml[:, None, :].to_broadcast([128, 4, 256]), op=ALU.mult),
                             r=[("zs", par, 0), ("zs", par, 1), "gml"], w=[("zs", par, 0), ("zs", par, 1)])

                    def rB(tb, ci):
                        par = tb % 2
                        c = tb * 4 + ci
                        T.op("dve", lambda: nc.vector.scalar_tensor_tensor(out=stR[:, ci, 0:1], in0=bk(5)[:, 256:257], scalar=-1.0,
                                                                           in1=colf[:, c:c + 1], op0=ALU.mult, op1=ALU.max),
                             r=[("bank", 5), ("col", 1)], w=[("stR", ci, 0)])
                        T.op("dve", lambda: nc.vector.tensor_tensor(out=stR[:, ci, 1:2], in0=stR[:, ci, 0:1], in1=bk(5)[:, 256:257],
                                                                    op=ALU.max),
                             r=[("stR", ci, 0), ("bank", 5)], w=[("stR", ci, 1)])
                        T.op("dve", lambda: nc.vector.reciprocal(out=stR[:, ci, 2:3], in_=stR[:, ci, 1:2]),
                             r=[("stR", ci, 1)], w=[("stR", ci, 2)])
                        T.op("dve", lambda: nc.vector.scalar_tensor_tensor(out=hmt4[par][:, ci, :], in0=bk(5)[:, 0:256],
                                                                           scalar=stR[:, ci, 2:3], in1=sg4[par][:, ci, :],
                                                                           op0=ALU.mult, op1=ALU.mult),
                             r=[("bank", 5), ("stR", ci, 2), ("sg", par, ci // 2)], w=[("hmt", par, ci)])
                        T.op("act", lambda: nc.scalar.activation(out=junkB[:], in_=hmt4[par][:, ci, :], func=AF.Square,
                                                                 accum_out=ssq4[par][:, ci:ci + 1]),
                             r=[("hmt", par, ci)], w=["junkB", ("ssq", par, ci)])

                    def R_chunk(tb, ci):
                        par = tb % 2
                        c = tb * 4 + ci
                        cs = slice(c * 128, (c + 1) * 128)
                        T.op("dve", lambda: nc.vector.tensor_scalar(out=Cd[:], in0=Cst[:], scalar1=decb[:, c:c + 1], scalar2=None,
                                                                    op0=ALU.mult), r=["Cst", "decb"], w=["Cd"])
                        T.op("dve", lambda: nc.vector.tensor_scalar(out=Cdb[:], in0=Cst[:], scalar1=decb[:, c:c + 1], scalar2=None,
                                                                    op0=ALU.mult), r=["Cst", "decb"], w=["Cdb"])
                        T.op("pe", lambda: nc.tensor.matmul(bk(6)[:, 0:257], lhsT=ktok4[par][:, ci, :], rhs=vw4[par][:, ci, :],
                                                            start=True, stop=True),
                             r=[("ktok", par, ci), ("vw", par, ci), ("vw1", par)], w=[("bank", 6)])
                        T.op("dve", lambda: nc.vector.tensor_tensor(out=Cst[:], in0=bk(6)[:, 0:257], in1=Cd[:], op=ALU.add),
                             r=[("bank", 6), "Cd"], w=["Cst"])
                        if ci > 0:
                            rB(tb, ci - 1)
                        T.op("pe", lambda: nc.tensor.matmul(bk(5)[:, 0:257], lhsT=SmT4[par][:, ci, :], rhs=vw4[par][:, ci, :],
                                                            start=True, stop=False),
                             r=[("SmT", par, ci), ("vw", par, ci), ("vw1", par)], w=[("bank", 5)], inc=False)
                        T.op("pe", lambda: nc.tensor.matmul(bk(5)[:, 0:257], lhsT=qTm[:, cs], rhs=Cdb[:], start=False, stop=True),
                             r=[("qk", 0, tb), "Cdb"], w=[("bank", 5)])

                    def N_pre(tb):
                        par = tb % 2
                        T.op("act", lambda: nc.scalar.activation(out=rs4[par][:, 0:4], in_=ssq4[par][:, 0:4], func=AF.Ln,
                                                                 scale=1.0 / 256, bias=EPS),
                             r=[("ssq", par, ci) for ci in range(4)], w=[("rs", par, 0)])
                        T.op("act", lambda: nc.scalar.activation(out=rs4[par][:, 4:8], in_=rs4[par][:, 0:4], func=AF.Exp, scale=-0.5),
                             r=[("rs", par, 0)], w=[("rs", par, 1)])
                        for ci in range(4):
                            T.op("dve", lambda: nc.vector.scalar_tensor_tensor(out=ytok4[:, ci, :], in0=hmt4[par][:, ci, :],
                                                                               scalar=rs4[par][:, 4 + ci:5 + ci], in1=zs4[par][:, ci, :],
                                                                               op0=ALU.mult, op1=ALU.mult),
                                 r=[("hmt", par, ci), ("rs", par, 1), ("zs", par, ci // 2)], w=[("ytok", ci)])

                    def N_T(tb, hf):
                        par = tb % 2
                        ytb = yTb[par]
                        for ci in range(4):
                            T.op("pe", lambda: nc.tensor.transpose(bkb(7)[:, ci * 128:(ci + 1) * 128],
                                                                   ytok4[:, ci, hf * 128:(hf + 1) * 128], ident[:]),
                                 r=[("ytok", ci), "ident"], w=[("bank", 7)], inc=(ci == 3))
                        T.op("act", lambda: nc.scalar.copy(out=ytb[:, hf, :], in_=bkb(7)[:, 0:512]),
                             r=[("bank", 7)], w=[("yTb", par)])

                    def N_out(tb):
                        par = tb % 2
                        r0 = 1024 + h * 256
                        T.dma("pool", yT[r0:r0 + 256, tb * 512:(tb + 1) * 512].rearrange("(a p) t -> p a t", p=128), yTb[par][:],
                              r=[("yTb", par)], w=[("yT", 8 + h, tb)])

                    P_pre(0)
                    for ci in range(4):
                        P_chunk(0, ci)
                    P_post(0)
                    cast_wqk_next = None
                    for tb in range(NB):
                        if h + 1 < 4 and tb == min(2, NB - 1):
                            cast_wqk_next = load_wqk(h + 1, defer=True)
                        elif cast_wqk_next is not None:
                            cast_wqk_next()
                            cast_wqk_next = None
                        if tb >= 1:
                            N_pre(tb - 1)
                        if tb + 1 < NB:
                            P_pre(tb + 1)
                        for ci in range(4):
                            if tb + 1 < NB:
                                P_chunk(tb + 1, ci)
                            R_chunk(tb, ci)
                            if tb >= 1 and ci == 1:
                                N_T(tb - 1, 0)
                            if tb >= 1 and ci == 3:
                                N_T(tb - 1, 1)
                                N_out(tb - 1)
                        rB(tb, 3)
                        if tb + 1 < NB:
                            P_post(tb + 1)
                    if cast_wqk_next is not None:
                        cast_wqk_next()
                        cast_wqk_next = None
                    N_pre(NB - 1)
                    N_T(NB - 1, 0)
                    N_T(NB - 1, 1)
                    N_out(NB - 1)
                T.barrier()
        h1dst = out
        scCD = es.enter_context(ExitStack())
        w1 = sb("w1", [128, 8, ODD_IN], BF16, scCD)
        rs1 = sb("rs1", [128, NT], F32, scCD)
        with ExitStack() as scC:
            wo0 = sb("wo0", [128, 16, D], BF16, scC)
            gp0 = sb("gp0", [128, D], F32, scC)
            T.dma("sp", gp0[:], l0_post_g.partition_broadcast(128), w=["gp0"])
            for kg in range(2):
                for cg in range(2):
                    T.dma("sp", wo0[:, kg * 8:(kg + 1) * 8, cg * 512:(cg + 1) * 512],
                          wo0bv[:, kg * 8:(kg + 1) * 8, cg * 512:(cg + 1) * 512], w=[("wo0", kg, cg)])
            wo0_keys = [("wo0", a, b) for a in range(2) for b in range(2)]
            w1_loads = list(range(ODD_IN // 512))
            yblk = [sb("yblk%d" % i, [128, 16, 512], BF16, scC) for i in range(2)]
            xt = [sb("xtC%d" % i, [128, D], F32, scC) for i in range(2)]
            junkC = sb("junkC", [128, D], BF16, scC)
            stC = [sb("stC%d" % i, [128, 8], F32, scC) for i in range(2)]
            h1 = [sb("h1_%d" % i, [128, D], F32, scC) for i in range(2)]
            for i in range(NT):
                tb, b = i // 4, i % 2
                yb_ = yblk[tb % 2]
                st_ = stC[b]
                if i % 4 == 0:
                    for half in range(2):
                        T.dma("sp", yb_[:, half * 8:(half + 1) * 8, :],
                              yT[half * 1024:(half + 1) * 1024, tb * 512:(tb + 1) * 512].rearrange("(a p) t -> p a t", p=128),
                              r=[("yT", hh, tb) for hh in range(half * 8, half * 8 + (8 if half == 0 else 4))],
                              w=[("yblk", tb % 2, half)])
                T.dma("sp", xt[b][:], x[i * 128:(i + 1) * 128, :], w=[("xtC", b)])
                if w1_loads and (i % 2 == 1 or NT - i <= len(w1_loads)):
                    cgp = w1_loads.pop(0)
                    T.dma("sp", w1[:, :, cgp * 512:(cgp + 1) * 512], w1bv[:, :, cgp * 512:(cgp + 1) * 512], w=[("w1", cgp)])
                ts_ = slice((i % 4) * 128, (i % 4 + 1) * 128)
                pbs = (2 * b, 2 * b + 1)
                for cg in range(2):
                    for kc in range(16):
                        T.op("pe", lambda: nc.tensor.matmul(bk(pbs[cg])[:, :], lhsT=yb_[:, kc, ts_], rhs=wo0[:, kc, cg * 512:(cg + 1) * 512],
                                                            start=(kc == 0), stop=(kc == 15)),
                             r=[("yblk", tb % 2, 0), ("yblk", tb % 2, 1)] + wo0_keys, w=[("bank", pbs[cg])], inc=(kc == 15))
                for cg in range(2):
                    T.op("act", lambda: nc.scalar.activation(out=junkC[:, cg * 512:(cg + 1) * 512], in_=bk(pbs[cg])[:, :], func=AF.Square,
                                                             accum_out=st_[:, cg:cg + 1]),
                         r=[("bank", pbs[cg])], w=[("junkC", cg), ("stC", b, cg)])
                T.op("dve", lambda: nc.vector.tensor_tensor(out=st_[:, 2:3], in0=st_[:, 0:1], in1=st_[:, 1:2], op=ALU.add),
                     r=[("stC", b, 0), ("stC", b, 1)], w=[("stC", b, 2)])
                rstd_from_ssq(st_[:, 2:3], st_[:, 3:4], st_[:, 4:5], D, [("stC", b, 2)], ("stC", b, 3), ("stC", b, 4))
                for cg in range(2):
                    T.op("dve", lambda: nc.vector.scalar_tensor_tensor(out=h1[b][:, cg * 512:(cg + 1) * 512], in0=bk(pbs[cg])[:, :],
                                                                       scalar=st_[:, 4:5], in1=gp0[:, cg * 512:(cg + 1) * 512],
                                                                       op0=ALU.mult, op1=ALU.mult),
                         r=[("bank", pbs[cg]), ("stC", b, 4), "gp0"], w=[("h1", b, cg)])
                T.op("dve", lambda: nc.vector.tensor_tensor(out=h1[b][:], in0=h1[b][:], in1=xt[b][:], op=ALU.add),
                     r=[("h1", b, 0), ("h1", b, 1), ("xtC", b)], w=[("h1", b, 0), ("h1", b, 1)])
                T.op("act", lambda: nc.scalar.activation(out=junkC[:], in_=h1[b][:], func=AF.Square, accum_out=st_[:, 5:6]),
                     r=[("h1", b, 0), ("h1", b, 1)], w=[("junkC", 0), ("junkC", 1), ("stC", b, 5)])
                rstd_from_ssq(st_[:, 5:6], st_[:, 6:7], rs1[:, i:i + 1], D, [("stC", b, 5)], ("stC", b, 6), ("rs1", i))
                T.dma("pool", h1dst[i * 128:(i + 1) * 128, :], h1[b][:], r=[("h1", b, 0), ("h1", b, 1)], w=[("h1d", i)])
                if dbg:
                    T.dma("pool", dbg_out["dbg_h1"][i * 128:(i + 1) * 128, :], h1[b][:], r=[("h1", b, 0), ("h1", b, 1)], w=[("dbgh1", i)])
            while w1_loads:
                cgp = w1_loads.pop(0)
                T.dma("sp", w1[:, :, cgp * 512:(cgp + 1) * 512], w1bv[:, :, cgp * 512:(cgp + 1) * 512], w=[("w1", cgp)])
            if dbg:
                T.barrier()
                T.dma("sp", dbg_out["dbg_yT"][:, :], yT[:, :])
            T.barrier()
        with ExitStack() as scD:
            wo1 = sb("wo1", [128, 16, D], BF16, scD)
            for kg in range(2):
                for cg in range(2):
                    T.dma("sp", wo1[:, kg * 8:(kg + 1) * 8, cg * 512:(cg + 1) * 512],
                          wo1bv[:, kg * 8:(kg + 1) * 8, cg * 512:(cg + 1) * 512], w=[("wo1", kg, cg)])
            wo1_keys = [("wo1", a, b) for a in range(2) for b in range(2)]
            gp1 = sb("gp1", [128, D], F32, scD)
            gq1 = sb("gq1", [128, D], F32, scD)
            gsg = sb("gsg", [128, 2048], F32, scD)
            bsp = sb("bsp", [128, 8], F32, scD)
            wmT = sb("wmT", [128, 8, 128], BF16, scD)
            T.dma("sp", gp1[:], l1_pre_g.partition_broadcast(128), w=["gp1"])
            T.dma("sp", gq1[:], l1_post_g.partition_broadcast(128), w=["gq1"])
            T.dma("sp", gsg[:], l1_sg_norm_g.partition_broadcast(128), w=["gsg"])
            T.dma("sp", bsp[:], l1_b_spatial.rearrange("g t -> t g"), w=["bsp"], allow_slow_non_contiguous=True)
            with ExitStack() as scS:
                wsp_f = sb("wsp_f", [128, 8, 128], F32, scS)
                wsp_b = sb("wsp_b", [128, 8, 128], BF16, scS)
                T.dma("sp", wsp_f[:], l1_w_spatial.rearrange("g t s -> t g s"), w=["wsp_f"])
                T.op("dve", lambda: nc.vector.tensor_copy(out=wsp_b[:], in_=wsp_f[:]), r=["wsp_f"], w=["wsp_b"])
                for g in range(8):
                    T.op("pe", lambda: nc.tensor.transpose(bkb(7)[:, g * 128:(g + 1) * 128], wsp_b[:, g, :], ident[:]),
                         r=["wsp_b", "ident"], w=[("bank", 7)], inc=(g == 7))
                T.op("dve", lambda: nc.vector.tensor_tensor(out=wmT[:], in0=bkb(7).rearrange("p (g t) -> p g t", g=8),
                                                            in1=tri[:, None, :].to_broadcast([128, 8, 128]), op=ALU.mult),
                     r=[("bank", 7), "tri"], w=["wmT"])
                T.barrier()

            ht = [sb("ht%d" % i, [128, D], F32, scD) for i in range(3)]
            hn1 = sb("hn1", [128, D], BF16, scD)
            hn1T = [sb("hn1T%d" % i, [128, 8, 128], BF16, scD) for i in range(2)]
            junkD = sb("junkD", [128, D], BF16, scD)
            stD = [sb("stD%d" % i, [128, 12], F32, scD) for i in range(2)]
            gu = [sb("gu%d" % i, [128, 2048], BF16, scD) for i in range(2)]
            gv = [sb("gv%d" % i, [128, 2048], BF16, scD) for i in range(2)]
            sz = [sb("sz%d" % i, [128, 2048], BF16, scD) for i in range(2)]
            vn = sb("vn", [128, 2048], BF16, scD)
            y1 = sb("y1", [128, 2048], BF16, scD)
            y1T = sb("y1T", [128, 16, 128], BF16, scD)
            ymx = sb("ymx", [128, D], F32, scD)
            guk = lambda p: [("gu", p, k) for k in range(4)]
            gvk = lambda p: [("gv", p, k) for k in range(4)]
            szk = lambda p: [("sz", p, k) for k in range(4)]
            y1k = [("y1", g) for g in range(8)]

            def stA(i):
                p = i % 2
                T.dma("sp", ht[i % 3][:], h1dst[i * 128:(i + 1) * 128, :], r=[("h1d", i)], w=[("ht", i % 3)])
                T.op("dve", lambda: nc.vector.scalar_tensor_tensor(out=hn1[:], in0=ht[i % 3][:], scalar=rs1[:, i:i + 1], in1=gp1[:],
                                                                   op0=ALU.mult, op1=ALU.mult),
                     r=[("ht", i % 3), "gp1"], w=["hn1"])
                for kc in range(8):
                    T.op("pe", lambda: nc.tensor.transpose(bkb(7)[:, kc * 128:(kc + 1) * 128], hn1[:, kc * 128:(kc + 1) * 128], ident[:]),
                         r=["hn1", "ident"], w=[("bank", 7)], inc=(kc == 7))
                T.op("dve", lambda: nc.vector.tensor_copy(out=hn1T[p][:], in_=bkb(7).rearrange("p (k t) -> p k t", k=8)),
                     r=[("bank", 7)], w=[("hn1T", p)])

            def stB(i, groups):
                p = i % 2
                for cgp in groups:
                    pb = (0, 1, 3, 4)[cgp % 4]
                    for kc in range(8):
                        T.op("pe", lambda: nc.tensor.matmul(bk(pb)[:, :], lhsT=hn1T[p][:, kc, :], rhs=w1[:, kc, cgp * 512:(cgp + 1) * 512],
                                                            start=(kc == 0), stop=(kc == 7)),
                             r=[("hn1T", p)], w=[("bank", pb)], inc=(kc == 7))
                    sec, cc = cgp // 4, (cgp % 4) * 512
                    dst, fn, key = ((gu[p], AF.Gelu_apprx_tanh, "gu"), (gv[p], AF.Gelu_apprx_tanh, "gv"), (sz[p], AF.Silu, "sz"))[sec]
                    T.op("act", lambda: nc.scalar.activation(out=dst[:, cc:cc + 512], in_=bk(pb)[:, :], func=fn),
                         r=[("bank", pb)], w=[(key, p, cgp % 4)])

            def stC1a(i):
                p = i % 2
                T.op("act", lambda: nc.scalar.activation(out=vn[:], in_=gv[p][:], func=AF.Square, accum_out=stD[p][:, 3:4]),
                     r=gvk(p), w=["vn", ("stD", p, 3)])
                rstd_from_ssq(stD[p][:, 3:4], stD[p][:, 4:5], stD[p][:, 5:6], 2048, [("stD", p, 3)], ("stD", p, 4), ("stD", p, 5))

            def stC1b(i):
                p = i % 2
                T.op("dve", lambda: nc.vector.scalar_tensor_tensor(out=vn[:], in0=gv[p][:], scalar=stD[p][:, 5:6], in1=gsg[:],
                                                                   op0=ALU.mult, op1=ALU.mult),
                     r=gvk(p) + [("stD", p, 5), "gsg"], w=["vn"])
                T.op("pool", lambda: nc.gpsimd.tensor_tensor(out=gu[p][:], in0=gu[p][:], in1=sz[p][:], op=ALU.mult),
                     r=guk(p) + szk(p), w=guk(p))

            def stC2(i):
                p = i % 2
                for half in range(2):
                    for gg in range(4):
                        g = half * 4 + gg
                        pb = 5 + gg // 2
                        T.op("pe", lambda: nc.tensor.matmul(bk(pb)[:, (gg % 2) * 256:(gg % 2 + 1) * 256], lhsT=wmT[:, g, :],
                                                            rhs=vn[:, g * 256:(g + 1) * 256], start=True, stop=True),
                             r=["wmT", "vn"], w=[("bank", pb)], inc=(gg % 2 == 1))
                    for gg in range(4):
                        g = half * 4 + gg
                        pb = 5 + gg // 2
                        T.op("dve", lambda: nc.vector.scalar_tensor_tensor(out=y1[:, g * 256:(g + 1) * 256],
                                                                           in0=bk(pb)[:, (gg % 2) * 256:(gg % 2 + 1) * 256],
                                                                           scalar=bsp[:, g:g + 1], in1=gu[p][:, g * 256:(g + 1) * 256],
                                                                           op0=ALU.add, op1=ALU.mult),
                             r=[("bank", pb), "bsp"] + guk(p), w=[("y1", g)])

            def stC3(i):
                for half in range(2):
                    tbk = 2 if half == 0 else 7
                    for kk in range(8):
                        kc = half * 8 + kk
                        T.op("pe", lambda: nc.tensor.transpose(bkb(tbk)[:, kk * 128:(kk + 1) * 128], y1[:, kc * 128:(kc + 1) * 128], ident[:]),
                             r=y1k + ["ident"], w=[("bank", tbk)], inc=(kk == 7))
                    if half == 0:
                        T.op("dve", lambda: nc.vector.tensor_copy(out=y1T[:, 0:8, :], in_=bkb(2).rearrange("p (k t) -> p k t", k=8)),
                             r=[("bank", 2)], w=[("y1T", 0)])
                    else:
                        T.op("act", lambda: nc.scalar.copy(out=y1T[:, 8:16, :], in_=bkb(7).rearrange("p (k t) -> p k t", k=8)),
                             r=[("bank", 7)], w=[("y1T", 1)])

            def stD_(i):
                p = i % 2
                for cg in range(2):
                    pb = 5 + cg
                    for kc in range(16):
                        T.op("pe", lambda: nc.tensor.matmul(bk(pb)[:, :], lhsT=y1T[:, kc, :], rhs=wo1[:, kc, cg * 512:(cg + 1) * 512],
                                                            start=(kc == 0), stop=(kc == 15)),
                             r=[("y1T", 0), ("y1T", 1)], w=[("bank", pb)], inc=(kc == 15))
                for cg in range(2):
                    T.op("act", lambda: nc.scalar.copy(out=ymx[:, cg * 512:(cg + 1) * 512], in_=bk(5 + cg)[:, :]),
                         r=[("bank", 5 + cg)], w=[("ymx", cg)])
                T.op("act", lambda: nc.scalar.activation(out=junkD[:, 0:D], in_=ymx[:], func=AF.Square, accum_out=stD[p][:, 6:7]),
                     r=[("ymx", 0), ("ymx", 1)], w=["junkD", ("stD", p, 6)])
                rstd_from_ssq(stD[p][:, 6:7], stD[p][:, 7:8], stD[p][:, 8:9], D, [("stD", p, 6)], ("stD", p, 7), ("stD", p, 8))
                T.op("dve", lambda: nc.vector.scalar_tensor_tensor(out=ymx[:], in0=ymx[:], scalar=stD[p][:, 8:9], in1=gq1[:],
                                                                   op0=ALU.mult, op1=ALU.mult),
                     r=[("ymx", 0), ("ymx", 1), ("stD", p, 8), "gq1"], w=[("ymx", 0), ("ymx", 1)])
                T.op("dve", lambda: nc.vector.tensor_tensor(out=ht[i % 3][:], in0=ymx[:], in1=ht[i % 3][:], op=ALU.add),
                     r=[("ymx", 0), ("ymx", 1), ("ht", i % 3)], w=[("ht", i % 3)])
                T.dma("pool", out[i * 128:(i + 1) * 128, :], ht[i % 3][:], r=[("ht", i % 3), ("h1d", i)], w=[("outd", i)])

            stA(0)
            for i in range(NT):
                stB(i, range(0, 3))
                if i >= 1:
                    stC1b(i - 1)
                stB(i, range(3, 5))
                if i + 1 < NT:
                    stA(i + 1)
                if i >= 1:
                    stC2(i - 1)
                stB(i, range(5, 8))
                if i >= 1:
                    stC3(i - 1)
                stB(i, range(8, 10))
                if i >= 1:
                    stD_(i - 1)
                stC1a(i)
                stB(i, range(10, 12))
            stC1b(NT - 1)
            stC2(NT - 1)
            stC3(NT - 1)
            stD_(NT - 1)
            T.finish("sp")
            T.barrier()
    return nc


_NC_CACHE = {}


def _consts():
    ident = np.eye(128, dtype=np.float32).astype(ml_dtypes.bfloat16)
    k = np.arange(128)
    tri = (k[:, None] <= k[None, :]).astype(np.float32).astype(ml_dtypes.bfloat16)
    return {"c_ident": ident, "c_tri": tri}


def kernel(**inputs):
    x = np.ascontiguousarray(inputs["x"], dtype=np.float32)
    B, S, _ = x.shape
    if S not in _NC_CACHE:
        _NC_CACHE[S] = build(S)
    nc = _NC_CACHE[S]
    shared = {k: np.ascontiguousarray(v, dtype=np.float32) for k, v in inputs.items() if k != "x"}
    shared.update(_consts())
    in_maps = []
    for b in range(B):
        m = dict(shared)
        m["x"] = x[b]
        in_maps.append(m)
    res = run_bass_kernel_spmd(nc, in_maps, core_ids=list(range(B)))
    return np.stack([r["out"] for r in res.results], axis=0).astype(np.float32)
```

```python
import numpy as np
import ml_dtypes
from contextlib import ExitStack
import concourse.bass as bass
import concourse.mybir as mybir
from concourse.bass_utils import run_bass_kernel_spmd

F32, BF16 = mybir.dt.float32, mybir.dt.bfloat16
AF = mybir.ActivationFunctionType
ALU = mybir.AluOpType

D = 1024
EVEN_IN = 8200
ODD_IN = 6144
EPS = 1e-6
NDS = 8


class Trk:
    def __init__(self, nc, es):
        self.nc = nc
        self.E = {"pe": nc.tensor, "act": nc.scalar, "dve": nc.vector, "pool": nc.gpsimd, "sp": nc.sync}
        self.csem, self.ccnt = {}, {}
        for e in ("pe", "act", "dve", "pool"):
            self.csem[e] = es.enter_context(nc.semaphore("c_" + e))
            self.ccnt[e] = 0
        self.dq = {}
        for q in ("sp", "pool", "act"):
            self.dq[q] = dict(sems=[es.enter_context(nc.semaphore("d_%s%d" % (q, i))) for i in range(NDS)],
                              cnt=[0] * NDS, nxt=0)
        self.waited = {e: {} for e in self.E}
        self.res = {}
        self.pend_r, self.pend_w = [], []

    def _deps(self, r, w):
        deps = []
        for k in r:
            ent = self.res.get(k)
            if ent and ent[0]:
                deps.append(ent[0])
        for k in w:
            ent = self.res.get(k)
            if ent:
                if ent[0]:
                    deps.append(ent[0])
                deps.extend(ent[1])
        return deps

    def _wait(self, e, deps):
        best = {}
        for (key, sem, val) in deps:
            if key not in best or best[key][1] < val:
                best[key] = (sem, val)
        for key, (sem, val) in best.items():
            if e == "pe" and key == "c_pe":
                continue
            if self.waited[e].get(key, 0) >= val:
                continue
            self.E[e].wait_ge(sem, val)
            self.waited[e][key] = val

    def _reg(self, ev, r, w):
        for k in r:
            ent = self.res.setdefault(k, [None, []])
            ent[1].append(ev)
        for k in w:
            self.res[k] = [ev, []]

    def op(self, e, emit, r=(), w=(), inc=True):
        r, w = list(r), list(w)
        self._wait(e, self._deps(r, w))
        ins = emit()
        if e == "pe" and not inc:
            self.pend_r += r
            self.pend_w += w
            return ins
        self.ccnt[e] += 1
        ins.then_inc(self.csem[e], 1)
        ev = ("c_" + e, self.csem[e], self.ccnt[e])
        if e == "pe":
            r = r + self.pend_r
            w = w + self.pend_w
            self.pend_r, self.pend_w = [], []
        self._reg(ev, r, w)
        return ins

    def dma(self, q, out, in_, r=(), w=(), **kw):
        r, w = list(r), list(w)
        Q = self.dq[q]
        k = Q["nxt"]
        Q["nxt"] = (k + 1) % NDS
        key = "d_%s%d" % (q, k)
        deps = self._deps(r, w)
        if Q["cnt"][k] > 0:
            deps.append((key, Q["sems"][k], Q["cnt"][k]))
        self._wait(q, deps)
        ins = self.E[q].dma_start(out=out, in_=in_, **kw)
        Q["cnt"][k] += 16
        ins.then_inc(Q["sems"][k], 16)
        self._reg((key, Q["sems"][k], Q["cnt"][k]), r, w)
        return ins

    def barrier(self):
        assert not self.pend_r and not self.pend_w
        evs = [("c_" + e, self.csem[e], self.ccnt[e]) for e in self.csem if self.ccnt[e] > 0]
        for q, Q in self.dq.items():
            for k in range(NDS):
                if Q["cnt"][k] > 0:
                    evs.append(("d_%s%d" % (q, k), Q["sems"][k], Q["cnt"][k]))
        for e in self.E:
            self._wait(e, evs)
        self.res = {}

    def finish(self, e="sp"):
        evs = []
        for q, Q in self.dq.items():
            for k in range(NDS):
                if Q["cnt"][k] > 0:
                    evs.append(("d_%s%d" % (q, k), Q["sems"][k], Q["cnt"][k]))
        self._wait(e, evs)


def build(S=4096, dbg=False):
    assert S % 512 == 0
    NT = S // 128
    NB = S // 512
    nc = bass.Bass("TRN2", target_bir_lowering=False)

    def din(name, shape, dt=F32):
        return nc.dram_tensor(name, list(shape), dt, kind="ExternalInput").ap()

    x = din("x", [S, D])
    l0_pre_g = din("l0_pre_g", [D]); l0_w_in = din("l0_w_in", [D, EVEN_IN])
    l0_b_igate = din("l0_b_igate", [4]); l0_b_fgate = din("l0_b_fgate", [4])
    l0_conv_w = din("l0_conv_w", [4, 1024]); l0_conv_b = din("l0_conv_b", [1024])
    lq1 = din("l0_lambda_q1", [64]); lk1 = din("l0_lambda_k1", [64])
    lq2 = din("l0_lambda_q2", [64]); lk2 = din("l0_lambda_k2", [64])
    l0_da_head_g = din("l0_da_head_g", [128]); l0_ml_head_g = din("l0_ml_head_g", [256])
    l0_w_out = din("l0_w_out", [2048, D]); l0_post_g = din("l0_post_g", [D])
    l1_pre_g = din("l1_pre_g", [D]); l1_w_in = din("l1_w_in", [D, ODD_IN])
    l1_sg_norm_g = din("l1_sg_norm_g", [2048]); l1_w_spatial = din("l1_w_spatial", [8, 128, 128])
    l1_b_spatial = din("l1_b_spatial", [8, 128]); l1_w_out = din("l1_w_out", [2048, D])
    l1_post_g = din("l1_post_g", [D])
    c_ident = din("c_ident", [128, 128], BF16)
    c_tri = din("c_tri", [128, 128], BF16)
    out = nc.dram_tensor("out", [S, D], F32, kind="ExternalOutput").ap()
    yT = nc.dram_tensor("yT_scr", [2048, S], BF16, kind="Internal").ap()
    rows_scr = nc.dram_tensor("rows_scr", [2, 4, S], F32, kind="Internal").ap()
    wo0_bf = nc.dram_tensor("wo0_bf", [2048, D], BF16, kind="Internal").ap()
    w1_bf = nc.dram_tensor("w1_bf", [D, ODD_IN], BF16, kind="Internal").ap()
    wo1_bf = nc.dram_tensor("wo1_bf", [2048, D], BF16, kind="Internal").ap()
    dbg_out = {}
    if dbg:
        dbg_out["dbg_h1"] = nc.dram_tensor("dbg_h1", [S, D], F32, kind="ExternalOutput").ap()
        dbg_out["dbg_yT"] = nc.dram_tensor("dbg_yT", [2048, S], BF16, kind="ExternalOutput").ap()
        dbg_out["dbg_rows"] = nc.dram_tensor("dbg_rows", [3, 4, S], F32, kind="ExternalOutput").ap()
        dbg_out["dbg_cols"] = nc.dram_tensor("dbg_cols", [4, 3, 128, S // 128], F32, kind="ExternalOutput").ap()
        dbg_out["dbg_r32"] = nc.dram_tensor("dbg_r32", [4, 2, S], F32, kind="ExternalOutput").ap()

    w0v = l0_w_in.rearrange("(kc p) e -> p kc e", p=128)
    w1v = l1_w_in.rearrange("(kc p) e -> p kc e", p=128)
    wo0v = l0_w_out.rearrange("(kc p) e -> p kc e", p=128)
    wo1v = l1_w_out.rearrange("(kc p) e -> p kc e", p=128)
    w1bv = w1_bf.rearrange("(kc p) e -> p kc e", p=128)
    wo0bv = wo0_bf.rearrange("(kc p) e -> p kc e", p=128)
    wo1bv = wo1_bf.rearrange("(kc p) e -> p kc e", p=128)

    with ExitStack() as es:
        E = es.enter_context
        T = Trk(nc, es)

        def sb(name, shape, dt=F32, scope=None):
            return (scope or es).enter_context(nc.sbuf_tensor(name, list(shape), dt))

        S2 = [E(nc.psum_tensor("S2_%d" % i, [128, 1024], F32)) for i in range(2)]
        banks = [None] * 4 + [E(nc.psum_tensor("bank%d" % i, [128, 512], F32)) for i in range(4, 8)]

        def bk(i):
            if i < 4:
                return S2[i // 2][:, (i % 2) * 512:(i % 2 + 1) * 512]
            return banks[i]

        def bkb(i):
            if i < 4:
                return S2[i // 2][:].bitcast(BF16)[:, (i % 2) * 1024:(i % 2 + 1) * 1024]
            return banks[i][:].bitcast(BF16)

        ident = sb("ident", [128, 128], BF16)
        tri = sb("tri", [128, 128], BF16)
        ones_bf = sb("ones_bf", [128, 128], BF16)
        inv128_bf = sb("inv128_bf", [128, 128], BF16)
        ones_f = sb("ones_f", [128, 128], F32)
        T.dma("sp", ident[:], c_ident[:, :], w=["ident"])
        T.dma("sp", tri[:], c_tri[:, :], w=["tri"])
        T.op("pool", lambda: nc.gpsimd.memset(ones_bf[:], 1.0), w=["ones_bf"])
        T.op("pool", lambda: nc.gpsimd.memset(inv128_bf[:], 1.0 / 128), w=["inv128_bf"])
        T.op("pool", lambda: nc.gpsimd.memset(ones_f[:], 1.0), w=["ones_f"])

        NSTv = [2]
        wst = []
        wst_i = [0]
        wst_gen = [0]

        def alloc_wst(scope):
            wst_gen[0] += 1
            wst[:] = [sb("wst%d_%d" % (wst_gen[0], i), [128, 8, 512], F32, scope) for i in range(NSTv[0])]

        def load_w(dst_fn, srcs, dst_key, defer=False):
            b = wst_i[0] % len(wst)
            wst_i[0] += 1
            tot = 0
            kcn = srcs[0][0].shape[1]
            for (src, off) in srcs:
                n = src.shape[2]
                T.dma("sp", wst[b][:, 0:kcn, off:off + n], src, w=[("wst", b, off)], r=[])
                tot = max(tot, off + n)
            def cast():
                T.op("dve", lambda: nc.vector.tensor_copy(out=dst_fn(), in_=wst[b][:, 0:kcn, 0:tot]),
                     r=[("wst", b, off) for (_, off) in srcs], w=[dst_key])
            if defer:
                return cast
            cast()

        def rstd_from_ssq(ssq_ap, tmp_ap, out_ap, n, keys_r, key_tmp, key_out):
            T.op("act", lambda: nc.scalar.activation(out=tmp_ap, in_=ssq_ap, func=AF.Ln, scale=1.0 / n, bias=EPS),
                 r=keys_r, w=[key_tmp])
            T.op("act", lambda: nc.scalar.activation(out=out_ap, in_=tmp_ap, func=AF.Exp, scale=-0.5),
                 r=[key_tmp], w=[key_out])

        with ExitStack() as sc0:
            hnT = sb("hnT", [128, 8, S], BF16, sc0)
            with ExitStack() as scA:
                NSTv[0] = 2
                alloc_wst(scA)
                QT = sb("QT", [128, S], BF16, scA)
                KT = sb("KT", [128, S], BF16, scA)
                V = sb("V", [128, NT, 128], BF16, scA)
                zT2 = [sb("zT%d" % i, [128, S], BF16, scA) for i in range(2)]
                Ocp = sb("Ocp", [128, 2, 512], F32, scA)
                Lcp = sb("Lcp", [128, 2, 512], F32, scA)
                pend_tail = [None]
                wbf2 = [sb("wbfA%d" % i, [128, 8, 512], BF16, scA) for i in range(2)]
                P2 = [sb("P2_%d" % b, [128, 2, 512], BF16, scA) for b in range(2)]
                r1 = sb("r1", [128, 512], F32, scA)
                t1 = sb("t1", [128, 512], F32, scA)
                t2 = sb("t2", [128, 512], F32, scA)
                oT = sb("oT", [128, 512], F32, scA)
                sq = sb("sqA", [128, 512], BF16, scA)
                lnv = sb("lnvA", [128, 512], F32, scA)
                rsd = sb("rsdA", [128, 512], F32, scA)
                yb = [sb("ybA%d" % i, [128, 512], BF16, scA) for i in range(2)]
                lam4 = sb("lam4", [128, 4, 64], F32, scA)
                lamj = sb("lamj", [128, 64], F32, scA)
                lams = sb("lams", [128, 8], F32, scA)
                gda = sb("gda", [128, 2, 16], F32, scA)
                for i, a in enumerate((lq1, lk1, lq2, lk2)):
                    T.dma("sp", lam4[:, i, :], a.partition_broadcast(128), w=[("lam4", i)])
                for i in range(2):
                    T.op("dve", lambda: nc.vector.tensor_tensor(out=lamj[:], in0=lam4[:, 2 * i, :], in1=lam4[:, 2 * i + 1, :],
                                                                op=ALU.mult),
                         r=[("lam4", 2 * i), ("lam4", 2 * i + 1)], w=["lamj"])
                    T.op("dve", lambda: nc.vector.reduce_sum(out=lams[:, i:i + 1], in_=lamj[:], axis=mybir.AxisListType.X),
                         r=["lamj"], w=[("lams", i)])
                    T.op("act", lambda: nc.scalar.activation(out=lams[:, 2 + i:3 + i], in_=lams[:, i:i + 1], func=AF.Exp),
                         r=[("lams", i)], w=[("lams", 2 + i)])
                T.op("dve", lambda: nc.vector.tensor_tensor(out=lams[:, 4:5], in0=lams[:, 3:4], in1=lams[:, 2:3], op=ALU.subtract),
                     r=[("lams", 2), ("lams", 3)], w=[("lams", 4)])
                T.op("dve", lambda: nc.vector.tensor_scalar(out=lams[:, 5:6], in0=lams[:, 4:5], scalar1=-0.2, scalar2=None,
                                                            op0=ALU.add),
                     r=[("lams", 4)], w=["neglam"])
                neglam = lams[:, 5:6]
                T.dma("sp", gda[:, 0, 0:1], l0_da_head_g.rearrange("(p o) -> p o", o=1), w=["gda0"])
                T.op("dve", lambda: nc.vector.tensor_scalar(out=gda[:, 1, 0:1], in0=gda[:, 0, 0:1], scalar1=0.8, scalar2=None,
                                                            op0=ALU.mult), r=["gda0"], w=["gda"])

                def load_head(hh):
                    srcs = [(w0v[:, :, off + hh * 128: off + (hh + 1) * 128], i * 128)
                            for i, off in enumerate((0, 1024, 2048, 3072))]
                    load_w(lambda: wbf2[hh % 2][:, :, :], srcs, ("wbfA", hh % 2))

                pcnt = [0]

                def proj_block(h, tb, wbf, wkey, zT):
                    for (dst, c0, kind) in ((QT, 0, "q"), (KT, 128, "k"), (zT, 384, "z")):
                        pb = pcnt[0] % 4
                        pcnt[0] += 1
                        for kc in range(8):
                            T.op("pe", lambda: nc.tensor.matmul(bk(pb)[:, :], lhsT=wbf[:, kc, c0:c0 + 128],
                                                                rhs=hnT[:, kc, tb * 512:(tb + 1) * 512],
                                                                start=(kc == 0), stop=(kc == 7)),
                                 r=[wkey, ("hnT", tb)], w=[("bank", pb)], inc=(kc == 7))
                        d = dst[:, tb * 512:(tb + 1) * 512]
                        if kind == "z":
                            T.op("act", lambda: nc.scalar.activation(out=d, in_=bk(pb)[:, :], func=AF.Silu),
                                 r=[("bank", pb)], w=[("z", h % 2, tb)])
                        else:
                            T.op("dve", lambda: nc.vector.tensor_copy(out=d, in_=bk(pb)[:, :]), r=[("bank", pb)], w=[(kind, tb)])
                    tg = tb
                    pb = pcnt[0] % 4
                    pcnt[0] += 1
                    for ti in range(4):
                        tt = tg * 4 + ti
                        for kc in range(8):
                            T.op("pe", lambda: nc.tensor.matmul(bk(pb)[:, ti * 128:(ti + 1) * 128],
                                                                lhsT=hnT[:, kc, tt * 128:(tt + 1) * 128],
                                                                rhs=wbf[:, kc, 256:384], start=(kc == 0), stop=(kc == 7)),
                                 r=[wkey, ("hnT", tg)], w=[("bank", pb)], inc=(kc == 7 and ti == 3))
                    T.op("dve", lambda: nc.vector.tensor_copy(out=V[:, tg * 4:(tg + 1) * 4, :],
                                                              in_=bk(pb)[:, :].rearrange("p (a b) -> p a b", a=4)),
                         r=[("bank", pb)], w=[("v", tg)])

                load_head(0)
                with ExitStack() as sc1:
                    g0bc = sb("g0bc", [128, D], F32, sc1)
                    T.dma("sp", g0bc[:], l0_pre_g.partition_broadcast(128), w=["g0bc"])
                    xt = [sb("xt%d" % i, [128, D], F32, sc1) for i in range(3)]
                    xn = [sb("xn%d" % i, [128, D], BF16, sc1) for i in range(2)]
                    junk = [sb("junk%d" % i, [128, D], BF16, sc1) for i in range(2)]
                    st = [sb("st%d" % i, [128, 4], F32, sc1) for i in range(2)]
                    for i in range(NT):
                        b = i % 2
                        bx = i % 3
                        T.dma("sp" if i % 2 == 0 else "pool", xt[bx][:], x[i * 128:(i + 1) * 128, :], w=[("xt", bx)])
                        T.op("act", lambda: nc.scalar.activation(out=junk[b][:], in_=xt[bx][:], func=AF.Square,
                                                                 accum_out=st[b][:, 0:1]),
                             r=[("xt", bx)], w=[("junk", b), ("ssq", b)])
                        rstd_from_ssq(st[b][:, 0:1], st[b][:, 1:2], st[b][:, 2:3], D, [("ssq", b)], ("lnv", b), ("rstd", b))
                        T.op("dve", lambda: nc.vector.scalar_tensor_tensor(out=xn[b][:], in0=xt[bx][:], scalar=st[b][:, 2:3],
                                                                           in1=g0bc[:], op0=ALU.mult, op1=ALU.mult),
                             r=[("xt", bx), ("rstd", b), "g0bc"], w=[("xn", b)])
                        pb = 6 + b
                        for kc in range(8):
                            T.op("pe", lambda: nc.tensor.transpose(bkb(pb)[:, kc * 128:(kc + 1) * 128],
                                                                   xn[b][:, kc * 128:(kc + 1) * 128], ident[:]),
                                 r=[("xn", b), "ident"], w=[("bank", pb)], inc=(kc == 7))
                        src = bkb(pb).rearrange("p (k t) -> p k t", k=8)
                        dst = hnT[:, :, i * 128:(i + 1) * 128]
                        if i % 2 == 0:
                            T.op("dve", lambda: nc.vector.tensor_copy(out=dst, in_=src), r=[("bank", pb)], w=[("hnT", i // 4)])
                        else:
                            T.op("act", lambda: nc.scalar.copy(out=dst, in_=src), r=[("bank", pb)], w=[("hnT", i // 4)])
                        if i % 4 == 3 and i // 4 >= 1:
                            proj_block(0, i // 4 - 1, wbf2[0], ("wbfA", 0), zT2[0])
                    proj_block(0, NB - 1, wbf2[0], ("wbfA", 0), zT2[0])
                for h in range(8):
                    wbf = wbf2[h % 2]
                    wkey = ("wbfA", h % 2)
                    zT = zT2[h % 2]
                    if h > 0:
                        for tb in range(NB):
                            proj_block(h, tb, wbf, wkey, zT)
                    if h == 0:
                        cast_jobs = []
                        for (src, dst, nrow) in ((l0_w_out, wo0_bf, 2048), (l1_w_in, w1_bf, D), (l1_w_out, wo1_bf, 2048)):
                            for r0 in range(0, nrow, 256):
                                cast_jobs.append((dst[r0:r0 + 256, :], src[r0:r0 + 256, :]))
                    if h + 1 < 8:
                        load_head(h + 1)

                    for qb in range(NB):
                        nj = 4 * qb + 4
                        if h >= 1 and cast_jobs:
                            cj = cast_jobs.pop(0)
                            T.dma("pool", cj[0], cj[1])

                        def qk(j):
                            c0 = max(0, j - 4 * qb) * 128
                            sbuf_i = j % 2
                            for c in range(2):
                                pb = 2 * sbuf_i + c
                                T.op("pe", lambda: nc.tensor.matmul(bk(pb)[:, c0:512],
                                                                    lhsT=KT[c * 64:(c + 1) * 64, j * 128:(j + 1) * 128],
                                                                    rhs=QT[c * 64:(c + 1) * 64, qb * 512 + c0:(qb + 1) * 512],
                                                                    start=True, stop=True),
                                     r=[("k", j // 4), ("q", qb)], w=[("bank", pb)], inc=True)

                        qk(0)
                        for j in range(nj):
                            c0 = max(0, j - 4 * qb) * 128
                            si = j % 2
                            if j + 1 < nj:
                                qk(j + 1)
                            T.op("act", lambda: nc.scalar.activation(
                                out=P2[si][:, :, c0:512],
                                in_=S2[si][:, :].rearrange("p (c q) -> p c q", c=2)[:, :, c0:512],
                                func=AF.Exp, scale=0.125),
                                 r=[("bank", 2 * si), ("bank", 2 * si + 1)], w=[("P", si)])
                            if j >= 4 * qb:
                                meng, mfn = ("dve", nc.vector.tensor_tensor) if qb >= 2 else ("pool", nc.gpsimd.tensor_tensor)
                                T.op(meng, lambda: mfn(out=P2[si][:, :, c0:c0 + 128], in0=P2[si][:, :, c0:c0 + 128],
                                                       in1=tri[:, None, :].to_broadcast([128, 2, 128]), op=ALU.mult),
                                     r=[("P", si), "tri"], w=[("P", si)])
                            if pend_tail[0] is not None and j == min(8, nj - 1):
                                pend_tail[0](2 * si)
                                pend_tail[0] = None
                            for c in range(2):
                                T.op("pe", lambda: nc.tensor.matmul(bk(4 + c)[:, c0:512], lhsT=V[:, j, :],
                                                                    rhs=P2[si][:, c, c0:512], start=(j == 0), stop=(j == nj - 1)),
                                     r=[("P", si), ("v", j // 4)], w=[("bank", 4 + c)], inc=False)
                                T.op("pe", lambda: nc.tensor.matmul(bk(6 + c)[:, c0:512], lhsT=ones_bf[:],
                                                                    rhs=P2[si][:, c, c0:512], start=(j == 0), stop=(j == nj - 1)),
                                     r=[("P", si), "ones_bf"], w=[("bank", 6 + c)], inc=True)
                        for c in range(2):
                            T.op("dve", lambda: nc.vector.tensor_copy(out=Lcp[:, c, :], in_=bk(6 + c)[:, :]), r=[("bank", 6 + c)], w=[("Lcp", c)])
                            T.op("dve", lambda: nc.vector.tensor_copy(out=Ocp[:, c, :], in_=bk(4 + c)[:, :]), r=[("bank", 4 + c)], w=[("Ocp", c)])
                        T.op("dve", lambda: nc.vector.reciprocal(out=r1[:], in_=Lcp[:, 0, :]), r=[("Lcp", 0)], w=["r1"])
                        T.op("dve", lambda: nc.vector.tensor_tensor(out=t1[:], in0=Ocp[:, 0, :], in1=r1[:], op=ALU.mult),
                             r=[("Ocp", 0), "r1"], w=["t1"])
                        T.op("dve", lambda: nc.vector.reciprocal(out=r1[:], in_=Lcp[:, 1, :]), r=[("Lcp", 1)], w=["r1"])
                        T.op("dve", lambda: nc.vector.tensor_tensor(out=t2[:], in0=Ocp[:, 1, :], in1=r1[:], op=ALU.mult),
                             r=[("Ocp", 1), "r1"], w=["t2"])
                        T.op("dve", lambda: nc.vector.scalar_tensor_tensor(out=oT[:], in0=t2[:], scalar=neglam, in1=t1[:],
                                                                           op0=ALU.mult, op1=ALU.add),
                             r=["t1", "t2", "neglam"], w=["oT"])
                        T.op("dve", lambda: nc.vector.tensor_tensor(out=sq[:], in0=oT[:], in1=oT[:], op=ALU.mult),
                             r=["oT"], w=["sqA"])

                        def tail(pbank, h=h, qb=qb, zT=zT):
                            T.op("pe", lambda: nc.tensor.matmul(bk(pbank)[:, :], lhsT=inv128_bf[:], rhs=sq[:], start=True, stop=True),
                                 r=["sqA", "inv128_bf"], w=[("bank", pbank)])
                            T.op("act", lambda: nc.scalar.activation(out=lnv[:], in_=bk(pbank)[:, :], func=AF.Ln, bias=EPS),
                                 r=[("bank", pbank)], w=["lnvA"])
                            T.op("act", lambda: nc.scalar.activation(out=rsd[:], in_=lnv[:], func=AF.Exp, scale=-0.5),
                                 r=["lnvA"], w=["rsdA"])
                            T.op("dve", lambda: nc.vector.tensor_tensor(out=t1[:], in0=oT[:], in1=rsd[:], op=ALU.mult),
                                 r=["oT", "rsdA"], w=["t1"])
                            ybb = yb[qb % 2]
                            T.op("dve", lambda: nc.vector.scalar_tensor_tensor(out=ybb[:], in0=t1[:], scalar=gda[:, 1, 0:1],
                                                                               in1=zT[:, qb * 512:(qb + 1) * 512],
                                                                               op0=ALU.mult, op1=ALU.mult),
                                 r=["t1", "gda", ("z", h % 2, qb)], w=[("ybA", qb % 2)])
                            T.dma("pool", yT[h * 128:(h + 1) * 128, qb * 512:(qb + 1) * 512], ybb[:],
                                  r=[("ybA", qb % 2)], w=[("yT", h, qb)])

                        assert pend_tail[0] is None
                        pend_tail[0] = tail
                pend_tail[0](0)
                pend_tail[0] = None
                while cast_jobs:
                    cj = cast_jobs.pop(0)
                    T.dma("pool", cj[0], cj[1])
                T.barrier()

            with ExitStack() as scB:
                NSTv[0] = 1
                alloc_wst(scB)
                rowA_t = sb("rowA", [64, S], F32, scB)
                rowG_t = sb("rowG", [64, S], F32, scB)
                rowA = rowA_t[0:4, :]
                rowG = rowG_t[0:4, :]
                rows1 = (rowA_t, rowG_t)
                rowM = sb("rowM", [4, S], F32, scB)
                rowc = sb("rowc", [4, 4, NT], F32, scB)
                rowc1 = sb("rowc1", [64, 2, NT], F32, scB)
                Lm2 = sb("Lm2", [NT + 1, 2, 128], F32, scB)
                Rm = sb("Rm", [NT + 1, NT], F32, scB)
                gb = sb("gb", [4, 3, 16], F32, scB)
                wif = sb("wif", [128, 8, 8], BF16, scB)
                colw = sb("colw", [128, NT], F32, scB)
                colf = sb("colf", [128, NT], F32, scB)
                decb = sb("decb", [128, NT], F32, scB)
                qTm = sb("qTm", [128, S], BF16, scB)
                kTm = sb("kTm", [128, S], BF16, scB)
                wqk2 = [sb("wqk%d" % i, [128, 8, 256], BF16, scB) for i in range(2)]
                wvoz = sb("wvoz", [128, 8, 768], BF16, scB)
                pre = [sb("pre%d" % i, [128, 515], F32, scB) for i in range(2)]
                acc2 = [sb("acc%d" % i, [128, 512], F32, scB) for i in range(2)]
                cw_all = sb("cw", [128, 8, 16], F32, scB)
                cb_all = sb("cb", [128, 8, 16], F32, scB)
                for hh in range(4):
                    for qi in range(2):
                        ch0 = qi * 512 + hh * 128
                        T.dma("pool", cw_all[:, hh * 2 + qi, 0:4], l0_conv_w[:, ch0:ch0 + 128].rearrange("j c -> c j"),
                              w=[("cw", hh, qi)], allow_slow_non_contiguous=True)
                        T.dma("pool", cb_all[:, hh * 2 + qi, 0:1], l0_conv_b[ch0:ch0 + 128].rearrange("(p o) -> p o", o=1),
                              w=[("cb", hh, qi)])
                gml = sb("gml", [128, 256], F32, scB)
                Cst = sb("Cst", [128, 257], F32, scB)
                Cd = sb("Cd", [128, 257], F32, scB)
                Cdb = sb("Cdb", [128, 257], BF16, scB)
                SmT4 = [sb("SmT4_%d" % i, [128, 4, 128], BF16, scB) for i in range(2)]
                ktok4 = [sb("ktok4_%d" % i, [128, 4, 128], BF16, scB) for i in range(2)]
                vw4 = [sb("vw4_%d" % i, [128, 4, 257], BF16, scB) for i in range(2)]
                sg4 = [sb("sg4_%d" % i, [128, 4, 256], BF16, scB) for i in range(2)]
                zs4 = [sb("zs4_%d" % i, [128, 4, 256], BF16, scB) for i in range(2)]
                hmt4 = [sb("hmt4_%d" % i, [128, 4, 256], BF16, scB) for i in range(2)]
                ssq4 = [sb("ssq4_%d" % i, [128, 4], F32, scB) for i in range(2)]
                rs4 = [sb("rs4_%d" % i, [128, 8], F32, scB) for i in range(2)]
                stR = sb("stR", [128, 4, 4], F32, scB)
                junkB = sb("junkB", [128, 256], BF16, scB)
                ytok4 = sb("ytok4", [128, 4, 256], BF16, scB)
                yTb = [sb("yTb%d" % i, [128, 2, 512], BF16, scB) for i in range(2)]

                T.dma("sp", gml[:], l0_ml_head_g.partition_broadcast(128), w=["gml"])
                T.dma("sp", gb[:, 0, 0:1], l0_b_igate.rearrange("(p o) -> p o", o=1), w=["gb0"])
                T.dma("sp", gb[:, 2, 0:1], l0_b_fgate.rearrange("(p o) -> p o", o=1), w=["gb2"])
                T.op("dve", lambda: nc.vector.tensor_scalar(out=gb[:, 1, 0:1], in0=gb[:, 2, 0:1], scalar1=-1.0, scalar2=None,
                                                            op0=ALU.mult), r=["gb2"], w=["gb1"])
                load_w(lambda: wif[:, :, :], [(w0v[:, :, 6144:6152], 0)], "wif")
                for gi, (row, key) in enumerate(((rowA, "rowA"), (rowG, "rowG"))):
                    for tb in range(NB):
                        pb = tb % 2
                        for kc in range(8):
                            T.op("pe", lambda: nc.tensor.matmul(bk(pb)[0:4, :], lhsT=wif[:, kc, gi * 4:gi * 4 + 4],
                                                                rhs=hnT[:, kc, tb * 512:(tb + 1) * 512],
                                                                start=(kc == 0), stop=(kc == 7)),
                                 r=["wif", ("hnT", tb)], w=[("bank", pb)], inc=(kc == 7))
                        T.op("dve", lambda: nc.vector.tensor_copy(out=row[:, tb * 512:(tb + 1) * 512], in_=bk(pb)[0:4, :]),
                             r=[("bank", pb)], w=[key])
                T.op("act", lambda: nc.scalar.activation(out=rowG[:], in_=rowG[:], func=AF.Exp, scale=-1.0, bias=gb[:, 1, 0:1]),
                     r=["rowG", "gb1"], w=["rowG"])
                T.op("act", lambda: nc.scalar.activation(out=rowG[:], in_=rowG[:], func=AF.Ln, bias=1.0),
                     r=["rowG"], w=["rowG"])
                T.op("dve", lambda: nc.vector.tensor_tensor_scan(out=rowG[:], data0=ones_f[0:4, 0:1].to_broadcast([4, S]),
                                                                 data1=rowG[:], initial=0.0, op0=ALU.mult, op1=ALU.add),
                     r=["rowG", "ones_f"], w=["rowG"])
                T.op("dve", lambda: nc.vector.scalar_tensor_tensor(out=rowA[:], in0=rowA[:], scalar=gb[:, 0, 0:1], in1=rowG[:],
                                                                   op0=ALU.add, op1=ALU.add),
                     r=["rowA", "rowG", "gb0"], w=["rowA"])
                T.op("dve", lambda: nc.vector.tensor_tensor_scan(out=rowM[:], data0=rowA[:], data1=rowA[:], initial=0.0,
                                                                 op0=ALU.max, op1=ALU.max),
                     r=["rowA"], w=["rowM"])
                mend = rowM[:].rearrange("p (c t) -> p c t", t=128)[:, :, 127]
                T.op("dve", lambda: nc.vector.tensor_scalar(out=rowc[:, 0, :], in0=mend, scalar1=-1.0, scalar2=None, op0=ALU.mult),
                     r=["rowM"], w=[("rowc", 0)])
                T.op("dve", lambda: nc.vector.memset(rowc[:, 1, 0:1], 0.0), w=[("rowc", 1, 0)])
                if NT > 1:
                    T.op("dve", lambda: nc.vector.tensor_copy(out=rowc[:, 1, 1:NT], in_=mend[:, 0:NT - 1]),
                         r=["rowM"], w=[("rowc", 1, 1)])
                T.op("dve", lambda: nc.vector.tensor_tensor(out=rowc[:, 2, :], in0=rowc[:, 1, :], in1=rowc[:, 0, :], op=ALU.add),
                     r=[("rowc", 0), ("rowc", 1, 0), ("rowc", 1, 1)], w=[("rowc", 2)])

                if dbg:
                    T.dma("sp", dbg_out["dbg_rows"][0], rowA, r=["rowA"])
                    T.dma("sp", dbg_out["dbg_rows"][1], rowG, r=["rowG"])
                    T.dma("sp", dbg_out["dbg_rows"][2], rowM[:], r=["rowM"])
                T.dma("sp", rows_scr[0], rowA, r=["rowA"], w=[("rows_scr", 0)])
                T.dma("sp", rows_scr[1], rowG, r=["rowG"], w=[("rows_scr", 1)])
                T.op("dve", lambda: nc.vector.memset(Lm2[:], 1.0), w=[("Lm2", 0), ("Lm2", 1)])
                T.op("dve", lambda: nc.vector.tensor_copy(out=Rm[0:NT, :], in_=ident[0:NT, 0:NT]), r=["ident"], w=["Rm_id"])
                for h in range(4):
                    for which in range(2):
                        T.dma("sp", Lm2[0:NT, which, :], rows_scr[which, h, :].rearrange("(c t) -> c t", t=128),
                              r=[("rows_scr", which)], w=[("Lm2", which)])
                    T.dma("sp", Rm[NT:NT + 1, :], rowc[h:h + 1, 0, :], r=[("rowc", 0)], w=["Rm_m"])
                    T.dma("sp", rowc1[32:33, 1, :], rowc[h:h + 1, 2, :], r=[("rowc", 2)], w=[("rowc1", 1)])
                    for which, (bnk, dstc, bias) in enumerate(((0, colw, 0.0), (1, colf, 0.5 * float(np.log(128.0))))):
                        T.op("pe", lambda: nc.tensor.matmul(bk(bnk)[:, 0:NT], lhsT=Lm2[:, which, :], rhs=Rm[:, :], start=True, stop=True),
                             r=[("Lm2", which), "Rm_id", "Rm_m"], w=[("bank", bnk)])
                        T.op("act", lambda: nc.scalar.activation(out=dstc[:], in_=bk(bnk)[:, 0:NT], func=AF.Exp, bias=bias),
                             r=[("bank", bnk)], w=[("col", which)])
                    T.op("pe", lambda: nc.tensor.matmul(bk(2)[:, 0:NT], lhsT=ones_f[32:33, 0:128], rhs=rowc1[32:33, 1, :],
                                                        start=True, stop=True),
                         r=[("rowc1", 1), "ones_f"], w=[("bank", 2)])
                    T.op("act", lambda: nc.scalar.activation(out=decb[:], in_=bk(2)[:, 0:NT], func=AF.Exp),
                         r=[("bank", 2)], w=["decb"])
                    if dbg:
                        T.dma("sp", dbg_out["dbg_cols"][h, 0], colw[:], r=[("col", 0)])
                        T.dma("sp", dbg_out["dbg_cols"][h, 1], colf[:], r=[("col", 1)])
                        T.dma("sp", dbg_out["dbg_cols"][h, 2], decb[:], r=["decb"])
                    def load_wqk(hh, defer=False):
                        return load_w(lambda: wqk2[hh % 2][:, :, :], [(w0v[:, :, 4096 + hh * 128:4096 + (hh + 1) * 128], 0),
                                                                      (w0v[:, :, 4608 + hh * 128:4608 + (hh + 1) * 128], 128)],
                                      ("wqk", hh % 2), defer=defer)

                    wqk = wqk2[h % 2]
                    if h == 0:
                        load_wqk(0)
                    cast_voz0 = load_w(lambda: wvoz[:, :, 0:512], [(w0v[:, :, 5120 + h * 256:5120 + (h + 1) * 256], 0),
                                                                   (w0v[:, :, 6152 + h * 256:6152 + (h + 1) * 256], 256)], "wvoz0",
                                       defer=True)
                    steps = [(qi, tb) for qi in range(2) for tb in range(NB)]
                    dsts = (qTm, kTm)

                    def conv_front(n):
                        qi, tb = steps[n]
                        pb = 4 + (n % 2)
                        for kc in range(8):
                            T.op("pe", lambda: nc.tensor.matmul(bk(pb)[:, :], lhsT=wqk[:, kc, qi * 128:(qi + 1) * 128],
                                                                rhs=hnT[:, kc, tb * 512:(tb + 1) * 512],
                                                                start=(kc == 0), stop=(kc == 7)),
                                 r=[("wqk", h % 2), ("hnT", tb)], w=[("bank", pb)], inc=(kc == 7))
                        T.op("act", lambda: nc.scalar.copy(out=pre[n % 2][:, 3:515], in_=bk(pb)[:, :]),
                             r=[("bank", pb)], w=[("pre", n % 2, "m")])

                    def conv_back(n):
                        qi, tb = steps[n]
                        pr = pre[n % 2]
                        if tb == 0:
                            T.op("dve", lambda: nc.vector.memset(pr[:, 0:3], 0.0), w=[("pre", n % 2, "c")])
                        T.op("dve", lambda: nc.vector.tensor_scalar(out=acc2[n % 2][:], in0=pr[:, 3:515], scalar1=cw_all[:, h * 2 + qi, 3:4],
                                                                    scalar2=None, op0=ALU.mult),
                             r=[("pre", n % 2, "m"), ("cw", h, qi)], w=[("acc", n % 2)])
                        for j in (2, 1, 0):
                            T.op("dve", lambda: nc.vector.scalar_tensor_tensor(out=acc2[n % 2][:], in0=pr[:, j:j + 512],
                                                                               scalar=cw_all[:, h * 2 + qi, j:j + 1], in1=acc2[n % 2][:],
                                                                               op0=ALU.mult, op1=ALU.add),
                                 r=[("pre", n % 2, "m"), ("pre", n % 2, "c"), ("cw", h, qi), ("acc", n % 2)], w=[("acc", n % 2)])
                        T.op("act", lambda: nc.scalar.activation(out=dsts[qi][:, tb * 512:(tb + 1) * 512], in_=acc2[n % 2][:],
                                                                 func=AF.Silu, bias=cb_all[:, h * 2 + qi, 0:1]),
                             r=[("acc", n % 2), ("cb", h, qi)], w=[("qk", qi, tb)])
                        if n + 1 < len(steps) and steps[n + 1][1] > 0:
                            T.op("dve", lambda: nc.vector.tensor_copy(out=pre[(n + 1) % 2][:, 0:3], in_=pr[:, 512:515]),
                                 r=[("pre", n % 2, "m")], w=[("pre", (n + 1) % 2, "c")])

                    conv_front(0)
                    cast_voz1 = None
                    for n in range(len(steps)):
                        if n + 1 < len(steps):
                            conv_front(n + 1)
                        conv_back(n)
                        if n == len(steps) // 2 - 1:
                            cast_voz0()
                            cast_voz1 = load_w(lambda: wvoz[:, :, 512:768], [(w0v[:, :, 7176 + h * 256:7176 + (h + 1) * 256], 0)],
                                               "wvoz1", defer=True)
                    cast_voz1()
                    T.op("dve", lambda: nc.vector.memset(Cst[:], 0.0), w=["Cst"])

                    import os
                    _sk = os.environ.get("KSKIP", "")

                    def P_pre(tb):
                        par = tb % 2
                        for ci in range(4):
                            c = tb * 4 + ci
                            T.op("pe", lambda: nc.tensor.transpose(bkb(7)[:, 512 + ci * 128:512 + (ci + 1) * 128],
                                                                   kTm[:, c * 128:(c + 1) * 128], ident[:]),
                                 r=[("qk", 1, tb), "ident"], w=[("bank", 7)], inc=(ci == 3))
                        T.op("act", lambda: nc.scalar.copy(out=ktok4[par][:], in_=bkb(7)[:, 512:1024].rearrange("p (a b) -> p a b", a=4)),
                             r=[("bank", 7)], w=[("ktok", par, ci) for ci in range(4)])
                        if "v" not in _sk:
                          T.op("pool", lambda: nc.gpsimd.tensor_copy(out=vw4[par][:, :, 256], in_=colw[:, tb * 4:(tb + 1) * 4]),
                             r=[("col", 0)], w=[("vw1", par)])

                    def P_chunk(tb, ci):
                        par = tb % 2
                        if True:
                            c = tb * 4 + ci
                            cs = slice(c * 128, (c + 1) * 128)
                            xb = ci % 2
                            T.op("pe", lambda: nc.tensor.matmul(bk(xb)[:, 0:128], lhsT=kTm[:, cs], rhs=qTm[:, cs], start=True, stop=True),
                                 r=[("qk", 0, tb), ("qk", 1, tb)], w=[("bank", xb)])
                            for kc in range(8):
                                T.op("pe", lambda: nc.tensor.matmul(bk(xb)[:, 128:384], lhsT=hnT[:, kc, cs], rhs=wvoz[:, kc, 0:256],
                                                                    start=(kc == 0), stop=(kc == 7)),
                                     r=[("hnT", tb), "wvoz0"], w=[("bank", xb)], inc=(kc == 7))
                            hf = ci % 2
                            ob, zb = 2, 3 + ci // 2
                            for kc in range(8):
                                T.op("pe", lambda: nc.tensor.matmul(bk(ob)[:, hf * 256:(hf + 1) * 256], lhsT=hnT[:, kc, cs],
                                                                    rhs=wvoz[:, kc, 256:512], start=(kc == 0), stop=(kc == 7)),
                                     r=[("hnT", tb), "wvoz0"], w=[("bank", 2)], inc=(kc == 7))
                            for kc in range(8):
                                T.op("pe", lambda: nc.tensor.matmul(bk(zb)[:, hf * 256:(hf + 1) * 256], lhsT=hnT[:, kc, cs],
                                                                    rhs=wvoz[:, kc, 512:768], start=(kc == 0), stop=(kc == 7)),
                                     r=[("hnT", tb), "wvoz1"], w=[("bank", 3 + ci // 2)], inc=(kc == 7))
                            T.op("dve", lambda: nc.vector.tensor_tensor(out=SmT4[par][:, ci, :], in0=bk(xb)[:, 0:128], in1=tri[:], op=ALU.mult),
                                 r=[("bank", xb), "tri"], w=[("SmT", par, ci)])
                            T.op("dve", lambda: nc.vector.tensor_scalar(out=vw4[par][:, ci, 0:256], in0=bk(xb)[:, 128:384],
                                                                        scalar1=colw[:, c:c + 1], scalar2=None, op0=ALU.mult),
                                 r=[("bank", xb), ("col", 0)], w=[("vw", par, ci)])
                            if hf == 1 and "s" not in _sk:
                                T.op("act", lambda: nc.scalar.activation(out=sg4[par][:, ci - 1:ci + 1, :],
                                                                         in_=bk(ob)[:, :].rearrange("p (a b) -> p a b", a=2),
                                                                         func=AF.Sigmoid),
                                     r=[("bank", 2)], w=[("sg", par, ci // 2)])

                    def P_post(tb):
                        par = tb % 2
                        for pr_ in range(2 if "z" not in _sk else 0):
                            T.op("act", lambda: nc.scalar.activation(out=zs4[par][:, 2 * pr_:2 * pr_ + 2, :],
                                                                     in_=bk(3 + pr_)[:, :].rearrange("p (a b) -> p a b", a=2),
                                                                     func=AF.Silu),
                                 r=[("bank", 3 + pr_)], w=[("zs", par, pr_)])
                        if "g" not in _sk:
                          T.op("pool", lambda: nc.gpsimd.tensor_tensor(out=zs4[par][:], in0=zs4[par][:],
                                                                     in1=gml[:, None, :].to_broadcast([128, 4, 256]), op=ALU.mult),
                             r=[("zs", par, 0), ("zs", par, 1), "gml"], w=[("zs", par, 0), ("zs", par, 1)])

                    def rB(tb, ci):
                        par = tb % 2
                        c = tb * 4 + ci
                        T.op("dve", lambda: nc.vector.scalar_tensor_tensor(out=stR[:, ci, 0:1], in0=bk(5)[:, 256:257], scalar=-1.0,
                                                                           in1=colf[:, c:c + 1], op0=ALU.mult, op1=ALU.max),
                             r=[("bank", 5), ("col", 1)], w=[("stR", ci, 0)])
                        T.op("dve", lambda: nc.vector.tensor_tensor(out=stR[:, ci, 1:2], in0=stR[:, ci, 0:1], in1=bk(5)[:, 256:257],
                                                                    op=ALU.max),
                             r=[("stR", ci, 0), ("bank", 5)], w=[("stR", ci, 1)])
                        T.op("dve", lambda: nc.vector.reciprocal(out=stR[:, ci, 2:3], in_=stR[:, ci, 1:2]),
                             r=[("stR", ci, 1)], w=[("stR", ci, 2)])
                        T.op("dve", lambda: nc.vector.scalar_tensor_tensor(out=hmt4[par][:, ci, :], in0=bk(5)[:, 0:256],
                                                                           scalar=stR[:, ci, 2:3], in1=sg4[par][:, ci, :],
                                                                           op0=ALU.mult, op1=ALU.mult),
                             r=[("bank", 5), ("stR", ci, 2), ("sg", par, ci // 2)], w=[("hmt", par, ci)])
                        T.op("act", lambda: nc.scalar.activation(out=junkB[:], in_=hmt4[par][:, ci, :], func=AF.Square,
                                                                 accum_out=ssq4[par][:, ci:ci + 1]),
                             r=[("hmt", par, ci)], w=["junkB", ("ssq", par, ci)])

                    def R_chunk(tb, ci):
                        par = tb % 2
                        c = tb * 4 + ci
                        cs = slice(c * 128, (c + 1) * 128)
                        T.op("dve", lambda: nc.vector.tensor_scalar(out=Cd[:], in0=Cst[:], scalar1=decb[:, c:c + 1], scalar2=None,
                                                                    op0=ALU.mult), r=["Cst", "decb"], w=["Cd"])
                        T.op("dve", lambda: nc.vector.tensor_scalar(out=Cdb[:], in0=Cst[:], scalar1=decb[:, c:c + 1], scalar2=None,
                                                                    op0=ALU.mult), r=["Cst", "decb"], w=["Cdb"])
                        T.op("pe", lambda: nc.tensor.matmul(bk(6)[:, 0:257], lhsT=ktok4[par][:, ci, :], rhs=vw4[par][:, ci, :],
                                                            start=True, stop=True),
                             r=[("ktok", par, ci), ("vw", par, ci), ("vw1", par)], w=[("bank", 6)])
                        T.op("dve", lambda: nc.vector.tensor_tensor(out=Cst[:], in0=bk(6)[:, 0:257], in1=Cd[:], op=ALU.add),
                             r=[("bank", 6), "Cd"], w=["Cst"])
                        if ci > 0:
                            rB(tb, ci - 1)
                        T.op("pe", lambda: nc.tensor.matmul(bk(5)[:, 0:257], lhsT=SmT4[par][:, ci, :], rhs=vw4[par][:, ci, :],
                                                            start=True, stop=False),
                             r=[("SmT", par, ci), ("vw", par, ci), ("vw1", par)], w=[("bank", 5)], inc=False)
                        T.op("pe", lambda: nc.tensor.matmul(bk(5)[:, 0:257], lhsT=qTm[:, cs], rhs=Cdb[:], start=False, stop=True),
                             r=[("qk", 0, tb), "Cdb"], w=[("bank", 5)])

                    def N_pre(tb):
                        par = tb % 2
                        T.op("act", lambda: nc.scalar.activation(out=rs4[par][:, 0:4], in_=ssq4[par][:, 0:4], func=AF.Ln,
                                                                 scale=1.0 / 256, bias=EPS),
                             r=[("ssq", par, ci) for ci in range(4)], w=[("rs", par, 0)])
                        T.op("act", lambda: nc.scalar.activation(out=rs4[par][:, 4:8], in_=rs4[par][:, 0:4], func=AF.Exp, scale=-0.5),
                             r=[("rs", par, 0)], w=[("rs", par, 1)])
                        for ci in range(4):
                            T.op("dve", lambda: nc.vector.scalar_tensor_tensor(out=ytok4[:, ci, :], in0=hmt4[par][:, ci, :],
                                                                               scalar=rs4[par][:, 4 + ci:5 + ci], in1=zs4[par][:, ci, :],
                                                                               op0=ALU.mult, op1=ALU.mult),
                                 r=[("hmt", par, ci), ("rs", par, 1), ("zs", par, ci // 2)], w=[("ytok", ci)])

                    def N_T(tb, hf):
                        par = tb % 2
                        ytb = yTb[par]
                        for ci in range(4):
                            T.op("pe", lambda: nc.tensor.transpose(bkb(7)[:, ci * 128:(ci + 1) * 128],
                                                                   ytok4[:, ci, hf * 128:(hf + 1) * 128], ident[:]),
                                 r=[("ytok", ci), "ident"], w=[("bank", 7)], inc=(ci == 3))
                        T.op("act", lambda: nc.scalar.copy(out=ytb[:, hf, :], in_=bkb(7)[:, 0:512]),
                             r=[("bank", 7)], w=[("yTb", par)])

                    def N_out(tb):
                        par = tb % 2
                        r0 = 1024 + h * 256
                        T.dma("pool", yT[r0:r0 + 256, tb * 512:(tb + 1) * 512].rearrange("(a p) t -> p a t", p=128), yTb[par][:],
                              r=[("yTb", par)], w=[("yT", 8 + h, tb)])

                    P_pre(0)
                    for ci in range(4):
                        P_chunk(0, ci)
                    P_post(0)
                    cast_wqk_next = None
                    for tb in range(NB):
                        if h + 1 < 4 and tb == min(2, NB - 1):
                            cast_wqk_next = load_wqk(h + 1, defer=True)
                        elif cast_wqk_next is not None:
                            cast_wqk_next()
                            cast_wqk_next = None
                        if tb >= 1:
                            N_pre(tb - 1)
                        if tb + 1 < NB:
                            P_pre(tb + 1)
                        for ci in range(4):
                            if tb + 1 < NB:
                                P_chunk(tb + 1, ci)
                            R_chunk(tb, ci)
                            if tb >= 1 and ci == 1:
                                N_T(tb - 1, 0)
                            if tb >= 1 and ci == 3:
                                N_T(tb - 1, 1)
                                N_out(tb - 1)
                        rB(tb, 3)
                        if tb + 1 < NB:
                            P_post(tb + 1)
                    if cast_wqk_next is not None:
                        cast_wqk_next()
                        cast_wqk_next = None
                    N_pre(NB - 1)
                    N_T(NB - 1, 0)
                    N_T(NB - 1, 1)
                    N_out(NB - 1)
                T.barrier()
        h1dst = out
        scCD = es.enter_context(ExitStack())
        w1 = sb("w1", [128, 8, ODD_IN], BF16, scCD)
        rs1 = sb("rs1", [128, NT], F32, scCD)
        with ExitStack() as scC:
            wo0 = sb("wo0", [128, 16, D], BF16, scC)
            gp0 = sb("gp0", [128, D], F32, scC)
            T.dma("sp", gp0[:], l0_post_g.partition_broadcast(128), w=["gp0"])
            for kg in range(2):
                for cg in range(2):
                    T.dma("sp", wo0[:, kg * 8:(kg + 1) * 8, cg * 512:(cg + 1) * 512],
                          wo0bv[:, kg * 8:(kg + 1) * 8, cg * 512:(cg + 1) * 512], w=[("wo0", kg, cg)])
            wo0_keys = [("wo0", a, b) for a in range(2) for b in range(2)]
            w1_loads = list(range(ODD_IN // 512))
            yblk = [sb("yblk%d" % i, [128, 16, 512], BF16, scC) for i in range(2)]
            xt = [sb("xtC%d" % i, [128, D], F32, scC) for i in range(2)]
            junkC = sb("junkC", [128, D], BF16, scC)
            stC = [sb("stC%d" % i, [128, 8], F32, scC) for i in range(2)]
            h1 = [sb("h1_%d" % i, [128, D], F32, scC) for i in range(2)]
            for i in range(NT):
                tb, b = i // 4, i % 2
                yb_ = yblk[tb % 2]
                st_ = stC[b]
                if i % 4 == 0:
                    for half in range(2):
                        T.dma("sp", yb_[:, half * 8:(half + 1) * 8, :],
                              yT[half * 1024:(half + 1) * 1024, tb * 512:(tb + 1) * 512].rearrange("(a p) t -> p a t", p=128),
                              r=[("yT", hh, tb) for hh in range(half * 8, half * 8 + (8 if half == 0 else 4))],
                              w=[("yblk", tb % 2, half)])
                T.dma("sp", xt[b][:], x[i * 128:(i + 1) * 128, :], w=[("xtC", b)])
                if w1_loads and (i % 2 == 1 or NT - i <= len(w1_loads)):
                    cgp = w1_loads.pop(0)
                    T.dma("sp", w1[:, :, cgp * 512:(cgp + 1) * 512], w1bv[:, :, cgp * 512:(cgp + 1) * 512], w=[("w1", cgp)])
                ts_ = slice((i % 4) * 128, (i % 4 + 1) * 128)
                pbs = (2 * b, 2 * b + 1)
                for cg in range(2):
                    for kc in range(16):
                        T.op("pe", lambda: nc.tensor.matmul(bk(pbs[cg])[:, :], lhsT=yb_[:, kc, ts_], rhs=wo0[:, kc, cg * 512:(cg + 1) * 512],
                                                            start=(kc == 0), stop=(kc == 15)),
                             r=[("yblk", tb % 2, 0), ("yblk", tb % 2, 1)] + wo0_keys, w=[("bank", pbs[cg])], inc=(kc == 15))
                for cg in range(2):
                    T.op("act", lambda: nc.scalar.activation(out=junkC[:, cg * 512:(cg + 1) * 512], in_=bk(pbs[cg])[:, :], func=AF.Square,
                                                             accum_out=st_[:, cg:cg + 1]),
                         r=[("bank", pbs[cg])], w=[("junkC", cg), ("stC", b, cg)])
                T.op("dve", lambda: nc.vector.tensor_tensor(out=st_[:, 2:3], in0=st_[:, 0:1], in1=st_[:, 1:2], op=ALU.add),
                     r=[("stC", b, 0), ("stC", b, 1)], w=[("stC", b, 2)])
                rstd_from_ssq(st_[:, 2:3], st_[:, 3:4], st_[:, 4:5], D, [("stC", b, 2)], ("stC", b, 3), ("stC", b, 4))
                for cg in range(2):
                    T.op("dve", lambda: nc.vector.scalar_tensor_tensor(out=h1[b][:, cg * 512:(cg + 1) * 512], in0=bk(pbs[cg])[:, :],
                                                                       scalar=st_[:, 4:5], in1=gp0[:, cg * 512:(cg + 1) * 512],
                                                                       op0=ALU.mult, op1=ALU.mult),
                         r=[("bank", pbs[cg]), ("stC", b, 4), "gp0"], w=[("h1", b, cg)])
                T.op("dve", lambda: nc.vector.tensor_tensor(out=h1[b][:], in0=h1[b][:], in1=xt[b][:], op=ALU.add),
                     r=[("h1", b, 0), ("h1", b, 1), ("xtC", b)], w=[("h1", b, 0), ("h1", b, 1)])
                T.op("act", lambda: nc.scalar.activation(out=junkC[:], in_=h1[b][:], func=AF.Square, accum_out=st_[:, 5:6]),
                     r=[("h1", b, 0), ("h1", b, 1)], w=[("junkC", 0), ("junkC", 1), ("stC", b, 5)])
                rstd_from_ssq(st_[:, 5:6], st_[:, 6:7], rs1[:, i:i + 1], D, [("stC", b, 5)], ("stC", b, 6), ("rs1", i))
                T.dma("pool", h1dst[i * 128:(i + 1) * 128, :], h1[b][:], r=[("h1", b, 0), ("h1", b, 1)], w=[("h1d", i)])
                if dbg:
                    T.dma("pool", dbg_out["dbg_h1"][i * 128:(i + 1) * 128, :], h1[b][:], r=[("h1", b, 0), ("h1", b, 1)], w=[("dbgh1", i)])
            while w1_loads:
                cgp = w1_loads.pop(0)
                T.dma("sp", w1[:, :, cgp * 512:(cgp + 1) * 512], w1bv[:, :, cgp * 512:(cgp + 1) * 512], w=[("w1", cgp)])
            if dbg:
                T.barrier()
                T.dma("sp", dbg_out["dbg_yT"][:, :], yT[:, :])
            T.barrier()
        with ExitStack() as scD:
            wo1 = sb("wo1", [128, 16, D], BF16, scD)
            for kg in range(2):
                for cg in range(2):
                    T.dma("sp", wo1[:, kg * 8:(kg + 1) * 8, cg * 512:(cg + 1) * 512],
                          wo1bv[:, kg * 8:(kg + 1) * 8, cg * 512:(cg + 1) * 512], w=[("wo1", kg, cg)])
            wo1_keys = [("wo1", a, b) for a in range(2) for b in range(2)]
            gp1 = sb("gp1", [128, D], F32, scD)
            gq1 = sb("gq1", [128, D], F32, scD)
            gsg = sb("gsg", [128, 2048], F32, scD)
            bsp = sb("bsp", [128, 8], F32, scD)
            wmT = sb("wmT", [128, 8, 128], BF16, scD)
            T.dma("sp", gp1[:], l1_pre_g.partition_broadcast(128), w=["gp1"])
            T.dma("sp", gq1[:], l1_post_g.partition_broadcast(128), w=["gq1"])
            T.dma("sp", gsg[:], l1_sg_norm_g.partition_broadcast(128), w=["gsg"])
            T.dma("sp", bsp[:], l1_b_spatial.rearrange("g t -> t g"), w=["bsp"], allow_slow_non_contiguous=True)
            with ExitStack() as scS:
                wsp_f = sb("wsp_f", [128, 8, 128], F32, scS)
                wsp_b = sb("wsp_b", [128, 8, 128], BF16, scS)
                T.dma("sp", wsp_f[:], l1_w_spatial.rearrange("g t s -> t g s"), w=["wsp_f"])
                T.op("dve", lambda: nc.vector.tensor_copy(out=wsp_b[:], in_=wsp_f[:]), r=["wsp_f"], w=["wsp_b"])
                for g in range(8):
                    T.op("pe", lambda: nc.tensor.transpose(bkb(7)[:, g * 128:(g + 1) * 128], wsp_b[:, g, :], ident[:]),
                         r=["wsp_b", "ident"], w=[("bank", 7)], inc=(g == 7))
                T.op("dve", lambda: nc.vector.tensor_tensor(out=wmT[:], in0=bkb(7).rearrange("p (g t) -> p g t", g=8),
                                                            in1=tri[:, None, :].to_broadcast([128, 8, 128]), op=ALU.mult),
                     r=[("bank", 7), "tri"], w=["wmT"])
                T.barrier()

            ht = [sb("ht%d" % i, [128, D], F32, scD) for i in range(3)]
            hn1 = sb("hn1", [128, D], BF16, scD)
            hn1T = [sb("hn1T%d" % i, [128, 8, 128], BF16, scD) for i in range(2)]
            junkD = sb("junkD", [128, D], BF16, scD)
            stD = [sb("stD%d" % i, [128, 12], F32, scD) for i in range(2)]
            gu = [sb("gu%d" % i, [128, 2048], BF16, scD) for i in range(2)]
            gv = [sb("gv%d" % i, [128, 2048], BF16, scD) for i in range(2)]
            sz = [sb("sz%d" % i, [128, 2048], BF16, scD) for i in range(2)]
            vn = sb("vn", [128, 2048], BF16, scD)
            y1 = sb("y1", [128, 2048], BF16, scD)
            y1T = sb("y1T", [128, 16, 128], BF16, scD)
            ymx = sb("ymx", [128, D], F32, scD)
            guk = lambda p: [("gu", p, k) for k in range(4)]
            gvk = lambda p: [("gv", p, k) for k in range(4)]
            szk = lambda p: [("sz", p, k) for k in range(4)]
            y1k = [("y1", g) for g in range(8)]

            def stA(i):
                p = i % 2
                T.dma("sp", ht[i % 3][:], h1dst[i * 128:(i + 1) * 128, :], r=[("h1d", i)], w=[("ht", i % 3)])
                T.op("dve", lambda: nc.vector.scalar_tensor_tensor(out=hn1[:], in0=ht[i % 3][:], scalar=rs1[:, i:i + 1], in1=gp1[:],
                                                                   op0=ALU.mult, op1=ALU.mult),
                     r=[("ht", i % 3), "gp1"], w=["hn1"])
                for kc in range(8):
                    T.op("pe", lambda: nc.tensor.transpose(bkb(7)[:, kc * 128:(kc + 1) * 128], hn1[:, kc * 128:(kc + 1) * 128], ident[:]),
                         r=["hn1", "ident"], w=[("bank", 7)], inc=(kc == 7))
                T.op("dve", lambda: nc.vector.tensor_copy(out=hn1T[p][:], in_=bkb(7).rearrange("p (k t) -> p k t", k=8)),
                     r=[("bank", 7)], w=[("hn1T", p)])

            def stB(i, groups):
                p = i % 2
                for cgp in groups:
                    pb = (0, 1, 3, 4)[cgp % 4]
                    for kc in range(8):
                        T.op("pe", lambda: nc.tensor.matmul(bk(pb)[:, :], lhsT=hn1T[p][:, kc, :], rhs=w1[:, kc, cgp * 512:(cgp + 1) * 512],
                                                            start=(kc == 0), stop=(kc == 7)),
                             r=[("hn1T", p)], w=[("bank", pb)], inc=(kc == 7))
                    sec, cc = cgp // 4, (cgp % 4) * 512
                    dst, fn, key = ((gu[p], AF.Gelu_apprx_tanh, "gu"), (gv[p], AF.Gelu_apprx_tanh, "gv"), (sz[p], AF.Silu, "sz"))[sec]
                    T.op("act", lambda: nc.scalar.activation(out=dst[:, cc:cc + 512], in_=bk(pb)[:, :], func=fn),
                         r=[("bank", pb)], w=[(key, p, cgp % 4)])

            def stC1a(i):
                p = i % 2
                T.op("act", lambda: nc.scalar.activation(out=vn[:], in_=gv[p][:], func=AF.Square, accum_out=stD[p][:, 3:4]),
                     r=gvk(p), w=["vn", ("stD", p, 3)])
                rstd_from_ssq(stD[p][:, 3:4], stD[p][:, 4:5], stD[p][:, 5:6], 2048, [("stD", p, 3)], ("stD", p, 4), ("stD", p, 5))

            def stC1b(i):
                p = i % 2
                T.op("dve", lambda: nc.vector.scalar_tensor_tensor(out=vn[:], in0=gv[p][:], scalar=stD[p][:, 5:6], in1=gsg[:],
                                                                   op0=ALU.mult, op1=ALU.mult),
                     r=gvk(p) + [("stD", p, 5), "gsg"], w=["vn"])
                T.op("pool", lambda: nc.gpsimd.tensor_tensor(out=gu[p][:], in0=gu[p][:], in1=sz[p][:], op=ALU.mult),
                     r=guk(p) + szk(p), w=guk(p))

            def stC2(i):
                p = i % 2
                for half in range(2):
                    for gg in range(4):
                        g = half * 4 + gg
                        pb = 5 + gg // 2
                        T.op("pe", lambda: nc.tensor.matmul(bk(pb)[:, (gg % 2) * 256:(gg % 2 + 1) * 256], lhsT=wmT[:, g, :],
                                                            rhs=vn[:, g * 256:(g + 1) * 256], start=True, stop=True),
                             r=["wmT", "vn"], w=[("bank", pb)], inc=(gg % 2 == 1))
                    for gg in range(4):
                        g = half * 4 + gg
                        pb = 5 + gg // 2
                        T.op("dve", lambda: nc.vector.scalar_tensor_tensor(out=y1[:, g * 256:(g + 1) * 256],
                                                                           in0=bk(pb)[:, (gg % 2) * 256:(gg % 2 + 1) * 256],
                                                                           scalar=bsp[:, g:g + 1], in1=gu[p][:, g * 256:(g + 1) * 256],
                                                                           op0=ALU.add, op1=ALU.mult),
                             r=[("bank", pb), "bsp"] + guk(p), w=[("y1", g)])

            def stC3(i):
                for half in range(2):
                    tbk = 2 if half == 0 else 7
                    for kk in range(8):
                        kc = half * 8 + kk
                        T.op("pe", lambda: nc.tensor.transpose(bkb(tbk)[:, kk * 128:(kk + 1) * 128], y1[:, kc * 128:(kc + 1) * 128], ident[:]),
                             r=y1k + ["ident"], w=[("bank", tbk)], inc=(kk == 7))
                    if half == 0:
                        T.op("dve", lambda: nc.vector.tensor_copy(out=y1T[:, 0:8, :], in_=bkb(2).rearrange("p (k t) -> p k t", k=8)),
                             r=[("bank", 2)], w=[("y1T", 0)])
                    else:
                        T.op("act", lambda: nc.scalar.copy(out=y1T[:, 8:16, :], in_=bkb(7).rearrange("p (k t) -> p k t", k=8)),
                             r=[("bank", 7)], w=[("y1T", 1)])

            def stD_(i):
                p = i % 2
                for cg in range(2):
                    pb = 5 + cg
                    for kc in range(16):
                        T.op("pe", lambda: nc.tensor.matmul(bk(pb)[:, :], lhsT=y1T[:, kc, :], rhs=wo1[:, kc, cg * 512:(cg + 1) * 512],
                                                            start=(kc == 0), stop=(kc == 15)),
                             r=[("y1T", 0), ("y1T", 1)], w=[("bank", pb)], inc=(kc == 15))
                for cg in range(2):
                    T.op("act", lambda: nc.scalar.copy(out=ymx[:, cg * 512:(cg + 1) * 512], in_=bk(5 + cg)[:, :]),
                         r=[("bank", 5 + cg)], w=[("ymx", cg)])
                T.op("act", lambda: nc.scalar.activation(out=junkD[:, 0:D], in_=ymx[:], func=AF.Square, accum_out=stD[p][:, 6:7]),
                     r=[("ymx", 0), ("ymx", 1)], w=["junkD", ("stD", p, 6)])
                rstd_from_ssq(stD[p][:, 6:7], stD[p][:, 7:8], stD[p][:, 8:9], D, [("stD", p, 6)], ("stD", p, 7), ("stD", p, 8))
                T.op("dve", lambda: nc.vector.scalar_tensor_tensor(out=ymx[:], in0=ymx[:], scalar=stD[p][:, 8:9], in1=gq1[:],
                                                                   op0=ALU.mult, op1=ALU.mult),
                     r=[("ymx", 0), ("ymx", 1), ("stD", p, 8), "gq1"], w=[("ymx", 0), ("ymx", 1)])
                T.op("dve", lambda: nc.vector.tensor_tensor(out=ht[i % 3][:], in0=ymx[:], in1=ht[i % 3][:], op=ALU.add),
                     r=[("ymx", 0), ("ymx", 1), ("ht", i % 3)], w=[("ht", i % 3)])
                T.dma("pool", out[i * 128:(i + 1) * 128, :], ht[i % 3][:], r=[("ht", i % 3), ("h1d", i)], w=[("outd", i)])

            stA(0)
            for i in range(NT):
                stB(i, range(0, 3))
                if i >= 1:
                    stC1b(i - 1)
                stB(i, range(3, 5))
                if i + 1 < NT:
                    stA(i + 1)
                if i >= 1:
                    stC2(i - 1)
                stB(i, range(5, 8))
                if i >= 1:
                    stC3(i - 1)
                stB(i, range(8, 10))
                if i >= 1:
                    stD_(i - 1)
                stC1a(i)
                stB(i, range(10, 12))
            stC1b(NT - 1)
            stC2(NT - 1)
            stC3(NT - 1)
            stD_(NT - 1)
            T.finish("sp")
            T.barrier()
    return nc


_NC_CACHE = {}


def _consts():
    ident = np.eye(128, dtype=np.float32).astype(ml_dtypes.bfloat16)
    k = np.arange(128)
    tri = (k[:, None] <= k[None, :]).astype(np.float32).astype(ml_dtypes.bfloat16)
    return {"c_ident": ident, "c_tri": tri}


def kernel(**inputs):
    x = np.ascontiguousarray(inputs["x"], dtype=np.float32)
    B, S, _ = x.shape
    if S not in _NC_CACHE:
        _NC_CACHE[S] = build(S)
    nc = _NC_CACHE[S]
    shared = {k: np.ascontiguousarray(v, dtype=np.float32) for k, v in inputs.items() if k != "x"}
    shared.update(_consts())
    in_maps = []
    for b in range(B):
        m = dict(shared)
        m["x"] = x[b]
        in_maps.append(m)
    res = run_bass_kernel_spmd(nc, in_maps, core_ids=list(range(B)))
    return np.stack([r["out"] for r in res.results], axis=0).astype(np.float32)
```

```python
import numpy as np
import ml_dtypes
from contextlib import ExitStack
import concourse.bass as bass
import concourse.mybir as mybir
from concourse.bass_utils import run_bass_kernel_spmd

F32, BF16 = mybir.dt.float32, mybir.dt.bfloat16
AF = mybir.ActivationFunctionType
ALU = mybir.AluOpType

D = 1024
EVEN_IN = 8200
ODD_IN = 6144
EPS = 1e-6
NDS = 8


class Trk:
    def __init__(self, nc, es):
        self.nc = nc
        self.E = {"pe": nc.tensor, "act": nc.scalar, "dve": nc.vector, "pool": nc.gpsimd, "sp": nc.sync}
        self.csem, self.ccnt = {}, {}
        for e in ("pe", "act", "dve", "pool"):
            self.csem[e] = es.enter_context(nc.semaphore("c_" + e))
            self.ccnt[e] = 0
        self.dq = {}
        for q in ("sp", "pool", "act"):
            self.dq[q] = dict(sems=[es.enter_context(nc.semaphore("d_%s%d" % (q, i))) for i in range(NDS)],
                              cnt=[0] * NDS, nxt=0)
        self.waited = {e: {} for e in self.E}
        self.res = {}
        self.pend_r, self.pend_w = [], []

    def _deps(self, r, w):
        deps = []
        for k in r:
            ent = self.res.get(k)
            if ent and ent[0]:
                deps.append(ent[0])
        for k in w:
            ent = self.res.get(k)
            if ent:
                if ent[0]:
                    deps.append(ent[0])
                deps.extend(ent[1])
        return deps

    def _wait(self, e, deps):
        best = {}
        for (key, sem, val) in deps:
            if key not in best or best[key][1] < val:
                best[key] = (sem, val)
        for key, (sem, val) in best.items():
            if e == "pe" and key == "c_pe":
                continue
            if self.waited[e].get(key, 0) >= val:
                continue
            self.E[e].wait_ge(sem, val)
            self.waited[e][key] = val

    def _reg(self, ev, r, w):
        for k in r:
            ent = self.res.setdefault(k, [None, []])
            ent[1].append(ev)
        for k in w:
            self.res[k] = [ev, []]

    def op(self, e, emit, r=(), w=(), inc=True):
        r, w = list(r), list(w)
        self._wait(e, self._deps(r, w))
        ins = emit()
        if e == "pe" and not inc:
            self.pend_r += r
            self.pend_w += w
            return ins
        self.ccnt[e] += 1
        ins.then_inc(self.csem[e], 1)
        ev = ("c_" + e, self.csem[e], self.ccnt[e])
        if e == "pe":
            r = r + self.pend_r
            w = w + self.pend_w
            self.pend_r, self.pend_w = [], []
        self._reg(ev, r, w)
        return ins

    def dma(self, q, out, in_, r=(), w=(), **kw):
        r, w = list(r), list(w)
        Q = self.dq[q]
        k = Q["nxt"]
        Q["nxt"] = (k + 1) % NDS
        key = "d_%s%d" % (q, k)
        deps = self._deps(r, w)
        if Q["cnt"][k] > 0:
            deps.append((key, Q["sems"][k], Q["cnt"][k]))
        self._wait(q, deps)
        ins = self.E[q].dma_start(out=out, in_=in_, **kw)
        Q["cnt"][k] += 16
        ins.then_inc(Q["sems"][k], 16)
        self._reg((key, Q["sems"][k], Q["cnt"][k]), r, w)
        return ins

    def barrier(self):
        assert not self.pend_r and not self.pend_w
        evs = [("c_" + e, self.csem[e], self.ccnt[e]) for e in self.csem if self.ccnt[e] > 0]
        for q, Q in self.dq.items():
            for k in range(NDS):
                if Q["cnt"][k] > 0:
                    evs.append(("d_%s%d" % (q, k), Q["sems"][k], Q["cnt"][k]))
        for e in self.E:
            self._wait(e, evs)
        self.res = {}

    def finish(self, e="sp"):
        evs = []
        for q, Q in self.dq.items():
            for k in range(NDS):
                if Q["cnt"][k] > 0:
                    evs.append(("d_%s%d" % (q, k), Q["sems"][k], Q["cnt"][k]))
        self._wait(e, evs)


def build(S=4096, dbg=False):
    assert S % 512 == 0
    NT = S // 128
    NB = S // 512
    nc = bass.Bass("TRN2", target_bir_lowering=False)

    def din(name, shape, dt=F32):
        return nc.dram_tensor(name, list(shape), dt, kind="ExternalInput").ap()

    x = din("x", [S, D])
    l0_pre_g = din("l0_pre_g", [D]); l0_w_in = din("l0_w_in", [D, EVEN_IN])
    l0_b_igate = din("l0_b_igate", [4]); l0_b_fgate = din("l0_b_fgate", [4])
    l0_conv_w = din("l0_conv_w", [4, 1024]); l0_conv_b = din("l0_conv_b", [1024])
    lq1 = din("l0_lambda_q1", [64]); lk1 = din("l0_lambda_k1", [64])
    lq2 = din("l0_lambda_q2", [64]); lk2 = din("l0_lambda_k2", [64])
    l0_da_head_g = din("l0_da_head_g", [128]); l0_ml_head_g = din("l0_ml_head_g", [256])
    l0_w_out = din("l0_w_out", [2048, D]); l0_post_g = din("l0_post_g", [D])
    l1_pre_g = din("l1_pre_g", [D]); l1_w_in = din("l1_w_in", [D, ODD_IN])
    l1_sg_norm_g = din("l1_sg_norm_g", [2048]); l1_w_spatial = din("l1_w_spatial", [8, 128, 128])
    l1_b_spatial = din("l1_b_spatial", [8, 128]); l1_w_out = din("l1_w_out", [2048, D])
    l1_post_g = din("l1_post_g", [D])
    c_ident = din("c_ident", [128, 128], BF16)
    c_tri = din("c_tri", [128, 128], BF16)
    out = nc.dram_tensor("out", [S, D], F32, kind="ExternalOutput").ap()
    yT = nc.dram_tensor("yT_scr", [2048, S], BF16, kind="Internal").ap()
    rows_scr = nc.dram_tensor("rows_scr", [2, 4, S], F32, kind="Internal").ap()
    wo0_bf = nc.dram_tensor("wo0_bf", [2048, D], BF16, kind="Internal").ap()
    w1_bf = nc.dram_tensor("w1_bf", [D, ODD_IN], BF16, kind="Internal").ap()
    wo1_bf = nc.dram_tensor("wo1_bf", [2048, D], BF16, kind="Internal").ap()
    dbg_out = {}
    if dbg:
        dbg_out["dbg_h1"] = nc.dram_tensor("dbg_h1", [S, D], F32, kind="ExternalOutput").ap()
        dbg_out["dbg_yT"] = nc.dram_tensor("dbg_yT", [2048, S], BF16, kind="ExternalOutput").ap()
        dbg_out["dbg_rows"] = nc.dram_tensor("dbg_rows", [3, 4, S], F32, kind="ExternalOutput").ap()
        dbg_out["dbg_cols"] = nc.dram_tensor("dbg_cols", [4, 3, 128, S // 128], F32, kind="ExternalOutput").ap()
        dbg_out["dbg_r32"] = nc.dram_tensor("dbg_r32", [4, 2, S], F32, kind="ExternalOutput").ap()

    w0v = l0_w_in.rearrange("(kc p) e -> p kc e", p=128)
    w1v = l1_w_in.rearrange("(kc p) e -> p kc e", p=128)
    wo0v = l0_w_out.rearrange("(kc p) e -> p kc e", p=128)
    wo1v = l1_w_out.rearrange("(kc p) e -> p kc e", p=128)
    w1bv = w1_bf.rearrange("(kc p) e -> p kc e", p=128)
    wo0bv = wo0_bf.rearrange("(kc p) e -> p kc e", p=128)
    wo1bv = wo1_bf.rearrange("(kc p) e -> p kc e", p=128)

    with ExitStack() as es:
        E = es.enter_context
        T = Trk(nc, es)

        def sb(name, shape, dt=F32, scope=None):
            return (scope or es).enter_context(nc.sbuf_tensor(name, list(shape), dt))

        S2 = [E(nc.psum_tensor("S2_%d" % i, [128, 1024], F32)) for i in range(2)]
        banks = [None] * 4 + [E(nc.psum_tensor("bank%d" % i, [128, 512], F32)) for i in range(4, 8)]

        def bk(i):
            if i < 4:
                return S2[i // 2][:, (i % 2) * 512:(i % 2 + 1) * 512]
            return banks[i]

        def bkb(i):
            if i < 4:
                return S2[i // 2][:].bitcast(BF16)[:, (i % 2) * 1024:(i % 2 + 1) * 1024]
            return banks[i][:].bitcast(BF16)

        ident = sb("ident", [128, 128], BF16)
        tri = sb("tri", [128, 128], BF16)
        ones_bf = sb("ones_bf", [128, 128], BF16)
        inv128_bf = sb("inv128_bf", [128, 128], BF16)
        ones_f = sb("ones_f", [128, 128], F32)
        T.dma("sp", ident[:], c_ident[:, :], w=["ident"])
        T.dma("sp", tri[:], c_tri[:, :], w=["tri"])
        T.op("pool", lambda: nc.gpsimd.memset(ones_bf[:], 1.0), w=["ones_bf"])
        T.op("pool", lambda: nc.gpsimd.memset(inv128_bf[:], 1.0 / 128), w=["inv128_bf"])
        T.op("pool", lambda: nc.gpsimd.memset(ones_f[:], 1.0), w=["ones_f"])

        NSTv = [2]
        wst = []
        wst_i = [0]
        wst_gen = [0]

        def alloc_wst(scope):
            wst_gen[0] += 1
            wst[:] = [sb("wst%d_%d" % (wst_gen[0], i), [128, 8, 512], F32, scope) for i in range(NSTv[0])]

        def load_w(dst_fn, srcs, dst_key, defer=False):
            b = wst_i[0] % len(wst)
            wst_i[0] += 1
            tot = 0
            kcn = srcs[0][0].shape[1]
            for (src, off) in srcs:
                n = src.shape[2]
                T.dma("sp", wst[b][:, 0:kcn, off:off + n], src, w=[("wst", b, off)], r=[])
                tot = max(tot, off + n)
            def cast():
                T.op("dve", lambda: nc.vector.tensor_copy(out=dst_fn(), in_=wst[b][:, 0:kcn, 0:tot]),
                     r=[("wst", b, off) for (_, off) in srcs], w=[dst_key])
            if defer:
                return cast
            cast()

        def rstd_from_ssq(ssq_ap, tmp_ap, out_ap, n, keys_r, key_tmp, key_out):
            T.op("act", lambda: nc.scalar.activation(out=tmp_ap, in_=ssq_ap, func=AF.Ln, scale=1.0 / n, bias=EPS),
                 r=keys_r, w=[key_tmp])
            T.op("act", lambda: nc.scalar.activation(out=out_ap, in_=tmp_ap, func=AF.Exp, scale=-0.5),
                 r=[key_tmp], w=[key_out])

        with ExitStack() as sc0:
            hnT = sb("hnT", [128, 8, S], BF16, sc0)
            with ExitStack() as scA:
                NSTv[0] = 2
                alloc_wst(scA)
                QT = sb("QT", [128, S], BF16, scA)
                KT = sb("KT", [128, S], BF16, scA)
                V = sb("V", [128, NT, 128], BF16, scA)
                zT2 = [sb("zT%d" % i, [128, S], BF16, scA) for i in range(2)]
                Ocp = sb("Ocp", [128, 2, 512], F32, scA)
                Lcp = sb("Lcp", [128, 2, 512], F32, scA)
                pend_tail = [None]
                wbf2 = [sb("wbfA%d" % i, [128, 8, 512], BF16, scA) for i in range(2)]
                P2 = [sb("P2_%d" % b, [128, 2, 512], BF16, scA) for b in range(2)]
                r1 = sb("r1", [128, 512], F32, scA)
                t1 = sb("t1", [128, 512], F32, scA)
                t2 = sb("t2", [128, 512], F32, scA)
                oT = sb("oT", [128, 512], F32, scA)
                sq = sb("sqA", [128, 512], BF16, scA)
                lnv = sb("lnvA", [128, 512], F32, scA)
                rsd = sb("rsdA", [128, 512], F32, scA)
                yb = [sb("ybA%d" % i, [128, 512], BF16, scA) for i in range(2)]
                lam4 = sb("lam4", [128, 4, 64], F32, scA)
                lamj = sb("lamj", [128, 64], F32, scA)
                lams = sb("lams", [128, 8], F32, scA)
                gda = sb("gda", [128, 2, 16], F32, scA)
                for i, a in enumerate((lq1, lk1, lq2, lk2)):
                    T.dma("sp", lam4[:, i, :], a.partition_broadcast(128), w=[("lam4", i)])
                for i in range(2):
                    T.op("dve", lambda: nc.vector.tensor_tensor(out=lamj[:], in0=lam4[:, 2 * i, :], in1=lam4[:, 2 * i + 1, :],
                                                                op=ALU.mult),
                         r=[("lam4", 2 * i), ("lam4", 2 * i + 1)], w=["lamj"])
                    T.op("dve", lambda: nc.vector.reduce_sum(out=lams[:, i:i + 1], in_=lamj[:], axis=mybir.AxisListType.X),
                         r=["lamj"], w=[("lams", i)])
                    T.op("act", lambda: nc.scalar.activation(out=lams[:, 2 + i:3 + i], in_=lams[:, i:i + 1], func=AF.Exp),
                         r=[("lams", i)], w=[("lams", 2 + i)])
                T.op("dve", lambda: nc.vector.tensor_tensor(out=lams[:, 4:5], in0=lams[:, 3:4], in1=lams[:, 2:3], op=ALU.subtract),
                     r=[("lams", 2), ("lams", 3)], w=[("lams", 4)])
                T.op("dve", lambda: nc.vector.tensor_scalar(out=lams[:, 5:6], in0=lams[:, 4:5], scalar1=-0.2, scalar2=None,
                                                            op0=ALU.add),
                     r=[("lams", 4)], w=["neglam"])
                neglam = lams[:, 5:6]
                T.dma("sp", gda[:, 0, 0:1], l0_da_head_g.rearrange("(p o) -> p o", o=1), w=["gda0"])
                T.op("dve", lambda: nc.vector.tensor_scalar(out=gda[:, 1, 0:1], in0=gda[:, 0, 0:1], scalar1=0.8, scalar2=None,
                                                            op0=ALU.mult), r=["gda0"], w=["gda"])

                def load_head(hh):
                    srcs = [(w0v[:, :, off + hh * 128: off + (hh + 1) * 128], i * 128)
                            for i, off in enumerate((0, 1024, 2048, 3072))]
                    load_w(lambda: wbf2[hh % 2][:, :, :], srcs, ("wbfA", hh % 2))

                pcnt = [0]

                def proj_block(h, tb, wbf, wkey, zT):
                    for (dst, c0, kind) in ((QT, 0, "q"), (KT, 128, "k"), (zT, 384, "z")):
                        pb = pcnt[0] % 4
                        pcnt[0] += 1
                        for kc in range(8):
                            T.op("pe", lambda: nc.tensor.matmul(bk(pb)[:, :], lhsT=wbf[:, kc, c0:c0 + 128],
                                                                rhs=hnT[:, kc, tb * 512:(tb + 1) * 512],
                                                                start=(kc == 0), stop=(kc == 7)),
                                 r=[wkey, ("hnT", tb)], w=[("bank", pb)], inc=(kc == 7))
                        d = dst[:, tb * 512:(tb + 1) * 512]
                        if kind == "z":
                            T.op("act", lambda: nc.scalar.activation(out=d, in_=bk(pb)[:, :], func=AF.Silu),
                                 r=[("bank", pb)], w=[("z", h % 2, tb)])
                        else:
                            T.op("dve", lambda: nc.vector.tensor_copy(out=d, in_=bk(pb)[:, :]), r=[("bank", pb)], w=[(kind, tb)])
                    tg = tb
                    pb = pcnt[0] % 4
                    pcnt[0] += 1
                    for ti in range(4):
                        tt = tg * 4 + ti
                        for kc in range(8):
                            T.op("pe", lambda: nc.tensor.matmul(bk(pb)[:, ti * 128:(ti + 1) * 128],
                                                                lhsT=hnT[:, kc, tt * 128:(tt + 1) * 128],
                                                                rhs=wbf[:, kc, 256:384], start=(kc == 0), stop=(kc == 7)),
                                 r=[wkey, ("hnT", tg)], w=[("bank", pb)], inc=(kc == 7 and ti == 3))
                    T.op("dve", lambda: nc.vector.tensor_copy(out=V[:, tg * 4:(tg + 1) * 4, :],
                                                              in_=bk(pb)[:, :].rearrange("p (a b) -> p a b", a=4)),
                         r=[("bank", pb)], w=[("v", tg)])

                load_head(0)
                with ExitStack() as sc1:
                    g0bc = sb("g0bc", [128, D], F32, sc1)
                    T.dma("sp", g0bc[:], l0_pre_g.partition_broadcast(128), w=["g0bc"])
                    xt = [sb("xt%d" % i, [128, D], F32, sc1) for i in range(3)]
                    xn = [sb("xn%d" % i, [128, D], BF16, sc1) for i in range(2)]
                    junk = [sb("junk%d" % i, [128, D], BF16, sc1) for i in range(2)]
                    st = [sb("st%d" % i, [128, 4], F32, sc1) for i in range(2)]
                    for i in range(NT):
                        b = i % 2
                        bx = i % 3
                        T.dma("sp" if i % 2 == 0 else "pool", xt[bx][:], x[i * 128:(i + 1) * 128, :], w=[("xt", bx)])
                        T.op("act", lambda: nc.scalar.activation(out=junk[b][:], in_=xt[bx][:], func=AF.Square,
                                                                 accum_out=st[b][:, 0:1]),
                             r=[("xt", bx)], w=[("junk", b), ("ssq", b)])
                        rstd_from_ssq(st[b][:, 0:1], st[b][:, 1:2], st[b][:, 2:3], D, [("ssq", b)], ("lnv", b), ("rstd", b))
                        T.op("dve", lambda: nc.vector.scalar_tensor_tensor(out=xn[b][:], in0=xt[bx][:], scalar=st[b][:, 2:3],
                                                                           in1=g0bc[:], op0=ALU.mult, op1=ALU.mult),
                             r=[("xt", bx), ("rstd", b), "g0bc"], w=[("xn", b)])
                        pb = 6 + b
                        for kc in range(8):
                            T.op("pe", lambda: nc.tensor.transpose(bkb(pb)[:, kc * 128:(kc + 1) * 128],
                                                                   xn[b][:, kc * 128:(kc + 1) * 128], ident[:]),
                                 r=[("xn", b), "ident"], w=[("bank", pb)], inc=(kc == 7))
                        src = bkb(pb).rearrange("p (k t) -> p k t", k=8)
                        dst = hnT[:, :, i * 128:(i + 1) * 128]
                        if i % 2 == 0:
                            T.op("dve", lambda: nc.vector.tensor_copy(out=dst, in_=src), r=[("bank", pb)], w=[("hnT", i // 4)])
                        else:
                            T.op("act", lambda: nc.scalar.copy(out=dst, in_=src), r=[("bank", pb)], w=[("hnT", i // 4)])
                        if i % 4 == 3 and i // 4 >= 1:
                            proj_block(0, i // 4 - 1, wbf2[0], ("wbfA", 0), zT2[0])
                    proj_block(0, NB - 1, wbf2[0], ("wbfA", 0), zT2[0])
                for h in range(8):
                    wbf = wbf2[h % 2]
                    wkey = ("wbfA", h % 2)
                    zT = zT2[h % 2]
                    if h > 0:
                        for tb in range(NB):
                            proj_block(h, tb, wbf, wkey, zT)
                    if h == 0:
                        cast_jobs = []
                        for (src, dst, nrow) in ((l0_w_out, wo0_bf, 2048), (l1_w_in, w1_bf, D), (l1_w_out, wo1_bf, 2048)):
                            for r0 in range(0, nrow, 256):
                                cast_jobs.append((dst[r0:r0 + 256, :], src[r0:r0 + 256, :]))
                    if h + 1 < 8:
                        load_head(h + 1)

                    for qb in range(NB):
                        nj = 4 * qb + 4
                        if h >= 1 and cast_jobs:
                            cj = cast_jobs.pop(0)
                            T.dma("pool", cj[0], cj[1])

                        def qk(j):
                            c0 = max(0, j - 4 * qb) * 128
                            sbuf_i = j % 2
                            for c in range(2):
                                pb = 2 * sbuf_i + c
                                T.op("pe", lambda: nc.tensor.matmul(bk(pb)[:, c0:512],
                                                                    lhsT=KT[c * 64:(c + 1) * 64, j * 128:(j + 1) * 128],
                                                                    rhs=QT[c * 64:(c + 1) * 64, qb * 512 + c0:(qb + 1) * 512],
                                                                    start=True, stop=True),
                                     r=[("k", j // 4), ("q", qb)], w=[("bank", pb)], inc=(c == 1))

                        qk(0)
                        for j in range(nj):
                            c0 = max(0, j - 4 * qb) * 128
                            si = j % 2
                            if j + 1 < nj:
                                qk(j + 1)
                            T.op("act", lambda: nc.scalar.activation(
                                out=P2[si][:, :, c0:512],
                                in_=S2[si][:, :].rearrange("p (c q) -> p c q", c=2)[:, :, c0:512],
                                func=AF.Exp, scale=0.125),
                                 r=[("bank", 2 * si), ("bank", 2 * si + 1)], w=[("P", si)])
                            if j >= 4 * qb:
                                meng, mfn = ("dve", nc.vector.tensor_tensor) if qb >= 2 else ("pool", nc.gpsimd.tensor_tensor)
                                T.op(meng, lambda: mfn(out=P2[si][:, :, c0:c0 + 128], in0=P2[si][:, :, c0:c0 + 128],
                                                       in1=tri[:, None, :].to_broadcast([128, 2, 128]), op=ALU.mult),
                                     r=[("P", si), "tri"], w=[("P", si)])
                            if pend_tail[0] is not None and j == min(8, nj - 1):
                                pend_tail[0](2 * si)
                                pend_tail[0] = None
                            for c in range(2):
                                T.op("pe", lambda: nc.tensor.matmul(bk(4 + c)[:, c0:512], lhsT=V[:, j, :],
                                                                    rhs=P2[si][:, c, c0:512], start=(j == 0), stop=(j == nj - 1)),
                                     r=[("P", si), ("v", j // 4)], w=[("bank", 4 + c)], inc=False)
                                T.op("pe", lambda: nc.tensor.matmul(bk(6 + c)[:, c0:512], lhsT=ones_bf[:],
                                                                    rhs=P2[si][:, c, c0:512], start=(j == 0), stop=(j == nj - 1)),
                                     r=[("P", si), "ones_bf"], w=[("bank", 6 + c)], inc=(c == 1))
                        for c in range(2):
                            T.op("dve", lambda: nc.vector.tensor_copy(out=Lcp[:, c, :], in_=bk(6 + c)[:, :]), r=[("bank", 6 + c)], w=[("Lcp", c)])
                            T.op("dve", lambda: nc.vector.tensor_copy(out=Ocp[:, c, :], in_=bk(4 + c)[:, :]), r=[("bank", 4 + c)], w=[("Ocp", c)])
                        T.op("dve", lambda: nc.vector.reciprocal(out=r1[:], in_=Lcp[:, 0, :]), r=[("Lcp", 0)], w=["r1"])
                        T.op("dve", lambda: nc.vector.tensor_tensor(out=t1[:], in0=Ocp[:, 0, :], in1=r1[:], op=ALU.mult),
                             r=[("Ocp", 0), "r1"], w=["t1"])
                        T.op("dve", lambda: nc.vector.reciprocal(out=r1[:], in_=Lcp[:, 1, :]), r=[("Lcp", 1)], w=["r1"])
                        T.op("dve", lambda: nc.vector.tensor_tensor(out=t2[:], in0=Ocp[:, 1, :], in1=r1[:], op=ALU.mult),
                             r=[("Ocp", 1), "r1"], w=["t2"])
                        T.op("dve", lambda: nc.vector.scalar_tensor_tensor(out=oT[:], in0=t2[:], scalar=neglam, in1=t1[:],
                                                                           op0=ALU.mult, op1=ALU.add),
                             r=["t1", "t2", "neglam"], w=["oT"])
                        T.op("dve", lambda: nc.vector.tensor_tensor(out=sq[:], in0=oT[:], in1=oT[:], op=ALU.mult),
                             r=["oT"], w=["sqA"])

                        def tail(pbank, h=h, qb=qb, zT=zT):
                            T.op("pe", lambda: nc.tensor.matmul(bk(pbank)[:, :], lhsT=inv128_bf[:], rhs=sq[:], start=True, stop=True),
                                 r=["sqA", "inv128_bf"], w=[("bank", pbank)])
                            T.op("act", lambda: nc.scalar.activation(out=lnv[:], in_=bk(pbank)[:, :], func=AF.Ln, bias=EPS),
                                 r=[("bank", pbank)], w=["lnvA"])
                            T.op("act", lambda: nc.scalar.activation(out=rsd[:], in_=lnv[:], func=AF.Exp, scale=-0.5),
                                 r=["lnvA"], w=["rsdA"])
                            T.op("dve", lambda: nc.vector.tensor_tensor(out=t1[:], in0=oT[:], in1=rsd[:], op=ALU.mult),
                                 r=["oT", "rsdA"], w=["t1"])
                            ybb = yb[qb % 2]
                            T.op("dve", lambda: nc.vector.scalar_tensor_tensor(out=ybb[:], in0=t1[:], scalar=gda[:, 1, 0:1],
                                                                               in1=zT[:, qb * 512:(qb + 1) * 512],
                                                                               op0=ALU.mult, op1=ALU.mult),
                                 r=["t1", "gda", ("z", h % 2, qb)], w=[("ybA", qb % 2)])
                            T.dma("pool", yT[h * 128:(h + 1) * 128, qb * 512:(qb + 1) * 512], ybb[:],
                                  r=[("ybA", qb % 2)], w=[("yT", h, qb)])

                        assert pend_tail[0] is None
                        pend_tail[0] = tail
                pend_tail[0](0)
                pend_tail[0] = None
                while cast_jobs:
                    cj = cast_jobs.pop(0)
                    T.dma("pool", cj[0], cj[1])
                T.barrier()

            with ExitStack() as scB:
                NSTv[0] = 1
                alloc_wst(scB)
                rowA_t = sb("rowA", [64, S], F32, scB)
                rowG_t = sb("rowG", [64, S], F32, scB)
                rowA = rowA_t[0:4, :]
                rowG = rowG_t[0:4, :]
                rows1 = (rowA_t, rowG_t)
                rowM = sb("rowM", [4, S], F32, scB)
                rowc = sb("rowc", [4, 4, NT], F32, scB)
                rowc1 = sb("rowc1", [64, 2, NT], F32, scB)
                Lm2 = sb("Lm2", [NT + 1, 2, 128], F32, scB)
                Rm = sb("Rm", [NT + 1, NT], F32, scB)
                gb = sb("gb", [4, 3, 16], F32, scB)
                wif = sb("wif", [128, 8, 8], BF16, scB)
                colw = sb("colw", [128, NT], F32, scB)
                colf = sb("colf", [128, NT], F32, scB)
                decb = sb("decb", [128, NT], F32, scB)
                qTm = sb("qTm", [128, S], BF16, scB)
                kTm = sb("kTm", [128, S], BF16, scB)
                wqk2 = [sb("wqk%d" % i, [128, 8, 256], BF16, scB) for i in range(2)]
                wvoz = sb("wvoz", [128, 8, 768], BF16, scB)
                pre = [sb("pre%d" % i, [128, 515], F32, scB) for i in range(2)]
                acc2 = [sb("acc%d" % i, [128, 512], F32, scB) for i in range(2)]
                cw_all = sb("cw", [128, 8, 16], F32, scB)
                cb_all = sb("cb", [128, 8, 16], F32, scB)
                for hh in range(4):
                    for qi in range(2):
                        ch0 = qi * 512 + hh * 128
                        T.dma("pool", cw_all[:, hh * 2 + qi, 0:4], l0_conv_w[:, ch0:ch0 + 128].rearrange("j c -> c j"),
                              w=[("cw", hh, qi)], allow_slow_non_contiguous=True)
                        T.dma("pool", cb_all[:, hh * 2 + qi, 0:1], l0_conv_b[ch0:ch0 + 128].rearrange("(p o) -> p o", o=1),
                              w=[("cb", hh, qi)])
                gml = sb("gml", [128, 256], F32, scB)
                Cst = sb("Cst", [128, 257], F32, scB)
                Cd = sb("Cd", [128, 257], F32, scB)
                Cdb = sb("Cdb", [128, 257], BF16, scB)
                SmT4 = [sb("SmT4_%d" % i, [128, 4, 128], BF16, scB) for i in range(2)]
                ktok4 = [sb("ktok4_%d" % i, [128, 4, 128], BF16, scB) for i in range(2)]
                vw4 = [sb("vw4_%d" % i, [128, 4, 257], BF16, scB) for i in range(2)]
                sg4 = [sb("sg4_%d" % i, [128, 4, 256], BF16, scB) for i in range(2)]
                zs4 = [sb("zs4_%d" % i, [128, 4, 256], BF16, scB) for i in range(2)]
                hmt4 = [sb("hmt4_%d" % i, [128, 4, 256], BF16, scB) for i in range(2)]
                ssq4 = [sb("ssq4_%d" % i, [128, 4], F32, scB) for i in range(2)]
                rs4 = [sb("rs4_%d" % i, [128, 8], F32, scB) for i in range(2)]
                stR = sb("stR", [128, 4, 4], F32, scB)
                junkB = sb("junkB", [128, 256], BF16, scB)
                ytok4 = sb("ytok4", [128, 4, 256], BF16, scB)
                yTb = [sb("yTb%d" % i, [128, 2, 512], BF16, scB) for i in range(2)]

                T.dma("sp", gml[:], l0_ml_head_g.partition_broadcast(128), w=["gml"])
                T.dma("sp", gb[:, 0, 0:1], l0_b_igate.rearrange("(p o) -> p o", o=1), w=["gb0"])
                T.dma("sp", gb[:, 2, 0:1], l0_b_fgate.rearrange("(p o) -> p o", o=1), w=["gb2"])
                T.op("dve", lambda: nc.vector.tensor_scalar(out=gb[:, 1, 0:1], in0=gb[:, 2, 0:1], scalar1=-1.0, scalar2=None,
                                                            op0=ALU.mult), r=["gb2"], w=["gb1"])
                load_w(lambda: wif[:, :, :], [(w0v[:, :, 6144:6152], 0)], "wif")
                for gi, (row, key) in enumerate(((rowA, "rowA"), (rowG, "rowG"))):
                    for tb in range(NB):
                        pb = tb % 2
                        for kc in range(8):
                            T.op("pe", lambda: nc.tensor.matmul(bk(pb)[0:4, :], lhsT=wif[:, kc, gi * 4:gi * 4 + 4],
                                                                rhs=hnT[:, kc, tb * 512:(tb + 1) * 512],
                                                                start=(kc == 0), stop=(kc == 7)),
                                 r=["wif", ("hnT", tb)], w=[("bank", pb)], inc=(kc == 7))
                        T.op("dve", lambda: nc.vector.tensor_copy(out=row[:, tb * 512:(tb + 1) * 512], in_=bk(pb)[0:4, :]),
                             r=[("bank", pb)], w=[key])
                T.op("act", lambda: nc.scalar.activation(out=rowG[:], in_=rowG[:], func=AF.Exp, scale=-1.0, bias=gb[:, 1, 0:1]),
                     r=["rowG", "gb1"], w=["rowG"])
                T.op("act", lambda: nc.scalar.activation(out=rowG[:], in_=rowG[:], func=AF.Ln, bias=1.0),
                     r=["rowG"], w=["rowG"])
                T.op("dve", lambda: nc.vector.tensor_tensor_scan(out=rowG[:], data0=ones_f[0:4, 0:1].to_broadcast([4, S]),
                                                                 data1=rowG[:], initial=0.0, op0=ALU.mult, op1=ALU.add),
                     r=["rowG", "ones_f"], w=["rowG"])
                T.op("dve", lambda: nc.vector.scalar_tensor_tensor(out=rowA[:], in0=rowA[:], scalar=gb[:, 0, 0:1], in1=rowG[:],
                                                                   op0=ALU.add, op1=ALU.add),
                     r=["rowA", "rowG", "gb0"], w=["rowA"])
                T.op("dve", lambda: nc.vector.tensor_tensor_scan(out=rowM[:], data0=rowA[:], data1=rowA[:], initial=0.0,
                                                                 op0=ALU.max, op1=ALU.max),
                     r=["rowA"], w=["rowM"])
                mend = rowM[:].rearrange("p (c t) -> p c t", t=128)[:, :, 127]
                T.op("dve", lambda: nc.vector.tensor_scalar(out=rowc[:, 0, :], in0=mend, scalar1=-1.0, scalar2=None, op0=ALU.mult),
                     r=["rowM"], w=[("rowc", 0)])
                T.op("dve", lambda: nc.vector.memset(rowc[:, 1, 0:1], 0.0), w=[("rowc", 1, 0)])
                if NT > 1:
                    T.op("dve", lambda: nc.vector.tensor_copy(out=rowc[:, 1, 1:NT], in_=mend[:, 0:NT - 1]),
                         r=["rowM"], w=[("rowc", 1, 1)])
                T.op("dve", lambda: nc.vector.tensor_tensor(out=rowc[:, 2, :], in0=rowc[:, 1, :], in1=rowc[:, 0, :], op=ALU.add),
                     r=[("rowc", 0), ("rowc", 1, 0), ("rowc", 1, 1)], w=[("rowc", 2)])

                if dbg:
                    T.dma("sp", dbg_out["dbg_rows"][0], rowA, r=["rowA"])
                    T.dma("sp", dbg_out["dbg_rows"][1], rowG, r=["rowG"])
                    T.dma("sp", dbg_out["dbg_rows"][2], rowM[:], r=["rowM"])
                T.dma("sp", rows_scr[0], rowA, r=["rowA"], w=[("rows_scr", 0)])
                T.dma("sp", rows_scr[1], rowG, r=["rowG"], w=[("rows_scr", 1)])
                T.op("dve", lambda: nc.vector.memset(Lm2[:], 1.0), w=[("Lm2", 0), ("Lm2", 1)])
                T.op("dve", lambda: nc.vector.tensor_copy(out=Rm[0:NT, :], in_=ident[0:NT, 0:NT]), r=["ident"], w=["Rm_id"])
                for h in range(4):
                    for which in range(2):
                        T.dma("sp", Lm2[0:NT, which, :], rows_scr[which, h, :].rearrange("(c t) -> c t", t=128),
                              r=[("rows_scr", which)], w=[("Lm2", which)])
                    T.dma("sp", Rm[NT:NT + 1, :], rowc[h:h + 1, 0, :], r=[("rowc", 0)], w=["Rm_m"])
                    T.dma("sp", rowc1[32:33, 1, :], rowc[h:h + 1, 2, :], r=[("rowc", 2)], w=[("rowc1", 1)])
                    for which, (bnk, dstc, bias) in enumerate(((0, colw, 0.0), (1, colf, 0.5 * float(np.log(128.0))))):
                        T.op("pe", lambda: nc.tensor.matmul(bk(bnk)[:, 0:NT], lhsT=Lm2[:, which, :], rhs=Rm[:, :], start=True, stop=True),
                             r=[("Lm2", which), "Rm_id", "Rm_m"], w=[("bank", bnk)])
                        T.op("act", lambda: nc.scalar.activation(out=dstc[:], in_=bk(bnk)[:, 0:NT], func=AF.Exp, bias=bias),
                             r=[("bank", bnk)], w=[("col", which)])
                    T.op("pe", lambda: nc.tensor.matmul(bk(2)[:, 0:NT], lhsT=ones_f[32:33, 0:128], rhs=rowc1[32:33, 1, :],
                                                        start=True, stop=True),
                         r=[("rowc1", 1), "ones_f"], w=[("bank", 2)])
                    T.op("act", lambda: nc.scalar.activation(out=decb[:], in_=bk(2)[:, 0:NT], func=AF.Exp),
                         r=[("bank", 2)], w=["decb"])
                    if dbg:
                        T.dma("sp", dbg_out["dbg_cols"][h, 0], colw[:], r=[("col", 0)])
                        T.dma("sp", dbg_out["dbg_cols"][h, 1], colf[:], r=[("col", 1)])
                        T.dma("sp", dbg_out["dbg_cols"][h, 2], decb[:], r=["decb"])
                    def load_wqk(hh, defer=False):
                        return load_w(lambda: wqk2[hh % 2][:, :, :], [(w0v[:, :, 4096 + hh * 128:4096 + (hh + 1) * 128], 0),
                                                                      (w0v[:, :, 4608 + hh * 128:4608 + (hh + 1) * 128], 128)],
                                      ("wqk", hh % 2), defer=defer)

                    wqk = wqk2[h % 2]
                    if h == 0:
                        load_wqk(0)
                    cast_voz0 = load_w(lambda: wvoz[:, :, 0:512], [(w0v[:, :, 5120 + h * 256:5120 + (h + 1) * 256], 0),
                                                                   (w0v[:, :, 6152 + h * 256:6152 + (h + 1) * 256], 256)], "wvoz0",
                                       defer=True)
                    steps = [(qi, tb) for qi in range(2) for tb in range(NB)]
                    dsts = (qTm, kTm)

                    def conv_front(n):
                        qi, tb = steps[n]
                        pb = 4 + (n % 2)
                        for kc in range(8):
                            T.op("pe", lambda: nc.tensor.matmul(bk(pb)[:, :], lhsT=wqk[:, kc, qi * 128:(qi + 1) * 128],
                                                                rhs=hnT[:, kc, tb * 512:(tb + 1) * 512],
                                                                start=(kc == 0), stop=(kc == 7)),
                                 r=[("wqk", h % 2), ("hnT", tb)], w=[("bank", pb)], inc=(kc == 7))
                        T.op("act", lambda: nc.scalar.copy(out=pre[n % 2][:, 3:515], in_=bk(pb)[:, :]),
                             r=[("bank", pb)], w=[("pre", n % 2, "m")])

                    def conv_back(n):
                        qi, tb = steps[n]
                        pr = pre[n % 2]
                        if tb == 0:
                            T.op("dve", lambda: nc.vector.memset(pr[:, 0:3], 0.0), w=[("pre", n % 2, "c")])
                        T.op("dve", lambda: nc.vector.tensor_scalar(out=acc2[n % 2][:], in0=pr[:, 3:515], scalar1=cw_all[:, h * 2 + qi, 3:4],
                                                                    scalar2=None, op0=ALU.mult),
                             r=[("pre", n % 2, "m"), ("cw", h, qi)], w=[("acc", n % 2)])
                        for j in (2, 1, 0):
                            T.op("dve", lambda: nc.vector.scalar_tensor_tensor(out=acc2[n % 2][:], in0=pr[:, j:j + 512],
                                                                               scalar=cw_all[:, h * 2 + qi, j:j + 1], in1=acc2[n % 2][:],
                                                                               op0=ALU.mult, op1=ALU.add),
                                 r=[("pre", n % 2, "m"), ("pre", n % 2, "c"), ("cw", h, qi), ("acc", n % 2)], w=[("acc", n % 2)])
                        T.op("act", lambda: nc.scalar.activation(out=dsts[qi][:, tb * 512:(tb + 1) * 512], in_=acc2[n % 2][:],
                                                                 func=AF.Silu, bias=cb_all[:, h * 2 + qi, 0:1]),
                             r=[("acc", n % 2), ("cb", h, qi)], w=[("qk", qi, tb)])
                        if n + 1 < len(steps) and steps[n + 1][1] > 0:
                            T.op("dve", lambda: nc.vector.tensor_copy(out=pre[(n + 1) % 2][:, 0:3], in_=pr[:, 512:515]),
                                 r=[("pre", n % 2, "m")], w=[("pre", (n + 1) % 2, "c")])

                    conv_front(0)
                    cast_voz1 = None
                    for n in range(len(steps)):
                        if n + 1 < len(steps):
                            conv_front(n + 1)
                        conv_back(n)
                        if n == len(steps) // 2 - 1:
                            cast_voz0()
                            cast_voz1 = load_w(lambda: wvoz[:, :, 512:768], [(w0v[:, :, 7176 + h * 256:7176 + (h + 1) * 256], 0)],
                                               "wvoz1", defer=True)
                    cast_voz1()
                    T.op("dve", lambda: nc.vector.memset(Cst[:], 0.0), w=["Cst"])

                    import os
                    _sk = os.environ.get("KSKIP", "")

                    def P_pre(tb):
                        par = tb % 2
                        for ci in range(4):
                            c = tb * 4 + ci
                            T.op("pe", lambda: nc.tensor.transpose(bkb(7)[:, 512 + ci * 128:512 + (ci + 1) * 128],
                                                                   kTm[:, c * 128:(c + 1) * 128], ident[:]),
                                 r=[("qk", 1, tb), "ident"], w=[("bank", 7)], inc=(ci == 3))
                        T.op("act", lambda: nc.scalar.copy(out=ktok4[par][:], in_=bkb(7)[:, 512:1024].rearrange("p (a b) -> p a b", a=4)),
                             r=[("bank", 7)], w=[("ktok", par, ci) for ci in range(4)])
                        if "v" not in _sk:
                          T.op("pool", lambda: nc.gpsimd.tensor_copy(out=vw4[par][:, :, 256], in_=colw[:, tb * 4:(tb + 1) * 4]),
                             r=[("col", 0)], w=[("vw1", par)])

                    def P_chunk(tb, ci):
                        par = tb % 2
                        if True:
                            c = tb * 4 + ci
                            cs = slice(c * 128, (c + 1) * 128)
                            xb = ci % 2
                            T.op("pe", lambda: nc.tensor.matmul(bk(xb)[:, 0:128], lhsT=kTm[:, cs], rhs=qTm[:, cs], start=True, stop=True),
                                 r=[("qk", 0, tb), ("qk", 1, tb)], w=[("bank", xb)])
                            for kc in range(8):
                                T.op("pe", lambda: nc.tensor.matmul(bk(xb)[:, 128:384], lhsT=hnT[:, kc, cs], rhs=wvoz[:, kc, 0:256],
                                                                    start=(kc == 0), stop=(kc == 7)),
                                     r=[("hnT", tb), "wvoz0"], w=[("bank", xb)], inc=(kc == 7))
                            hf = ci % 2
                            ob, zb = 2, 3 + ci // 2
                            for kc in range(8):
                                T.op("pe", lambda: nc.tensor.matmul(bk(ob)[:, hf * 256:(hf + 1) * 256], lhsT=hnT[:, kc, cs],
                                                                    rhs=wvoz[:, kc, 256:512], start=(kc == 0), stop=(kc == 7)),
                                     r=[("hnT", tb), "wvoz0"], w=[("bank", 2)], inc=(kc == 7))
                            for kc in range(8):
                                T.op("pe", lambda: nc.tensor.matmul(bk(zb)[:, hf * 256:(hf + 1) * 256], lhsT=hnT[:, kc, cs],
                                                                    rhs=wvoz[:, kc, 512:768], start=(kc == 0), stop=(kc == 7)),
                                     r=[("hnT", tb), "wvoz1"], w=[("bank", 3 + ci // 2)], inc=(kc == 7))
                            T.op("dve", lambda: nc.vector.tensor_tensor(out=SmT4[par][:, ci, :], in0=bk(xb)[:, 0:128], in1=tri[:], op=ALU.mult),
                                 r=[("bank", xb), "tri"], w=[("SmT", par, ci)])
                            T.op("dve", lambda: nc.vector.tensor_scalar(out=vw4[par][:, ci, 0:256], in0=bk(xb)[:, 128:384],
                                                                        scalar1=colw[:, c:c + 1], scalar2=None, op0=ALU.mult),
                                 r=[("bank", xb), ("col", 0)], w=[("vw", par, ci)])
                            if hf == 1 and "s" not in _sk:
                                T.op("act", lambda: nc.scalar.activation(out=sg4[par][:, ci - 1:ci + 1, :],
                                                                         in_=bk(ob)[:, :].rearrange("p (a b) -> p a b", a=2),
                                                                         func=AF.Sigmoid),
                                     r=[("bank", 2)], w=[("sg", par, ci // 2)])

                    def P_post(tb):
                        par = tb % 2
                        for pr_ in range(2 if "z" not in _sk else 0):
                            T.op("act", lambda: nc.scalar.activation(out=zs4[par][:, 2 * pr_:2 * pr_ + 2, :],
                                                                     in_=bk(3 + pr_)[:, :].rearrange("p (a b) -> p a b", a=2),
                                                                     func=AF.Silu),
                                 r=[("bank", 3 + pr_)], w=[("zs", par, pr_)])
                        if "g" not in _sk:
                          T.op("pool", lambda: nc.gpsimd.tensor_tensor(out=zs4[par][:], in0=zs4[par][:],
                                                                     in1=gml[:, None, :].to_broadcast([128, 4, 256]), op=ALU.mult),
                             r=[("zs", par, 0), ("zs", par, 1), "gml"], w=[("zs", par, 0), ("zs", par, 1)])

                    def rB(tb, ci):
                        par = tb % 2
                        c = tb * 4 + ci
                        T.op("dve", lambda: nc.vector.scalar_tensor_tensor(out=stR[:, ci, 0:1], in0=bk(5)[:, 256:257], scalar=-1.0,
                                                                           in1=colf[:, c:c + 1], op0=ALU.mult, op1=ALU.max),
                             r=[("bank", 5), ("col", 1)], w=[("stR", ci, 0)])
                        T.op("dve", lambda: nc.vector.tensor_tensor(out=stR[:, ci, 1:2], in0=stR[:, ci, 0:1], in1=bk(5)[:, 256:257],
                                                                    op=ALU.max),
                             r=[("stR", ci, 0), ("bank", 5)], w=[("stR", ci, 1)])
                        T.op("dve", lambda: nc.vector.reciprocal(out=stR[:, ci, 2:3], in_=stR[:, ci, 1:2]),
                             r=[("stR", ci, 1)], w=[("stR", ci, 2)])
                        T.op("dve", lambda: nc.vector.scalar_tensor_tensor(out=hmt4[par][:, ci, :], in0=bk(5)[:, 0:256],
                                                                           scalar=stR[:, ci, 2:3], in1=sg4[par][:, ci, :],
                                                                           op0=ALU.mult, op1=ALU.mult),
                             r=[("bank", 5), ("stR", ci, 2), ("sg", par, ci // 2)], w=[("hmt", par, ci)])
                        T.op("act", lambda: nc.scalar.activation(out=junkB[:], in_=hmt4[par][:, ci, :], func=AF.Square,
                                                                 accum_out=ssq4[par][:, ci:ci + 1]),
                             r=[("hmt", par, ci)], w=["junkB", ("ssq", par, ci)])

                    def R_chunk(tb, ci):
                        par = tb % 2
                        c = tb * 4 + ci
                        cs = slice(c * 128, (c + 1) * 128)
                        T.op("dve", lambda: nc.vector.tensor_scalar(out=Cd[:], in0=Cst[:], scalar1=decb[:, c:c + 1], scalar2=None,
                                                                    op0=ALU.mult), r=["Cst", "decb"], w=["Cd"])
                        T.op("dve", lambda: nc.vector.tensor_scalar(out=Cdb[:], in0=Cst[:], scalar1=decb[:, c:c + 1], scalar2=None,
                                                                    op0=ALU.mult), r=["Cst", "decb"], w=["Cdb"])
                        T.op("pe", lambda: nc.tensor.matmul(bk(6)[:, 0:257], lhsT=ktok4[par][:, ci, :], rhs=vw4[par][:, ci, :],
                                                            start=True, stop=True),
                             r=[("ktok", par, ci), ("vw", par, ci), ("vw1", par)], w=[("bank", 6)])
                        T.op("dve", lambda: nc.vector.tensor_tensor(out=Cst[:], in0=bk(6)[:, 0:257], in1=Cd[:], op=ALU.add),
                             r=[("bank", 6), "Cd"], w=["Cst"])
                        if ci > 0:
                            rB(tb, ci - 1)
                        T.op("pe", lambda: nc.tensor.matmul(bk(5)[:, 0:257], lhsT=SmT4[par][:, ci, :], rhs=vw4[par][:, ci, :],
                                                            start=True, stop=False),
                             r=[("SmT", par, ci), ("vw", par, ci), ("vw1", par)], w=[("bank", 5)], inc=False)
                        T.op("pe", lambda: nc.tensor.matmul(bk(5)[:, 0:257], lhsT=qTm[:, cs], rhs=Cdb[:], start=False, stop=True),
                             r=[("qk", 0, tb), "Cdb"], w=[("bank", 5)])

                    def N_pre(tb):
                        par = tb % 2
                        T.op("act", lambda: nc.scalar.activation(out=rs4[par][:, 0:4], in_=ssq4[par][:, 0:4], func=AF.Ln,
                                                                 scale=1.0 / 256, bias=EPS),
                             r=[("ssq", par, ci) for ci in range(4)], w=[("rs", par, 0)])
                        T.op("act", lambda: nc.scalar.activation(out=rs4[par][:, 4:8], in_=rs4[par][:, 0:4], func=AF.Exp, scale=-0.5),
                             r=[("rs", par, 0)], w=[("rs", par, 1)])
                        for ci in range(4):
                            T.op("dve", lambda: nc.vector.scalar_tensor_tensor(out=ytok4[:, ci, :], in0=hmt4[par][:, ci, :],
                                                                               scalar=rs4[par][:, 4 + ci:5 + ci], in1=zs4[par][:, ci, :],
                                                                               op0=ALU.mult, op1=ALU.mult),
                                 r=[("hmt", par, ci), ("rs", par, 1), ("zs", par, ci // 2)], w=[("ytok", ci)])

                    def N_T(tb, hf):
                        par = tb % 2
                        ytb = yTb[par]
                        for ci in range(4):
                            T.op("pe", lambda: nc.tensor.transpose(bkb(7)[:, ci * 128:(ci + 1) * 128],
                                                                   ytok4[:, ci, hf * 128:(hf + 1) * 128], ident[:]),
                                 r=[("ytok", ci), "ident"], w=[("bank", 7)], inc=(ci == 3))
                        T.op("act", lambda: nc.scalar.copy(out=ytb[:, hf, :], in_=bkb(7)[:, 0:512]),
                             r=[("bank", 7)], w=[("yTb", par)])

                    def N_out(tb):
                        par = tb % 2
                        r0 = 1024 + h * 256
                        T.dma("pool", yT[r0:r0 + 256, tb * 512:(tb + 1) * 512].rearrange("(a p) t -> p a t", p=128), yTb[par][:],
                              r=[("yTb", par)], w=[("yT", 8 + h, tb)])

                    P_pre(0)
                    for ci in range(4):
                        P_chunk(0, ci)
                    P_post(0)
                    cast_wqk_next = None
                    for tb in range(NB):
                        if h + 1 < 4 and tb == min(2, NB - 1):
                            cast_wqk_next = load_wqk(h + 1, defer=True)
                        elif cast_wqk_next is not None:
                            cast_wqk_next()
                            cast_wqk_next = None
                        if tb >= 1:
                            N_pre(tb - 1)
                        if tb + 1 < NB:
                            P_pre(tb + 1)
                        for ci in range(4):
                            if tb + 1 < NB:
                                P_chunk(tb + 1, ci)
                            R_chunk(tb, ci)
                            if tb >= 1 and ci == 1:
                                N_T(tb - 1, 0)
                            if tb >= 1 and ci == 3:
                                N_T(tb - 1, 1)
                                N_out(tb - 1)
                        rB(tb, 3)
                        if tb + 1 < NB:
                            P_post(tb + 1)
                    if cast_wqk_next is not None:
                        cast_wqk_next()
                        cast_wqk_next = None
                    N_pre(NB - 1)
                    N_T(NB - 1, 0)
                    N_T(NB - 1, 1)
                    N_out(NB - 1)
                T.barrier()
        h1dst = out
        scCD = es.enter_context(ExitStack())
        w1 = sb("w1", [128, 8, ODD_IN], BF16, scCD)
        rs1 = sb("rs1", [128, NT], F32, scCD)
        with ExitStack() as scC:
            wo0 = sb("wo0", [128, 16, D], BF16, scC)
            gp0 = sb("gp0", [128, D], F32, scC)
            T.dma("sp", gp0[:], l0_post_g.partition_broadcast(128), w=["gp0"])
            for kg in range(2):
                for cg in range(2):
                    T.dma("sp", wo0[:, kg * 8:(kg + 1) * 8, cg * 512:(cg + 1) * 512],
                          wo0bv[:, kg * 8:(kg + 1) * 8, cg * 512:(cg + 1) * 512], w=[("wo0", kg, cg)])
            wo0_keys = [("wo0", a, b) for a in range(2) for b in range(2)]
            w1_loads = list(range(ODD_IN // 512))
            yblk = [sb("yblk%d" % i, [128, 16, 512], BF16, scC) for i in range(2)]
            xt = [sb("xtC%d" % i, [128, D], F32, scC) for i in range(2)]
            junkC = sb("junkC", [128, D], BF16, scC)
            stC = [sb("stC%d" % i, [128, 8], F32, scC) for i in range(2)]
            h1 = [sb("h1_%d" % i, [128, D], F32, scC) for i in range(2)]
            for i in range(NT):
                tb, b = i // 4, i % 2
                yb_ = yblk[tb % 2]
                st_ = stC[b]
                if i % 4 == 0:
                    for half in range(2):
                        T.dma("sp", yb_[:, half * 8:(half + 1) * 8, :],
                              yT[half * 1024:(half + 1) * 1024, tb * 512:(tb + 1) * 512].rearrange("(a p) t -> p a t", p=128),
                              r=[("yT", hh, tb) for hh in range(half * 8, half * 8 + (8 if half == 0 else 4))],
                              w=[("yblk", tb % 2, half)])
                T.dma("sp", xt[b][:], x[i * 128:(i + 1) * 128, :], w=[("xtC", b)])
                if w1_loads and (i % 2 == 1 or NT - i <= len(w1_loads)):
                    cgp = w1_loads.pop(0)
                    T.dma("sp", w1[:, :, cgp * 512:(cgp + 1) * 512], w1bv[:, :, cgp * 512:(cgp + 1) * 512], w=[("w1", cgp)])
                ts_ = slice((i % 4) * 128, (i % 4 + 1) * 128)
                pbs = (2 * b, 2 * b + 1)
                for cg in range(2):
                    for kc in range(16):
                        T.op("pe", lambda: nc.tensor.matmul(bk(pbs[cg])[:, :], lhsT=yb_[:, kc, ts_], rhs=wo0[:, kc, cg * 512:(cg + 1) * 512],
                                                            start=(kc == 0), stop=(kc == 15)),
                             r=[("yblk", tb % 2, 0), ("yblk", tb % 2, 1)] + wo0_keys, w=[("bank", pbs[cg])], inc=(kc == 15))
                for cg in range(2):
                    T.op("act", lambda: nc.scalar.activation(out=junkC[:, cg * 512:(cg + 1) * 512], in_=bk(pbs[cg])[:, :], func=AF.Square,
                                                             accum_out=st_[:, cg:cg + 1]),
                         r=[("bank", pbs[cg])], w=[("junkC", cg), ("stC", b, cg)])
                T.op("dve", lambda: nc.vector.tensor_tensor(out=st_[:, 2:3], in0=st_[:, 0:1], in1=st_[:, 1:2], op=ALU.add),
                     r=[("stC", b, 0), ("stC", b, 1)], w=[("stC", b, 2)])
                rstd_from_ssq(st_[:, 2:3], st_[:, 3:4], st_[:, 4:5], D, [("stC", b, 2)], ("stC", b, 3), ("stC", b, 4))
                for cg in range(2):
                    T.op("dve", lambda: nc.vector.scalar_tensor_tensor(out=h1[b][:, cg * 512:(cg + 1) * 512], in0=bk(pbs[cg])[:, :],
                                                                       scalar=st_[:, 4:5], in1=gp0[:, cg * 512:(cg + 1) * 512],
                                                                       op0=ALU.mult, op1=ALU.mult),
                         r=[("bank", pbs[cg]), ("stC", b, 4), "gp0"], w=[("h1", b, cg)])
                T.op("dve", lambda: nc.vector.tensor_tensor(out=h1[b][:], in0=h1[b][:], in1=xt[b][:], op=ALU.add),
                     r=[("h1", b, 0), ("h1", b, 1), ("xtC", b)], w=[("h1", b, 0), ("h1", b, 1)])
                T.op("act", lambda: nc.scalar.activation(out=junkC[:], in_=h1[b][:], func=AF.Square, accum_out=st_[:, 5:6]),
                     r=[("h1", b, 0), ("h1", b, 1)], w=[("junkC", 0), ("junkC", 1), ("stC", b, 5)])
                rstd_from_ssq(st_[:, 5:6], st_[:, 6:7], rs1[:, i:i + 1], D, [("stC", b, 5)], ("stC", b, 6), ("rs1", i))
                T.dma("pool", h1dst[i * 128:(i + 1) * 128, :], h1[b][:], r=[("h1", b, 0), ("h1", b, 1)], w=[("h1d", i)])
                if dbg:
                    T.dma("pool", dbg_out["dbg_h1"][i * 128:(i + 1) * 128, :], h1[b][:], r=[("h1", b, 0), ("h1", b, 1)], w=[("dbgh1", i)])
            while w1_loads:
                cgp = w1_loads.pop(0)
                T.dma("sp", w1[:, :, cgp * 512:(cgp + 1) * 512], w1bv[:, :, cgp * 512:(cgp + 1) * 512], w=[("w1", cgp)])
            if dbg:
                T.barrier()
                T.dma("sp", dbg_out["dbg_yT"][:, :], yT[:, :])
            T.barrier()
        with ExitStack() as scD:
            wo1 = sb("wo1", [128, 16, D], BF16, scD)
            for kg in range(2):
                for cg in range(2):
                    T.dma("sp", wo1[:, kg * 8:(kg + 1) * 8, cg * 512:(cg + 1) * 512],
                          wo1bv[:, kg * 8:(kg + 1) * 8, cg * 512:(cg + 1) * 512], w=[("wo1", kg, cg)])
            wo1_keys = [("wo1", a, b) for a in range(2) for b in range(2)]
            gp1 = sb("gp1", [128, D], F32, scD)
            gq1 = sb("gq1", [128, D], F32, scD)
            gsg = sb("gsg", [128, 2048], F32, scD)
            bsp = sb("bsp", [128, 8], F32, scD)
            wmT = sb("wmT", [128, 8, 128], BF16, scD)
            T.dma("sp", gp1[:], l1_pre_g.partition_broadcast(128), w=["gp1"])
            T.dma("sp", gq1[:], l1_post_g.partition_broadcast(128), w=["gq1"])
            T.dma("sp", gsg[:], l1_sg_norm_g.partition_broadcast(128), w=["gsg"])
            T.dma("sp", bsp[:], l1_b_spatial.rearrange("g t -> t g"), w=["bsp"], allow_slow_non_contiguous=True)
            with ExitStack() as scS:
                wsp_f = sb("wsp_f", [128, 8, 128], F32, scS)
                wsp_b = sb("wsp_b", [128, 8, 128], BF16, scS)
                T.dma("sp", wsp_f[:], l1_w_spatial.rearrange("g t s -> t g s"), w=["wsp_f"])
                T.op("dve", lambda: nc.vector.tensor_copy(out=wsp_b[:], in_=wsp_f[:]), r=["wsp_f"], w=["wsp_b"])
                for g in range(8):
                    T.op("pe", lambda: nc.tensor.transpose(bkb(7)[:, g * 128:(g + 1) * 128], wsp_b[:, g, :], ident[:]),
                         r=["wsp_b", "ident"], w=[("bank", 7)], inc=(g == 7))
                T.op("dve", lambda: nc.vector.tensor_tensor(out=wmT[:], in0=bkb(7).rearrange("p (g t) -> p g t", g=8),
                                                            in1=tri[:, None, :].to_broadcast([128, 8, 128]), op=ALU.mult),
                     r=[("bank", 7), "tri"], w=["wmT"])
                T.barrier()

            ht = [sb("ht%d" % i, [128, D], F32, scD) for i in range(3)]
            hn1 = sb("hn1", [128, D], BF16, scD)
            hn1T = [sb("hn1T%d" % i, [128, 8, 128], BF16, scD) for i in range(2)]
            junkD = sb("junkD", [128, D], BF16, scD)
            stD = [sb("stD%d" % i, [128, 12], F32, scD) for i in range(2)]
            gu = [sb("gu%d" % i, [128, 2048], BF16, scD) for i in range(2)]
            gv = [sb("gv%d" % i, [128, 2048], BF16, scD) for i in range(2)]
            sz = [sb("sz%d" % i, [128, 2048], BF16, scD) for i in range(2)]
            vn = sb("vn", [128, 2048], BF16, scD)
            y1 = sb("y1", [128, 2048], BF16, scD)
            y1T = sb("y1T", [128, 16, 128], BF16, scD)
            ymx = sb("ymx", [128, D], F32, scD)
            guk = lambda p: [("gu", p, k) for k in range(4)]
            gvk = lambda p: [("gv", p, k) for k in range(4)]
            szk = lambda p: [("sz", p, k) for k in range(4)]
            y1k = [("y1", g) for g in range(8)]

            def stA(i):
                p = i % 2
                T.dma("sp", ht[i % 3][:], h1dst[i * 128:(i + 1) * 128, :], r=[("h1d", i)], w=[("ht", i % 3)])
                T.op("dve", lambda: nc.vector.scalar_tensor_tensor(out=hn1[:], in0=ht[i % 3][:], scalar=rs1[:, i:i + 1], in1=gp1[:],
                                                                   op0=ALU.mult, op1=ALU.mult),
                     r=[("ht", i % 3), "gp1"], w=["hn1"])
                for kc in range(8):
                    T.op("pe", lambda: nc.tensor.transpose(bkb(7)[:, kc * 128:(kc + 1) * 128], hn1[:, kc * 128:(kc + 1) * 128], ident[:]),
                         r=["hn1", "ident"], w=[("bank", 7)], inc=(kc == 7))
                T.op("dve", lambda: nc.vector.tensor_copy(out=hn1T[p][:], in_=bkb(7).rearrange("p (k t) -> p k t", k=8)),
                     r=[("bank", 7)], w=[("hn1T", p)])

            def stB(i, groups):
                p = i % 2
                for cgp in groups:
                    pb = (0, 1, 3, 4)[cgp % 4]
                    for kc in range(8):
                        T.op("pe", lambda: nc.tensor.matmul(bk(pb)[:, :], lhsT=hn1T[p][:, kc, :], rhs=w1[:, kc, cgp * 512:(cgp + 1) * 512],
                                                            start=(kc == 0), stop=(kc == 7)),
                             r=[("hn1T", p)], w=[("bank", pb)], inc=(kc == 7))
                    sec, cc = cgp // 4, (cgp % 4) * 512
                    dst, fn, key = ((gu[p], AF.Gelu_apprx_tanh, "gu"), (gv[p], AF.Gelu_apprx_tanh, "gv"), (sz[p], AF.Silu, "sz"))[sec]
                    T.op("act", lambda: nc.scalar.activation(out=dst[:, cc:cc + 512], in_=bk(pb)[:, :], func=fn),
                         r=[("bank", pb)], w=[(key, p, cgp % 4)])

            def stC1a(i):
                p = i % 2
                T.op("act", lambda: nc.scalar.activation(out=vn[:], in_=gv[p][:], func=AF.Square, accum_out=stD[p][:, 3:4]),
                     r=gvk(p), w=["vn", ("stD", p, 3)])
                rstd_from_ssq(stD[p][:, 3:4], stD[p][:, 4:5], stD[p][:, 5:6], 2048, [("stD", p, 3)], ("stD", p, 4), ("stD", p, 5))

            def stC1b(i):
                p = i % 2
                T.op("dve", lambda: nc.vector.scalar_tensor_tensor(out=vn[:], in0=gv[p][:], scalar=stD[p][:, 5:6], in1=gsg[:],
                                                                   op0=ALU.mult, op1=ALU.mult),
                     r=gvk(p) + [("stD", p, 5), "gsg"], w=["vn"])
                T.op("pool", lambda: nc.gpsimd.tensor_tensor(out=gu[p][:], in0=gu[p][:], in1=sz[p][:], op=ALU.mult),
                     r=guk(p) + szk(p), w=guk(p))

            def stC2(i):
                p = i % 2
                for half in range(2):
                    for gg in range(4):
                        g = half * 4 + gg
                        pb = 5 + gg // 2
                        T.op("pe", lambda: nc.tensor.matmul(bk(pb)[:, (gg % 2) * 256:(gg % 2 + 1) * 256], lhsT=wmT[:, g, :],
                                                            rhs=vn[:, g * 256:(g + 1) * 256], start=True, stop=True),
                             r=["wmT", "vn"], w=[("bank", pb)], inc=(gg % 2 == 1))
                    for gg in range(4):
                        g = half * 4 + gg
                        pb = 5 + gg // 2
                        T.op("dve", lambda: nc.vector.scalar_tensor_tensor(out=y1[:, g * 256:(g + 1) * 256],
                                                                           in0=bk(pb)[:, (gg % 2) * 256:(gg % 2 + 1) * 256],
                                                                           scalar=bsp[:, g:g + 1], in1=gu[p][:, g * 256:(g + 1) * 256],
                                                                           op0=ALU.add, op1=ALU.mult),
                             r=[("bank", pb), "bsp"] + guk(p), w=[("y1", g)])

            def stC3(i):
                for half in range(2):
                    tbk = 2 if half == 0 else 7
                    for kk in range(8):
                        kc = half * 8 + kk
                        T.op("pe", lambda: nc.tensor.transpose(bkb(tbk)[:, kk * 128:(kk + 1) * 128], y1[:, kc * 128:(kc + 1) * 128], ident[:]),
                             r=y1k + ["ident"], w=[("bank", tbk)], inc=(kk == 7))
                    if half == 0:
                        T.op("dve", lambda: nc.vector.tensor_copy(out=y1T[:, 0:8, :], in_=bkb(2).rearrange("p (k t) -> p k t", k=8)),
                             r=[("bank", 2)], w=[("y1T", 0)])
                    else:
                        T.op("act", lambda: nc.scalar.copy(out=y1T[:, 8:16, :], in_=bkb(7).rearrange("p (k t) -> p k t", k=8)),
                             r=[("bank", 7)], w=[("y1T", 1)])

            def stD_(i):
                p = i % 2
                for cg in range(2):
                    pb = 5 + cg
                    for kc in range(16):
                        T.op("pe", lambda: nc.tensor.matmul(bk(pb)[:, :], lhsT=y1T[:, kc, :], rhs=wo1[:, kc, cg * 512:(cg + 1) * 512],
                                                            start=(kc == 0), stop=(kc == 15)),
                             r=[("y1T", 0), ("y1T", 1)], w=[("bank", pb)], inc=(kc == 15))
                for cg in range(2):
                    T.op("act", lambda: nc.scalar.copy(out=ymx[:, cg * 512:(cg + 1) * 512], in_=bk(5 + cg)[:, :]),
                         r=[("bank", 5 + cg)], w=[("ymx", cg)])
                T.op("act", lambda: nc.scalar.activation(out=junkD[:, 0:D], in_=ymx[:], func=AF.Square, accum_out=stD[p][:, 6:7]),
                     r=[("ymx", 0), ("ymx", 1)], w=["junkD", ("stD", p, 6)])
                rstd_from_ssq(stD[p][:, 6:7], stD[p][:, 7:8], stD[p][:, 8:9], D, [("stD", p, 6)], ("stD", p, 7), ("stD", p, 8))
                T.op("dve", lambda: nc.vector.scalar_tensor_tensor(out=ymx[:], in0=ymx[:], scalar=stD[p][:, 8:9], in1=gq1[:],
                                                                   op0=ALU.mult, op1=ALU.mult),
                     r=[("ymx", 0), ("ymx", 1), ("stD", p, 8), "gq1"], w=[("ymx", 0), ("ymx", 1)])
                T.op("dve", lambda: nc.vector.tensor_tensor(out=ht[i % 3][:], in0=ymx[:], in1=ht[i % 3][:], op=ALU.add),
                     r=[("ymx", 0), ("ymx", 1), ("ht", i % 3)], w=[("ht", i % 3)])
                T.dma("pool", out[i * 128:(i + 1) * 128, :], ht[i % 3][:], r=[("ht", i % 3), ("h1d", i)], w=[("outd", i)])

            stA(0)
            for i in range(NT):
                stB(i, range(0, 3))
                if i >= 1:
                    stC1b(i - 1)
                stB(i, range(3, 5))
                if i + 1 < NT:
                    stA(i + 1)
                if i >= 1:
                    stC2(i - 1)
                stB(i, range(5, 8))
                if i >= 1:
                    stC3(i - 1)
                stB(i, range(8, 10))
                if i >= 1:
                    stD_(i - 1)
                stC1a(i)
                stB(i, range(10, 12))
            stC1b(NT - 1)
            stC2(NT - 1)
            stC3(NT - 1)
            stD_(NT - 1)
            T.finish("sp")
            T.barrier()
    return nc


_NC_CACHE = {}


def _consts():
    ident = np.eye(128, dtype=np.float32).astype(ml_dtypes.bfloat16)
    k = np.arange(128)
    tri = (k[:, None] <= k[None, :]).astype(np.float32).astype(ml_dtypes.bfloat16)
    return {"c_ident": ident, "c_tri": tri}


def kernel(**inputs):
    x = np.ascontiguousarray(inputs["x"], dtype=np.float32)
    B, S, _ = x.shape
    if S not in _NC_CACHE:
        _NC_CACHE[S] = build(S)
    nc = _NC_CACHE[S]
    shared = {k: np.ascontiguousarray(v, dtype=np.float32) for k, v in inputs.items() if k != "x"}
    shared.update(_consts())
    in_maps = []
    for b in range(B):
        m = dict(shared)
        m["x"] = x[b]
        in_maps.append(m)
    res = run_bass_kernel_spmd(nc, in_maps, core_ids=list(range(B)))
    return np.stack([r["out"] for r in res.results], axis=0).astype(np.float32)
```
